# Optimizing a Trainium2 kernel written in Bass

```python
import math
import jax, jax.numpy as jnp
from jax import lax
import numpy as np

D_MODEL = 1024
BATCH = 4
SEQ = 4096
DEPTH = 1

CTX_LEN = 256
GRID_W = 64
CHUNK = 64
EPS = 1e-6

GLA_HEADS = 4
GLA_DK = 64
GLA_DV = 128
GLA_INNER = GLA_HEADS * GLA_DV
GLA_RANK = 16
GLA_GATE_NORM = 16.0

SSD_HEADS = 8
SSD_P = 64
SSD_N = 64
SSD_GROUPS = 2
SSD_INNER = SSD_HEADS * SSD_P
SSD_CONV_DIM = SSD_INNER + 2 * SSD_GROUPS * SSD_N
CONV_K = 3

D_MIX = GLA_INNER + SSD_INNER
D_FF = -(-8 * D_MODEL // (3 * 256)) * 256

PROJ_SIZES = (GLA_HEADS * GLA_DK, GLA_HEADS * GLA_DK, GLA_INNER, GLA_INNER, 2 * GLA_RANK,
              SSD_INNER, SSD_CONV_DIM, 2 * SSD_HEADS)
D_IN_PROJ = sum(PROJ_SIZES)
PROJ_SPLITS = tuple(int(s) for s in np.cumsum(PROJ_SIZES)[:-1])

kernel_name = "hybrid_gla_ssd_prefix_dit_layer"


def rms_norm(x, g):
    xf = x.astype(jnp.float32)
    y = xf * lax.rsqrt(jnp.mean(xf * xf, axis=-1, keepdims=True) + EPS)
    return (y * g).astype(x.dtype)


def modulate(h, shift, scale):
    return h * (1 + scale[:, None]) + shift[:, None]


def _flip(*arrs):
    return tuple(a[:, ::-1] for a in arrs)


def _chunks(a):
    b, t = a.shape[:2]
    return jnp.moveaxis(a.reshape(b, t // CHUNK, CHUNK, *a.shape[2:]), 1, 0)


def _unchunk(a):
    a = jnp.moveaxis(a, 0, 1)
    return a.reshape(a.shape[0], a.shape[1] * a.shape[2], *a.shape[3:])


def _grid_dwconv(u, w, grid_hw):
    b, t, ch = u.shape
    img = u.reshape(b, grid_hw[0], grid_hw[1], ch)
    out = lax.conv_general_dilated(img, w[:, :, None, :].astype(u.dtype), (1, 1), "SAME",
                                   dimension_numbers=("NHWC", "HWIO", "NHWC"),
                                   feature_group_count=ch)
    return out.reshape(b, t, ch)


def _gla_scan(q, k, v, g, s0):
    mask = jnp.tril(jnp.ones((CHUNK, CHUNK), bool))

    def step(s, inp):
        qc, kc, vc, gc = inp
        cum = jnp.cumsum(gc.astype(jnp.float32), axis=1)
        inter = jnp.einsum("blhk,bhkv->blhv", qc * jnp.exp(cum), s)
        diff = cum[:, :, None] - cum[:, None, :]
        decay = jnp.exp(jnp.where(mask[None, :, :, None, None], diff, -jnp.inf))
        att = jnp.einsum("bihk,bjhk,bijhk->bhij", qc, kc, decay)
        intra = jnp.einsum("bhij,bjhv->bihv", att, vc)
        tot = cum[:, -1]
        s_new = s * jnp.exp(tot)[..., None] + jnp.einsum(
            "blhk,blhv->bhkv", kc * jnp.exp(tot[:, None] - cum), vc)
        return s_new, inter + intra

    s_fin, out = lax.scan(step, s0, (_chunks(q), _chunks(k), _chunks(v), _chunks(g)))
    return _unchunk(out), s_fin


def _ssd_scan(xs, dt, a, bm, cm, s0):
    mask = jnp.tril(jnp.ones((CHUNK, CHUNK), bool))

    def step(s, inp):
        xc, dtc, ac, bc, cc = inp
        cum = jnp.cumsum(ac.astype(jnp.float32), axis=1)
        seg = cum[:, :, None] - cum[:, None, :]
        lmat = jnp.exp(jnp.where(mask[None, :, :, None], seg, -jnp.inf))
        scores = jnp.einsum("bihn,bjhn,bijh->bhij", cc, bc, lmat)
        intra = jnp.einsum("bhij,bjhp->bihp", scores, xc * dtc[..., None])
        inter = jnp.einsum("bihn,bhpn->bihp", cc, s) * jnp.exp(cum)[..., None]
        tot = cum[:, -1]
        w = (jnp.exp(tot[:, None] - cum) * dtc)[..., None]
        s_new = s * jnp.exp(tot)[:, :, None, None] + jnp.einsum("blhn,blhp->bhpn", bc * w, xc)
        return s_new, inter + intra

    s_fin, out = lax.scan(step, s0, (_chunks(xs), _chunks(dt), _chunks(a), _chunks(bm), _chunks(cm)))
    return _unchunk(out), s_fin


def _bidir(scan_fn, ctx_f, ctx_b, lat_f, lat_b, s0):
    yc_f, sc_f = scan_fn(*ctx_f, s0)
    yl_f, _ = scan_fn(*lat_f, sc_f)
    yc_b, sc_b = scan_fn(*_flip(*ctx_b), s0)
    yl_b, _ = scan_fn(*_flip(*lat_b), sc_b)
    return yc_f + yc_b[:, ::-1], yl_f + yl_b[:, ::-1]


def _stream_features(h, grid_hw, w_in, conv_w, conv_b, gla_wg_f, gla_bg_f, gla_wg_b, gla_bg_b,
                     a_log_f, a_log_b, dt_bias_f, dt_bias_b):
    b, t, _ = h.shape
    q, k, v, r, lr, z, xbc, dt_raw = jnp.split(h @ w_in, PROJ_SPLITS, axis=-1)
    heads = lambda arr, n: arr.reshape(b, t, n, -1)
    lr_f, lr_b = jnp.split(lr, 2, axis=-1)
    g_f = heads(jax.nn.log_sigmoid(lr_f @ gla_wg_f + gla_bg_f) / GLA_GATE_NORM, GLA_HEADS)
    g_b = heads(jax.nn.log_sigmoid(lr_b @ gla_wg_b + gla_bg_b) / GLA_GATE_NORM, GLA_HEADS)
    xbc = jax.nn.silu(_grid_dwconv(xbc, conv_w, grid_hw) + conv_b)
    xs, bm, cm = jnp.split(xbc, [SSD_INNER, SSD_INNER + SSD_GROUPS * SSD_N], axis=-1)
    rep = SSD_HEADS // SSD_GROUPS
    dt_f_raw, dt_b_raw = jnp.split(dt_raw, 2, axis=-1)
    dt_f = jax.nn.softplus(dt_f_raw + dt_bias_f)
    dt_b = jax.nn.softplus(dt_b_raw + dt_bias_b)
    return {
        "q": heads(q, GLA_HEADS) * GLA_DK ** -0.5,
        "k": heads(k, GLA_HEADS),
        "v": heads(v, GLA_HEADS),
        "r": r, "g_f": g_f, "g_b": g_b,
        "xs": heads(xs, SSD_HEADS),
        "bm": jnp.repeat(heads(bm, SSD_GROUPS), rep, axis=2),
        "cm": jnp.repeat(heads(cm, SSD_GROUPS), rep, axis=2),
        "dt_f": dt_f, "a_f": dt_f * -jnp.exp(a_log_f),
        "dt_b": dt_b, "a_b": dt_b * -jnp.exp(a_log_b),
        "z": z,
    }


def _merge_heads(f, y_gla, y_ssd, gla_norm, d_skip, ssd_norm, w_out):
    b, t = y_gla.shape[:2]
    o_gla = rms_norm(y_gla, gla_norm) * jax.nn.silu(f["r"]).reshape(b, t, GLA_HEADS, GLA_DV)
    y = (y_ssd + d_skip[:, None] * f["xs"]).reshape(b, t, SSD_INNER) * jax.nn.silu(f["z"])
    o_ssd = rms_norm(y.reshape(b, t, SSD_GROUPS, -1), ssd_norm.reshape(SSD_GROUPS, -1))
    o = jnp.concatenate([o_gla.reshape(b, t, GLA_INNER), o_ssd.reshape(b, t, SSD_INNER)], axis=-1)
    return o @ w_out


def hybrid_mixer(h_ctx, h_lat, rows, w_in, conv_w, conv_b, gla_wg_f, gla_bg_f, gla_wg_b, gla_bg_b,
                 gla_norm, a_log_f, a_log_b, dt_bias_f, dt_bias_b, d_skip, ssd_norm, w_out):
    per_stream = (w_in, conv_w, conv_b, gla_wg_f, gla_bg_f, gla_wg_b, gla_bg_b,
                  a_log_f, a_log_b, dt_bias_f, dt_bias_b)
    fc = _stream_features(h_ctx, (1, h_ctx.shape[1]), *per_stream)
    fl = _stream_features(h_lat, (rows, GRID_W), *per_stream)
    b = h_lat.shape[0]
    gla0 = jnp.zeros((b, GLA_HEADS, GLA_DK, GLA_DV), jnp.float32)
    ssd0 = jnp.zeros((b, SSD_HEADS, SSD_P, SSD_N), jnp.float32)
    gla_in = lambda f, d: (f["q"], f["k"], f["v"], f["g_" + d])
    ssd_in = lambda f, d: (f["xs"], f["dt_" + d], f["a_" + d], f["bm"], f["cm"])
    gla_c, gla_l = _bidir(_gla_scan, gla_in(fc, "f"), gla_in(fc, "b"),
                          gla_in(fl, "f"), gla_in(fl, "b"), gla0)
    ssd_c, ssd_l = _bidir(_ssd_scan, ssd_in(fc, "f"), ssd_in(fc, "b"),
                          ssd_in(fl, "f"), ssd_in(fl, "b"), ssd0)
    y_ctx = _merge_heads(fc, gla_c, ssd_c, gla_norm, d_skip, ssd_norm, w_out)
    y_lat = _merge_heads(fl, gla_l, ssd_l, gla_norm, d_skip, ssd_norm, w_out)
    return y_ctx, y_lat


def swiglu(h, w_gate, w_up, w_down):
    return (jax.nn.silu(h @ w_gate) * (h @ w_up)) @ w_down


def setup_inputs(seed: int = 0) -> dict:
    key = jax.random.key(seed)
    ks = jax.random.split(key, 28)
    f32 = jnp.float32
    nrm = lambda k, shape, scale: jax.random.normal(k, shape, f32) * scale
    gain = lambda k, shape: 1.0 + 0.1 * jax.random.normal(k, shape, f32)

    def dt_bias(k):
        dt = jnp.exp(jax.random.uniform(k, (DEPTH, SSD_HEADS), f32, math.log(1e-3), math.log(1e-1)))
        return dt + jnp.log(-jnp.expm1(-dt))

    def a_log(k):
        return jnp.log(jax.random.uniform(k, (DEPTH, SSD_HEADS), f32, 1.0, 16.0))

    return {
        "x": nrm(ks[0], (BATCH, SEQ, D_MODEL), 1.0),
        "c": nrm(ks[1], (BATCH, D_MODEL), 1.0),
        "ctx": nrm(ks[2], (BATCH, CTX_LEN, D_MODEL), 1.0),
        "c_ctx": nrm(ks[3], (D_MODEL,), 0.5),
        "w_mod": nrm(ks[4], (DEPTH, D_MODEL, 6 * D_MODEL), D_MODEL ** -0.5),
        "b_mod": nrm(ks[5], (DEPTH, 6 * D_MODEL), 0.02),
        "norm_mix_pre": gain(ks[6], (DEPTH, D_MODEL)),
        "norm_mix_post": gain(ks[7], (DEPTH, D_MODEL)),
        "norm_ffn_pre": gain(ks[8], (DEPTH, D_MODEL)),
        "norm_ffn_post": gain(ks[9], (DEPTH, D_MODEL)),
        "w_in": nrm(ks[10], (DEPTH, D_MODEL, D_IN_PROJ), D_MODEL ** -0.5),
        "conv_w": nrm(ks[11], (DEPTH, CONV_K, CONV_K, SSD_CONV_DIM), (CONV_K * CONV_K) ** -0.5),
        "conv_b": nrm(ks[12], (DEPTH, SSD_CONV_DIM), 0.02),
        "gla_wg_f": nrm(ks[13], (DEPTH, GLA_RANK, GLA_HEADS * GLA_DK), GLA_RANK ** -0.5),
        "gla_bg_f": nrm(ks[14], (DEPTH, GLA_HEADS * GLA_DK), 0.1),
        "gla_wg_b": nrm(ks[15], (DEPTH, GLA_RANK, GLA_HEADS * GLA_DK), GLA_RANK ** -0.5),
        "gla_bg_b": nrm(ks[16], (DEPTH, GLA_HEADS * GLA_DK), 0.1),
        "gla_norm": gain(ks[17], (DEPTH, GLA_DV)),
        "a_log_f": a_log(ks[18]),
        "a_log_b": a_log(ks[19]),
        "dt_bias_f": dt_bias(ks[20]),
        "dt_bias_b": dt_bias(ks[21]),
        "d_skip": gain(ks[22], (DEPTH, SSD_HEADS)),
        "ssd_norm": gain(ks[23], (DEPTH, SSD_INNER)),
        "w_out": nrm(ks[24], (DEPTH, D_MIX, D_MODEL), D_MIX ** -0.5),
        "w_gate": nrm(ks[25], (DEPTH, D_MODEL, D_FF), D_MODEL ** -0.5),
        "w_up": nrm(ks[26], (DEPTH, D_MODEL, D_FF), D_MODEL ** -0.5),
        "w_down": nrm(ks[27], (DEPTH, D_FF, D_MODEL), D_FF ** -0.5),
    }


def reference(x, c, ctx, c_ctx, w_mod, b_mod, norm_mix_pre, norm_mix_post, norm_ffn_pre, norm_ffn_post,
              w_in, conv_w, conv_b, gla_wg_f, gla_bg_f, gla_wg_b, gla_bg_b, gla_norm,
              a_log_f, a_log_b, dt_bias_f, dt_bias_b, d_skip, ssd_norm, w_out, w_gate, w_up, w_down):
    rows = x.shape[1] // GRID_W
    x_lat, x_ctx = x, ctx
    for layer in range(DEPTH):
        m_lat = jnp.split(jax.nn.silu(c) @ w_mod[layer] + b_mod[layer], 6, axis=-1)
        m_ctx = jnp.split(jax.nn.silu(c_ctx)[None] @ w_mod[layer] + b_mod[layer], 6, axis=-1)

        h_lat = modulate(rms_norm(x_lat, norm_mix_pre[layer]), m_lat[0], m_lat[1])
        h_ctx = modulate(rms_norm(x_ctx, norm_mix_pre[layer]), m_ctx[0], m_ctx[1])
        y_ctx, y_lat = hybrid_mixer(
            h_ctx, h_lat, rows, w_in[layer], conv_w[layer], conv_b[layer],
            gla_wg_f[layer], gla_bg_f[layer], gla_wg_b[layer], gla_bg_b[layer], gla_norm[layer],
            a_log_f[layer], a_log_b[layer], dt_bias_f[layer], dt_bias_b[layer],
            d_skip[layer], ssd_norm[layer], w_out[layer])

        x_lat = x_lat + m_lat[2][:, None] * rms_norm(y_lat, norm_mix_post[layer])
        f_lat = swiglu(modulate(rms_norm(x_lat, norm_ffn_pre[layer]), m_lat[3], m_lat[4]),
                       w_gate[layer], w_up[layer], w_down[layer])
        x_lat = x_lat + m_lat[5][:, None] * rms_norm(f_lat, norm_ffn_post[layer])

        if layer + 1 < DEPTH:
            x_ctx = x_ctx + m_ctx[2][:, None] * rms_norm(y_ctx, norm_mix_post[layer])
            f_ctx = swiglu(modulate(rms_norm(x_ctx, norm_ffn_pre[layer]), m_ctx[3], m_ctx[4]),
                           w_gate[layer], w_up[layer], w_down[layer])
            x_ctx = x_ctx + m_ctx[5][:, None] * rms_norm(f_ctx, norm_ffn_post[layer])
    return x_lat
```

```python
import os
from contextlib import ExitStack

import ml_dtypes
import numpy as np

import concourse.bass as bass
import concourse.mybir as mybir
from concourse.bass_utils import run_bass_kernel_spmd

F32 = mybir.dt.float32
BF16 = mybir.dt.bfloat16
AF = mybir.ActivationFunctionType
ALU = mybir.AluOpType
AX = mybir.AxisListType

D = 1024
SEQ = 4096
CTX = 256
OWN = 2048
DIN = 2864
DFF = 2816
NEG = -30000.0


class Sched:
    def __init__(self, nc, n_dma_sems=32):
        self.nc = nc
        self.engs = ["pe", "act", "dve", "pool", "sp"]
        self.lists = {e: [] for e in self.engs}
        self.count = {e: 0 for e in self.engs}
        self.waited = {e: {} for e in self.engs}
        self.last_w = {}
        self.readers = {}
        self.n_dma_sems = n_dma_sems
        self.dma_cnt = [0] * n_dma_sems
        self.dma_rr = 0
        self.sems = {}

    def _need(self, eng, best):
        out = []
        for sk, val in best.items():
            if sk == eng and eng == "pe":
                continue
            if self.waited[eng].get(sk, 0) >= val:
                continue
            self.waited[eng][sk] = val
            out.append((sk, val))
        return out

    def _deps(self, reads, writes):
        best = {}

        def add(tok):
            if best.get(tok[0], 0) < tok[1]:
                best[tok[0]] = tok[1]

        for k in reads:
            if k in self.last_w:
                add(self.last_w[k])
        for k in writes:
            if k in self.last_w:
                add(self.last_w[k])
            for t in self.readers.get(k, ()):
                add(t)
        return best

    def _commit(self, tok, reads, writes):
        for k in reads:
            self.readers.setdefault(k, []).append(tok)
        for k in writes:
            self.last_w[k] = tok
            self.readers[k] = []

    def op(self, eng, fn, reads=(), writes=()):
        best = self._deps(reads, writes)
        waits = self._need(eng, best)
        self.count[eng] += 1
        tok = (eng, self.count[eng])
        self.lists[eng].append((waits, fn, (eng, 1)))
        self._commit(tok, reads, writes)
        return tok

    def dma(self, eng, fn, reads=(), writes=()):
        i = self.dma_rr
        self.dma_rr = (self.dma_rr + 1) % self.n_dma_sems
        sk = "dma%d" % i
        best = self._deps(reads, writes)
        if self.dma_cnt[i] > 0:
            best[sk] = max(best.get(sk, 0), self.dma_cnt[i])
        waits = self._need(eng, best)
        self.dma_cnt[i] += 16
        tok = (sk, self.dma_cnt[i])
        self.lists[eng].append((waits, fn, (sk, 16)))
        self._commit(tok, reads, writes)
        return tok

    def barrier(self):
        best = {}
        for e in self.engs:
            if self.count[e] > 0:
                best[e] = self.count[e]
        for i in range(self.n_dma_sems):
            if self.dma_cnt[i] > 0:
                best["dma%d" % i] = self.dma_cnt[i]
        for e in self.engs:
            b = {k: v for k, v in best.items() if k != e}
            waits = self._need(e, b)
            if waits:
                self.lists[e].append((waits, None, None))
        self.last_w = {}
        self.readers = {}

    def wait_all(self, eng, toks):
        best = {}
        for s, v in toks:
            best[s] = max(best.get(s, 0), v)
        waits = self._need(eng, best)
        self.lists[eng].append((waits, None, None))

    def emit(self, stack):
        nc = self.nc
        for e in self.engs:
            self.sems[e] = stack.enter_context(nc.semaphore("s_" + e))
        for i in range(self.n_dma_sems):
            self.sems["dma%d" % i] = stack.enter_context(nc.semaphore("s_dma%d" % i))
        block = stack.enter_context(nc.Block())
        sems = self.sems

        def run(engh, lst):
            for waits, fn, inc in lst:
                for sk, v in waits:
                    engh.wait_ge(sems[sk], v)
                if fn is not None:
                    ins = fn(engh)
                    ins.then_inc(sems[inc[0]], inc[1])

        @block.tensor
        def _(e):
            run(e, self.lists["pe"])

        @block.scalar
        def _(e):
            run(e, self.lists["act"])

        @block.vector
        def _(e):
            run(e, self.lists["dve"])

        @block.gpsimd
        def _(e):
            run(e, self.lists["pool"])

        @block.sync
        def _(e):
            run(e, self.lists["sp"])


class Arena:
    def __init__(self, tile, nbytes):
        self.t = tile
        self.cap = nbytes
        self.off = 0
        self.n = 0

    def alloc(self, shape, dtype, name=None):
        n = int(np.prod(shape))
        esz = 4 if dtype == F32 else 2
        sz = (n * esz + 63) // 64 * 64
        off = self.off
        assert off + sz <= self.cap, ('arena overflow', name, off, sz, self.cap)
        self.off += sz
        ap = self.t[:, off // 2:(off + n * esz) // 2]
        if dtype == F32:
            ap = ap.bitcast(F32)
        if len(shape) == 2:
            ap = ap.rearrange("p (a b) -> p a b", a=shape[0])
        elif len(shape) == 3:
            ap = ap.rearrange("p (a b c) -> p a b c", a=shape[0], b=shape[1])
        self.n += 1
        return ap, (name or "t") + "#%d" % self.n


class _Stop(Exception):
    pass


def build_program(debug=None, stop=None):
    nc = bass.Bass("TRN2", target_bir_lowering=False)
    di = {}

    def dram_in(name, shape, dt=F32):
        di[name] = nc.dram_tensor(name, list(shape), dt, kind="ExternalInput").ap()
        return di[name]

    x_d = dram_in("x", [SEQ, D])
    ctx_d = dram_in("ctx", [CTX, D])
    cc_d = dram_in("cc", [128, 16])
    wmod_d = dram_in("w_mod", [D, 6 * D])
    bmod_d = dram_in("b_mod2", [2, 6 * D])
    ncol_d = dram_in("ncol", [128, 16])
    nrow_d = dram_in("nrow", [1, 2 * D])
    win_d = dram_in("w_in", [D, DIN])
    cw_d = dram_in("cw", [128, 54])
    cb_d = dram_in("cb", [128, 6])
    wg_d = dram_in("wg", [16, 512])
    bg_d = dram_in("bg", [1, 512])
    mixg_d = dram_in("mixg", [128, 8])
    alog_d = dram_in("alog", [128, 16])
    dtb_d = dram_in("dtb", [128, 16])
    dsk_d = dram_in("dsk", [128, 512])
    wout_d = dram_in("w_out", [D, D])
    wgate_d = dram_in("w_gate", [D, DFF])
    wup_d = dram_in("w_up", [D, DFF])
    wdown_d = dram_in("w_down", [DFF, D])
    ident_d = dram_in("ident", [128, 128], BF16)
    tribf_d = dram_in("tri_bf", [128, 256], BF16)
    trif_d = dram_in("tri_f", [128, 256])
    uppf_d = dram_in("upp_f", [128, 256])
    onesf_d = dram_in("ones_f", [128, 128])
    negm_d = dram_in("negm", [128, 1024], BF16)
    uallt_d = dram_in("uall_t", [48, 128], BF16)
    vallt_d = dram_in("vall_t", [48, 1024], BF16)
    bdg_d = dram_in("bdg", [128, 256])
    bds_d = dram_in("bds", [128, 8])
    gmask_d = dram_in("gmask", [128, 2])
    y_d = nc.dram_tensor("y", [OWN, D], F32, kind="ExternalOutput").ap()
    yf_scr = nc.dram_tensor("yf_scr", [OWN, D], F32, kind="Internal").ap()
    xn_scr = nc.dram_tensor("xn_scr", [OWN, D], F32, kind="Internal").ap()
    dbg_out = {}
    if debug:
        for nm, shp in debug.items():
            dbg_out[nm] = nc.dram_tensor("dbg_" + nm, list(shp), F32, kind="ExternalOutput").ap()

    st = ExitStack()
    dbg_toks = []
    with st:
        ARENA_BYTES = 207 * 1024
        arena_t = st.enter_context(nc.sbuf_tensor("arena", [128, ARENA_BYTES // 2], BF16))
        psum = [st.enter_context(nc.psum_tensor("pb%d" % i, [128, 512], F32)) for i in range(6)]
        psum2 = st.enter_context(nc.psum_tensor("pbig", [128, 1024], F32))
        S = Sched(nc)
        A = Arena(arena_t, ARENA_BYTES)

        PB = [(psum[i], "pb%d" % i) for i in range(6)]
        PB2 = (psum2, "pbig")

        REC = [None]
        ATOM = [0]

        class Chain:
            def __init__(self):
                self.steps = []

            def __enter__(self):
                self.prev = REC[0]
                REC[0] = self.steps
                return self

            def __exit__(self, *a):
                REC[0] = self.prev

        class atomic:
            def __enter__(self):
                if REC[0] is not None and ATOM[0] == 0:
                    REC[0].append([])
                ATOM[0] += 1

            def __exit__(self, *a):
                ATOM[0] -= 1

        def _sched(kind, eng, fn, R, W):
            def thunk():
                if kind == "op":
                    return S.op(eng, fn, reads=R, writes=W)
                return S.dma(eng, fn, reads=R, writes=W)
            if REC[0] is None:
                return thunk()
            if ATOM[0] > 0:
                REC[0][-1].append(thunk)
            else:
                REC[0].append([thunk])
            return None

        def interleave(chains):
            chains = [c for c in chains if c is not None and len(c.steps) > 0]
            pos = [0] * len(chains)
            while True:
                best, bi = None, -1
                for i, c in enumerate(chains):
                    if pos[i] < len(c.steps):
                        fr = (pos[i] + 0.5) / len(c.steps)
                        if best is None or fr < best:
                            best, bi = fr, i
                if bi < 0:
                    break
                for th in chains[bi].steps[pos[bi]]:
                    th()
                pos[bi] += 1

        def MM(out, lhsT, rhs, start=True, stop=True, R=(), W=()):
            return _sched("op", "pe", lambda e: e.matmul(out, lhsT, rhs, start=start, stop=stop), R, W)

        def TR(out, in_, R=(), W=()):
            return _sched("op", "pe", lambda e: e.transpose(out, in_, ident[0:in_.shape[0], 0:in_.shape[0]]), R, W)

        def ACT(out, in_, func, bias=None, scale=None, accum=None, R=(), W=()):
            kw = {}
            if bias is not None:
                kw["bias"] = bias
            if scale is not None:
                kw["scale"] = scale
            if accum is not None:
                kw["accum_out"] = accum
            return _sched("op", "act", lambda e: e.activation(out=out, in_=in_, func=func, **kw), R, W)

        def TT(eng, out, in0, in1, op, R=(), W=()):
            return _sched("op", eng, lambda e: e.tensor_tensor(out=out, in0=in0, in1=in1, op=op), R, W)

        def TS(eng, out, in0, s1, op0, s2=None, op1=None, R=(), W=()):
            if op1 is None:
                return _sched("op", eng, lambda e: e.tensor_scalar(out=out, in0=in0, scalar1=s1, scalar2=None, op0=op0), R, W)
            return _sched("op", eng, lambda e: e.tensor_scalar(out=out, in0=in0, scalar1=s1, scalar2=s2, op0=op0, op1=op1), R, W)

        def STT(out, in0, scalar, in1, op0, op1, R=(), W=()):
            return _sched("op", "dve", lambda e: e.scalar_tensor_tensor(out=out, in0=in0, scalar=scalar, in1=in1, op0=op0, op1=op1), R, W)

        def CP(eng, out, in_, R=(), W=()):
            if eng == "act":
                return ACT(out, in_, AF.Copy, R=R, W=W)
            return _sched("op", eng, lambda e: e.tensor_copy(out=out, in_=in_), R, W)

        def RECIP(out, in_, R=(), W=()):
            return _sched("op", "dve", lambda e: e.reciprocal(out=out, in_=in_), R, W)

        def MSET(eng, ap, val, W=()):
            return _sched("op", eng, lambda e: e.memset(ap, val), (), W)

        def DMA(q, out, in_, R=(), W=(), **kw):
            return _sched("dma", q, lambda e: e.dma_start(out=out, in_=in_, **kw), R, W)

        def bc(ap, shape):
            return ap.broadcast_to(shape)

        def dump(name, ap, key):
            dt_ = ap.dtype
            shp = [int(v) for v in ap.shape]
            o = nc.dram_tensor("dbg_" + name, shp, dt_, kind="ExternalOutput").ap()
            dbg_toks.append(DMA("sp", o, ap, R=[key], W=["dbg_" + name]))

        out_toks = []
        try:
            ident, k_ident = A.alloc([128], BF16, "ident")
            tribf, k_tribf = A.alloc([2, 128], BF16, "tribf")
            trif, k_trif = A.alloc([2, 128], F32, "trif")
            uppf, k_uppf = A.alloc([2, 128], F32, "uppf")
            onesf, k_onesf = A.alloc([128], F32, "onesf")
            onesb, k_onesb = A.alloc([128], BF16, "onesb")
            negm, k_negm = A.alloc([2, 512], BF16, "negm")
            bdg, k_bdg = A.alloc([256], F32, "bdg")
            bds, k_bds = A.alloc([8], F32, "bds")
            gmask, k_gmask = A.alloc([2], F32, "gmask")
            mixg, k_mixg = A.alloc([8], F32, "mixg")
            dskb, k_dskb = A.alloc([512], F32, "dskb")
            cb, k_cb = A.alloc([6], F32, "cb")
            gate_mix, k_gmix = A.alloc([1024], F32, "gate_mix")
            gate_ffn, k_gffn = A.alloc([1024], F32, "gate_ffn")
            mcols, k_mcols = A.alloc([64], F32, "mcols")
            CONST_K = [k_ident, k_tribf, k_trif, k_uppf, k_onesf, k_onesb, k_negm, k_bdg, k_bds, k_gmask, k_mixg, k_dskb, k_cb]
            def load_consts():
                for (ap, k, src) in [(ident, k_ident, ident_d), (onesf, k_onesf, onesf_d), (bdg, k_bdg, bdg_d),
                                     (bds, k_bds, bds_d), (gmask, k_gmask, gmask_d), (mixg, k_mixg, mixg_d),
                                     (dskb, k_dskb, dsk_d), (cb, k_cb, cb_d)]:
                    DMA("act", ap, src, W=[k])
                DMA("act", tribf, tribf_d.rearrange("p (a b) -> p a b", a=2), W=[k_tribf])
                DMA("act", trif, trif_d.rearrange("p (a b) -> p a b", a=2), W=[k_trif])
                DMA("act", uppf, uppf_d.rearrange("p (a b) -> p a b", a=2), W=[k_uppf])
                DMA("act", negm, negm_d.rearrange("p (a b) -> p a b", a=2), W=[k_negm])
                CP("dve", onesb, onesf, R=[k_onesf], W=[k_onesb])
            win_off = A.off
            win, k_win = A.alloc([8, DIN], BF16, "win")
            win_end = A.off
            win_view = win_d.rearrange("(k p) n -> p k n", p=128)
            K_WIN = [k_win + ":%d" % kc for kc in range(8)]
            scbP, k_scbP = A.alloc([16], BF16, "scbP")
            cwt, k_cwt = A.alloc([54], F32, "cwt")
            cwd, k_cwd = A.alloc([6, 9, 128], BF16, "cwd")
            identf, k_identf = A.alloc([128], F32, "identf")
            wgs, k_wgs = A.alloc([512], BF16, "wgs")
            bgs, k_bgs = A.alloc([512], BF16, "bgs")
            aneg, k_aneg = A.alloc([16], F32, "aneg")
            dtb, k_dtb = A.alloc([16], F32, "dtb")

            def setup_small():
                DMA("act", cwt, cw_d, W=[k_cwt])
                DMA("pool", wgs[0:16, :], wg_d, W=[k_wgs])
                DMA("pool", bgs[0:1, :], bg_d, W=[k_bgs])
                DMA("act", aneg, alog_d, W=[k_aneg])
                DMA("act", dtb, dtb_d, W=[k_dtb])
                ACT(aneg, aneg, AF.Exp, R=[k_aneg], W=[k_aneg])
                TS("dve", aneg, aneg, -1.0, ALU.mult, R=[k_aneg], W=[k_aneg])
                CP("dve", identf, ident, R=[k_ident], W=[k_identf])
                for c6 in range(6):
                    for tap in range(9):
                        TS("dve", cwd[:, c6, tap, :], identf, cwt[:, c6 * 9 + tap: c6 * 9 + tap + 1], ALU.mult,
                           R=[k_identf, k_cwt], W=[k_cwd])
            persist_mark = A.off

            ccs, k_ccs = A.alloc([16], F32, "ccs")
            sc, k_sc = A.alloc([16], F32, "sc")
            mrow, k_mrow = A.alloc([6 * D], F32, "mrow")
            bmod, k_bmod = A.alloc([2 * D], F32, "bmod")
            mctx, k_mctx = A.alloc([2 * D], F32, "mctx")
            ncol, k_ncol = A.alloc([16], F32, "ncol")
            wmb = [A.alloc([8, 512], F32, "wmb%d" % i) for i in range(3)]
            wmbb = [A.alloc([8, 512], BF16, "wmbb%d" % i) for i in range(2)]
            scb, k_scb = A.alloc([16], BF16, "scb")
            wm_view = wmod_d.rearrange("(k p) n -> p k n", p=128)
            DMA("sp", ccs, cc_d, W=[k_ccs])
            for nb in range(3):
                DMA("sp", wmb[nb][0], wm_view[:, :, nb * 512:(nb + 1) * 512], W=[wmb[nb][1]])
            ACT(sc, ccs, AF.Silu, R=[k_ccs], W=[k_sc])
            CP("dve", scb, sc, R=[k_sc], W=[k_scb])
            scv = scb.rearrange("p (k j) -> p k j", j=2)
            CP("dve", scbP, sc, R=[k_sc], W=[k_scbP])
            DMA("act", bmod[0:2, :], bmod_d[:, 0:2 * D], W=[k_bmod])
            DMA("act", ncol, ncol_d, W=[k_ncol])
            load_consts()
            for kc in range(8):
                DMA("pool", win[:, kc, :], win_view[:, kc, :], W=[K_WIN[kc]], max_dma_last_dim=4096)
            if stop == "mod0":
                dump("sc", sc, k_sc)
                dump("tribf", tribf, k_tribf)
                dump("onesb", onesb, k_onesb)
                raise _Stop()
            for nb in range(4):
                wt, kw_ = wmb[nb % 3]
                wb, kwb = wmbb[nb % 2]
                CP("dve", wb[:, 0:4, :], wt[:, 0:4, :], R=[kw_], W=[kwb])
                CP("act", wb[:, 4:8, :], wt[:, 4:8, :], R=[kw_], W=[kwb])
                if nb + 3 < 4:
                    DMA("sp", wt, wm_view[:, :, (nb + 3) * 512:(nb + 4) * 512], W=[kw_])
                pt, kp = PB[nb % 2]
                for kc in range(8):
                    MM(pt[0:2, :], scv[:, kc, :], wb[:, kc, :], start=(kc == 0), stop=(kc == 7), R=[k_scb, kwb], W=[kp])
                TT("dve", mrow[0:2, nb * 512:(nb + 1) * 512], pt[0:2, :], bmod[0:2, nb * 512:(nb + 1) * 512], ALU.add,
                   R=[kp, k_bmod], W=[k_mrow])
            if stop == "mod1":
                dump("mrow", mrow[0:2, :], k_mrow)
                raise _Stop()
            setup_small()
            DMA("sp", mctx[0:1, :], mrow[1:2, 0:2 * D], R=[k_mrow], W=[k_mctx])
            if stop == "mod2":
                dump("mctx", mctx[0:1, :], k_mctx)
                raise _Stop()
            pcol, kpc = PB[2]
            col_srcs = [(0, mrow, 0), (1, mrow, D), (4, mctx, 0), (5, mctx, D)]
            for (vi, src, off) in col_srcs:
                for kc in range(8):
                    ci = vi * 8 + kc
                    MM(pcol[:, 2 * ci: 2 * ci + 2], src[0:1, off + kc * 128: off + (kc + 1) * 128], onesf[0:1, 0:2],
                       R=[k_mrow, k_mctx, k_onesf], W=[kpc])
            CP("dve", mcols[:, 0:16], pcol[:, 0:32].rearrange("p (c two) -> p c two", two=2)[:, :, 0], R=[kpc], W=[k_mcols])
            CP("dve", mcols[:, 32:48], pcol[:, 64:96].rearrange("p (c two) -> p c two", two=2)[:, :, 0], R=[kpc], W=[k_mcols])
            if stop == "mod3":
                dump("mcols", mcols[:, 0:48], k_mcols)
                raise _Stop()
            for (sl, g0) in [(slice(8, 16), 0), (slice(40, 48), 0)]:
                TS("dve", mcols[:, sl], mcols[:, sl], 1.0, ALU.add, R=[k_mcols], W=[k_mcols])
                TT("dve", mcols[:, sl], mcols[:, sl], ncol[:, g0:g0 + 8], ALU.mult, R=[k_mcols, k_ncol], W=[k_mcols])
            SH_LAT, G_LAT, SH_FFN, G_FFN, SH_CTX, G_CTX = 0, 8, 16, 24, 32, 40
            if stop == "mod":
                dump("mcols", mcols[:, 0:48], k_mcols)
                dump("gate_mix", gate_mix, k_gmix)
                dump("gate_ffn", gate_ffn, k_gffn)
                raise _Stop()
            if debug and "mcols" in dbg_out:
                DMA("sp", dbg_out["mcols"], mcols[:, 0:48], R=[k_mcols], W=["dbg_mcols"])
            if debug and "gate_mix" in dbg_out:
                DMA("sp", dbg_out["gate_mix"], gate_mix, R=[k_gmix], W=["dbg_gm"])
            S.barrier()
            A.off = persist_mark

            if stop == "A0":
                dump("win0", win[:, 0, 0:512], K_WIN[0])
                dump("win7", win[:, 7, 2352:2864], K_WIN[7])
                dump("cwd", cwd[:, 2, :, :], k_cwd)
                dump("aneg", aneg, k_aneg)
                dump("wgs", wgs[0:16, :], k_wgs)
                raise _Stop()
            class FS:
                pass

            def new_fs(tag):
                f = FS()
                f.tag = tag
                f.qT, f.k_qT = A.alloc([2, 256], BF16, "qT" + tag)
                f.kT, f.k_kT = A.alloc([2, 256], BF16, "kT" + tag)
                f.ktm, f.k_ktm = A.alloc([2, 256], BF16, "ktm" + tag)
                f.v, f.k_v = A.alloc([2, 512], BF16, "v" + tag)
                f.g, f.k_g = A.alloc([2, 2, 256], BF16, "g" + tag)
                f.x, f.k_x = A.alloc([2, 512], BF16, "x" + tag)
                f.Btm, f.k_Btm = A.alloc([2, 128], BF16, "Btm" + tag)
                f.BTm, f.k_BTm = A.alloc([2, 256], BF16, "BTm" + tag)
                f.CT, f.k_CT = A.alloc([256], BF16, "CT" + tag)
                f.dt, f.k_dt = A.alloc([2, 16], F32, "dt" + tag)
                f.a, f.k_a = A.alloc([2, 16], F32, "a" + tag)
                f.r, f.k_r = A.alloc([2, 512], BF16, "r" + tag)
                f.z, f.k_z = A.alloc([2, 512], BF16, "z" + tag)
                return f

            fsets = [new_fs("A"), new_fs("B")]

            xt = [A.alloc([D], F32, "xt%d" % i) for i in range(3)]
            xsb = [A.alloc([D], BF16, "xs%d" % i) for i in range(3)]
            stat, k_stat = A.alloc([3, 4], F32, "stat")
            hTs = [A.alloc([8, 384], BF16, "hT%d" % i) for i in range(2)]
            hT, k_hT = hTs[0]
            hT_rr = [0]
            pcl, k_pcl = A.alloc([6, 6, 66], BF16, "pcl")
            xa, k_xa = A.alloc([6, 256], BF16, "xa")
            lrT, k_lrT = A.alloc([2, 256], BF16, "lrT")
            gwe, k_gwe = A.alloc([512], F32, "gwe")
            gwl, k_gwl = gwe, k_gwe
            sile, k_sile = A.alloc([512], F32, "sile")
            cbn, k_cbn = A.alloc([6], F32, "cbn")
            TS("dve", cbn, cb, -1.0, ALU.mult, R=[k_cb], W=[k_cbn])
            dtw, k_dtw = A.alloc([16], F32, "dtw")
            MSET("pool", pcl, 0.0, W=[k_pcl])

            PT_BF = [(psum[4][:, :].bitcast(BF16), "pb4"), (psum[5][:, :].bitcast(BF16), "pb5")]

            evac_rr = [0]

            def evac_eng():
                evac_rr[0] += 1
                return "dve" if evac_rr[0] % 3 == 0 else "act"

            def feats_norm(hsel, src, tok0, is_ctx, nseq):
                hT, k_hT = hTs[hsel]
                ntok_ext = 256 if is_ctx else 384
                ntile = 2 if is_ctx else 3
                for ti in range(ntile):
                    xtt, kx = xt[ti]
                    if ti < 2:
                        DMA("sp", xtt, src[tok0 + ti * 128: tok0 + (ti + 1) * 128, :], W=[kx])
                    else:
                        lo = tok0 - 64 if tok0 >= 64 else 0
                        hi = tok0 + 256 if tok0 + 320 <= nseq else 0
                        DMA("sp", xtt[0:64, :], src[lo:lo + 64, :], W=[kx])
                        DMA("sp", xtt[64:128, :], src[hi:hi + 64, :], W=[kx])
                    ss = stat[:, ti, 0:1]
                    kst_ = k_stat + ":%d" % ti
                    ACT(xsb[ti][0], xtt, AF.Square, accum=ss, R=[kx], W=[xsb[ti][1], kst_])
                    ACT(stat[:, ti, 1:2], ss, AF.Ln, scale=1.0 / D, bias=1e-6, R=[kst_], W=[kst_])
                    ACT(stat[:, ti, 3:4], stat[:, ti, 1:2], AF.Exp, scale=-0.5, R=[kst_], W=[kst_])
                    xs_, kxs = xsb[ti]
                    TS("dve", xs_, xtt, stat[:, ti, 3:4], ALU.mult, R=[kx, kst_], W=[kxs])
                gofs = G_CTX if is_ctx else G_LAT
                sofs = SH_CTX if is_ctx else SH_LAT
                pt, kpt = PT_BF[0]
                for kp in range(4):
                    with atomic():
                        for j in range(2):
                            kc = kp * 2 + j
                            for ti in range(ntile):
                                TR(pt[:, j * 384 + ti * 128: j * 384 + (ti + 1) * 128], xsb[ti][0][:, kc * 128:(kc + 1) * 128],
                                   R=[xsb[ti][1], k_ident], W=[kpt])
                        for j in range(2):
                            kc = kp * 2 + j
                            TS("dve", hT[:, kc, 0:ntok_ext], pt[:, j * 384: j * 384 + ntok_ext],
                               mcols[:, gofs + kc: gofs + kc + 1], ALU.mult, mcols[:, sofs + kc: sofs + kc + 1], ALU.add,
                               R=[kpt, k_mcols], W=[k_hT + ":%d" % kc])

            prr = [0]

            def nextp():
                prr[0] += 1
                return PB[prr[0] % 2]

            def feats_proj(fs, hsel, src, tok0, is_ctx, nseq, mode="full"):
                hT, k_hT = hTs[hsel]
                ntok_ext = 256 if is_ctx else 384
                K_HT = [k_hT + ":%d" % kc for kc in range(8)]
                full = (mode == "full")
                dirs = [1] if mode == "sB" else [0, 1]
                nc6 = 6 if full else 5
                fm = []
                if full:
                    fm += [("q", 0, 128, 0), ("q", 128, 128, 1), ("k", 256, 128, 0), ("k", 384, 128, 1)]
                fm += [("lr", 1536 + 16 * d_, 16, d_) for d_ in dirs]
                for (nm, c0, M, idx) in fm:
                    pt, kpt = nextp()
                    with atomic():
                        for kc in range(8):
                            MM(pt[0:M, 0:256], win[:, kc, c0:c0 + M], hT[:, kc, 0:256], start=(kc == 0), stop=(kc == 7),
                               R=[K_WIN[kc], K_HT[kc]], W=[kpt])
                        if nm == "q":
                            ACT(fs.qT[:, idx, :], pt[:, 0:256], AF.Copy, scale=0.125, R=[kpt], W=[fs.k_qT])
                        elif nm == "k":
                            CP("act", fs.kT[:, idx, :], pt[:, 0:256], R=[kpt], W=[fs.k_kT])
                        else:
                            CP("act", lrT[0:16, idx, :], pt[0:16, 0:256], R=[kpt], W=[k_lrT])
                pc = pcx if is_ctx else pcl
                kpc_ = k_pcx if is_ctx else k_pcl
                for c6 in range(nc6):
                    pt, kpt = nextp()
                    with atomic():
                        for kc in range(8):
                            MM(pt[:, 0:ntok_ext], win[:, kc, 2080 + c6 * 128: 2080 + (c6 + 1) * 128], hT[:, kc, 0:ntok_ext],
                               start=(kc == 0), stop=(kc == 7), R=[K_WIN[kc], K_HT[kc]], W=[kpt])
                        eng = evac_eng()
                        if is_ctx:
                            CP(eng, pc[:, c6, 1, 1:257], pt[:, 0:256], R=[kpt], W=[kpc_])
                        else:
                            CP(eng, pc[:, c6, 1:5, 1:65], pt[:, 0:256].rearrange("p (r c) -> p r c", r=4), R=[kpt], W=[kpc_])
                            eng = evac_eng()
                            if tok0 >= 64 and tok0 + 320 <= nseq:
                                CP(eng, pc[:, c6, 0:6:5, 1:65], pt[:, 256:384].rearrange("p (r c) -> p r c", r=2), R=[kpt], W=[kpc_])
                            else:
                                if tok0 >= 64:
                                    CP(eng, pc[:, c6, 0, 1:65], pt[:, 256:320], R=[kpt], W=[kpc_])
                                else:
                                    MSET("pool", pc[:, c6, 0, 1:65], 0.0, W=[kpc_])
                                if tok0 + 320 <= nseq:
                                    CP(eng, pc[:, c6, 5, 1:65], pt[:, 320:384], R=[kpt], W=[kpc_])
                                else:
                                    MSET("pool", pc[:, c6, 5, 1:65], 0.0, W=[kpc_])
                for tpass in range(4):
                    ti = tpass % 2
                    if tpass < 2:
                        tml = [("v", 512, 512), ("k", 256, 256), ("dt", 2848, 16)]
                    else:
                        tml = [("r", 1024, 512), ("z", 1568, 512)] if full else []
                    for (nm, c0, N) in tml:
                        pt, kpt = nextp()
                        with atomic():
                            for kc in range(8):
                                MM(pt[:, 0:N], hT[:, kc, ti * 128:(ti + 1) * 128], win[:, kc, c0:c0 + N], start=(kc == 0), stop=(kc == 7),
                                   R=[K_WIN[kc], K_HT[kc]], W=[kpt])
                            if nm == "v":
                                CP("act", fs.v[:, ti, :], pt[:, :], R=[kpt], W=[fs.k_v])
                            elif nm == "r":
                                ACT(fs.r[:, ti, :], pt[:, :], AF.Silu, R=[kpt], W=[fs.k_r])
                            elif nm == "z":
                                ACT(fs.z[:, ti, :], pt[:, :], AF.Silu, R=[kpt], W=[fs.k_z])
                            elif nm == "k":
                                CP("dve", fs.ktm[:, ti, :], pt[:, 0:256], R=[kpt], W=[fs.k_ktm])
                            else:
                                TT("dve", dtw, pt[:, 0:16], dtb, ALU.add, R=[kpt, k_dtb], W=[k_dtw])
                        if nm == "dt":
                            ACT(dtw, dtw, AF.Exp, R=[k_dtw], W=[k_dtw])
                            ACT(fs.dt[:, ti, :], dtw, AF.Ln, bias=1.0, R=[k_dtw], W=[fs.k_dt])
                            TT("dve", fs.a[:, ti, :], fs.dt[:, ti, :], aneg, ALU.mult, R=[fs.k_dt, k_aneg], W=[fs.k_a])
                    if tpass >= 2:
                        continue
                    pt, kpt = nextp()
                    gsl = slice(dirs[0] * 256, 512)
                    with atomic():
                        for d in dirs:
                            MM(pt[:, d * 256:(d + 1) * 256], lrT[0:16, d, ti * 128:(ti + 1) * 128], wgs[0:16, d * 256:(d + 1) * 256],
                               start=True, stop=False, R=[k_lrT, k_wgs], W=[kpt])
                            MM(pt[:, d * 256:(d + 1) * 256], onesb[0:1, :], bgs[0:1, d * 256:(d + 1) * 256],
                               start=False, stop=True, R=[k_onesb, k_bgs], W=[kpt])
                        ACT(gwe[:, gsl], pt[:, gsl], AF.Exp, scale=-1.0, R=[kpt], W=[k_gwe])
                    ACT(gwl[:, gsl], gwe[:, gsl], AF.Ln, bias=1.0, R=[k_gwe], W=[k_gwl])
                    if os.environ.get("X_G"):
                        TS("pool", fs.g[:, dirs[0]:2, ti, :], gwl[:, gsl].rearrange("p (d f) -> p d f", f=256), -1.0 / 16.0, ALU.mult,
                           R=[k_gwl], W=[fs.k_g])
                    else:
                        ACT(fs.g[:, dirs[0]:2, ti, :], gwl[:, gsl].rearrange("p (d f) -> p d f", f=256), AF.Copy, scale=-1.0 / 16.0,
                            R=[k_gwl], W=[fs.k_g])
                taps = [(0, dc) for dc in (-1, 0, 1)] if is_ctx else [(dr, dc) for dr in (-1, 0, 1) for dc in (-1, 0, 1)]
                for c6 in range(nc6):
                    pt, kpt = nextp()
                    with atomic():
                        for i, (dr, dc) in enumerate(taps):
                            tap = (dr + 1) * 3 + (dc + 1)
                            if is_ctx:
                                rhs = pc[:, c6, 1, 1 + dc: 257 + dc]
                            else:
                                rhs = pc[:, c6, 1 + dr: 5 + dr, 1 + dc: 65 + dc]
                            MM(pt[:, 0:256], cwd[:, c6, tap, :], rhs, start=(i == 0), stop=(i == len(taps) - 1), R=[k_cwd, kpc_], W=[kpt])
                        if c6 == 5:
                            cdst, kcd = fs.CT, fs.k_CT
                        else:
                            cdst, kcd = xa[:, c6, :], k_xa + ":%d" % c6
                        ACT(cdst, pt[:, 0:256], AF.Silu, bias=cb[:, c6:c6 + 1], R=[kpt, k_cb], W=[kcd])
                if full:
                    for g in range(2):
                        TS("dve", fs.BTm[:, g, :], xa[:, 4, :], gmask[:, g:g + 1], ALU.mult, R=[k_xa + ":4", k_gmask], W=[fs.k_BTm])
                pt, kpt = PT_BF[0]
                for ti in range(2):
                    with atomic():
                        for c6 in range(5):
                            TR(pt[:, c6 * 128:(c6 + 1) * 128], xa[:, c6, ti * 128:(ti + 1) * 128], R=[k_xa + ":%d" % c6, k_ident], W=[kpt])
                        CP("dve", fs.x[:, ti, :], pt[:, 0:512], R=[kpt], W=[fs.k_x])
                        CP("dve", fs.Btm[:, ti, :], pt[:, 512:640], R=[kpt], W=[fs.k_Btm])

            Sg = [A.alloc([2, 256], F32, "Sg%d" % d) for d in range(2)]
            Sgb = [A.alloc([2, 256], BF16, "Sgb%d" % d) for d in range(2)]
            Ss = [A.alloc([512], F32, "Ss%d" % d) for d in range(2)]
            Ssb = [A.alloc([512], BF16, "Ssb%d" % d) for d in range(2)]
            for d in range(2):
                MSET("pool", Sg[d][0], 0.0, W=[Sg[d][1]])
                MSET("pool", Sgb[d][0], 0.0, W=[Sgb[d][1]])
                MSET("pool", Ss[d][0], 0.0, W=[Ss[d][1]])
                MSET("pool", Ssb[d][0], 0.0, W=[Ssb[d][1]])
            eq, k_eq = A.alloc([2, 256], F32, "eq")
            ek, k_ek = A.alloc([2, 256], F32, "ek")
            qs, k_qs = A.alloc([2, 256], BF16, "qs")
            ks = [A.alloc([2, 256], BF16, "ks%d" % h) for h in range(2)]
            ekt, k_ekt = A.alloc([2, 256], F32, "ekt")
            kst, k_kst = A.alloc([2, 256], BF16, "kst")
            attm, k_attm = A.alloc([4, 128], BF16, "attm")
            tmpg, k_tmpg = A.alloc([2, 256], F32, "tmpg")
            _mark = A.off
            pcx, k_pcx = A.alloc([6, 3, 258], BF16, "pcx")
            MSET("dve", pcx, 0.0, W=[k_pcx])
            A.off = _mark
            yo = [A.alloc([1024], F32, "yo%d" % i) for i in range(2)]
            sq, k_sq = A.alloc([512], F32, "sq")
            t1, k_t1 = A.alloc([512], F32, "t1")
            yfr = yo
            e3, k_e3 = A.alloc([3, 8], F32, "e3")
            wj, k_wj = A.alloc([8], F32, "wj")
            p1s, k_p1s = A.alloc([8], F32, "p1s")
            cspb, k_cspb = A.alloc([48], BF16, "cspb")
            uall, k_uall = A.alloc([128], BF16, "uall")
            vall, k_vall = A.alloc([1024], BF16, "vall")
            LT, k_LT = A.alloc([1024], BF16, "LT")
            scT, k_scT = A.alloc([8, 128], BF16, "scT")
            xdt, k_xdt = A.alloc([8, 64], BF16, "xdt")
            xw, k_xw = A.alloc([8, 64], BF16, "xw")
            tmpi, k_tmpi = A.alloc([512], F32, "tmpi")
            tmps, k_tmps = A.alloc([512], F32, "tmps")
            MSET("pool", cspb, 0.0, W=[k_cspb])
            DMA("sp", uall[0:48, :], uallt_d, W=[k_uall])
            DMA("sp", vall[0:48, :], vallt_d, W=[k_vall])
            bd16, k_bd16 = A.alloc([1024], BF16, "bd16")
            DMA("sp", bd16[0:16, :], vallt_d[32:48, :], W=[k_bd16])

            mst, k_mst = A.alloc([16], F32, "mst")
            ob, k_ob = xsb[2]
            oT, k_oT = xsb[1][0].rearrange("p (k t) -> p k t", k=8), xsb[1][1]
            xo, k_xo = xt[2]
            tmo, k_tmo = xt[1]
            pst, k_pst = A.alloc([4], F32, "pst")
            print("arena phase A bytes:", A.off)

            PG_A, PG_B = PB[2], PB[3]
            P5 = psum[5]
            P5BF = psum[5][:, :].bitcast(BF16)
            PBIG = psum2

            def gla_prep(fs, d, full):
                pt, kpt = PG_A
                with atomic():
                    for ti in range(2):
                        for p in range(2):
                            MM(pt[:, p * 256 + ti * 128: p * 256 + (ti + 1) * 128], fs.g[:, d, ti, p * 128:(p + 1) * 128], tribf[:, d, :],
                               R=[fs.k_g, k_tribf], W=[kpt])
                    ptv = pt[:, :].rearrange("p (a b) -> p a b", a=2)
                    ACT(eq, ptv, AF.Exp, R=[kpt], W=[k_eq])
                    if full:
                        ACT(ek, ptv, AF.Exp, scale=-1.0, R=[kpt], W=[k_ek])
                if full:
                    TT("dve", qs, fs.qT, eq, ALU.mult, R=[fs.k_qT, k_eq], W=[k_qs])
                    for hh in range(2):
                        STT(ks[hh][0], fs.kT, gmask[:, hh:hh + 1], ek, ALU.mult, ALU.mult, R=[fs.k_kT, k_gmask, k_ek], W=[ks[hh][1]])
                pt2, kpt2 = PG_B
                with atomic():
                    for ti in range(2):
                        MM(pt2[:, ti * 256:(ti + 1) * 256], tribf[:, d, :], fs.g[:, d, ti, :], R=[fs.k_g, k_tribf], W=[kpt2])
                    ACT(ekt, pt2[:, :].rearrange("p (a b) -> p a b", a=2), AF.Exp, scale=-1.0, R=[kpt2], W=[k_ekt])
                TT("pool", kst, fs.ktm, ekt, ALU.mult, R=[fs.k_ktm, k_ekt], W=[k_kst])

            def gla_tile(fs, d, full, ti):
                last = 127 if d == 0 else 0
                sg, ksg = Sg[d]
                sgb, ksgb = Sgb[d]
                yot, kyo = yo[ti]
                tsl = slice(ti * 128, (ti + 1) * 128)
                if full:
                    pa, kpa = PG_A
                    with atomic():
                        for h in range(4):
                            p, hh = h // 2, h % 2
                            MM(pa[:, h * 128:(h + 1) * 128], ks[hh][0][:, p, tsl], qs[:, p, tsl], R=[ks[hh][1], k_qs], W=[kpa])
                        TT("dve", attm, pa[:, :].rearrange("p (h i) -> p h i", h=4), bc(trif[:, d, :].unsqueeze(1), [128, 4, 128]), ALU.mult,
                           R=[kpa, k_trif], W=[k_attm])
                    po, kpo = PG_B
                    with atomic():
                        for p in range(2):
                            MM(po[:, p * 256:(p + 1) * 256], qs[:, p, tsl], sgb[:, p, :], start=True, stop=False, R=[k_qs, ksgb], W=[kpo])
                            for hh in range(2):
                                h = 2 * p + hh
                                MM(po[:, h * 128:(h + 1) * 128], attm[:, h, :], fs.v[:, ti, h * 128:(h + 1) * 128], start=False, stop=(hh == 1),
                                   R=[k_attm, fs.k_v], W=[kpo])
                        if d == 0:
                            CP("act", yot[:, 0:512], po[:, :], R=[kpo], W=[kyo + ":g"])
                        else:
                            TT("dve", yot[:, 0:512], po[:, :], yot[:, 0:512], ALU.add, R=[kpo, kyo + ":g"], W=[kyo + ":g"])
                psg, kpsg = PG_A
                with atomic():
                    for p in range(2):
                        MM(psg[:, p * 256:(p + 1) * 256], kst[:, ti, p * 128:(p + 1) * 128], fs.v[:, ti, p * 256:(p + 1) * 256],
                           R=[k_kst, fs.k_v], W=[kpsg])
                    TT("dve", tmpg, sg, psg[:, :].rearrange("p (a b) -> p a b", a=2), ALU.add, R=[ksg, kpsg], W=[k_tmpg])
                for p in range(2):
                    STT(sg[:, p, :], tmpg[:, p, :], eq[:, p, ti * 128 + last: ti * 128 + last + 1], bdg, ALU.mult, ALU.mult,
                        R=[k_tmpg, k_eq, k_bdg], W=[ksg])
                CP("act", sgb, sg, R=[ksg], W=[ksgb])

            def ssd_tile(fs, d, full, ti):
                ssd_, kss = Ss[d]
                ssb, kssb = Ssb[d]
                yot, kyo = yo[ti]
                tsl = slice(ti * 128, (ti + 1) * 128)
                av = fs.a[:, ti, d * 8:(d + 1) * 8]
                kp1 = "pb5:a"
                with atomic():
                    MM(P5[:, 0:8], trif[:, d, :], av, R=[k_trif, fs.k_a], W=[kp1])
                    MM(P5[:, 8:16], uppf[:, d, :], av, R=[k_uppf, fs.k_a], W=[kp1])
                    MM(P5[:, 16:24], onesf, av, R=[k_onesf, fs.k_a], W=[kp1])
                    ACT(e3, P5[:, 0:24].rearrange("p (a b) -> p a b", a=3), AF.Exp, R=[kp1], W=[k_e3])
                    if full:
                        ACT(p1s, P5[:, 0:8], AF.Copy, R=[kp1], W=[k_p1s])
                if full:
                    CP("pool", cspb[:, 0:8], p1s, R=[k_p1s], W=[k_cspb])
                    TT("pool", cspb[:, 8:16], p1s, cspb[:, 0:8], ALU.subtract, R=[k_p1s, k_cspb], W=[k_cspb])
                    TS("pool", cspb[:, 32:48], cspb[:, 0:16], -1.0, ALU.mult, R=[k_cspb], W=[k_cspb])
                TT("pool", wj, e3[:, 1, :], fs.dt[:, ti, d * 8:(d + 1) * 8], ALU.mult, R=[k_e3, fs.k_dt], W=[k_wj])
                if full:
                    ptb = P5BF[:, 640:768]
                    kptb = "pb5:t"
                    with atomic():
                        TR(ptb[0:48, :], cspb, R=[k_cspb, k_ident], W=[kptb])
                        TT("dve", vall[0:16, :].rearrange("p (h i) -> p h i", h=8), bc(ptb[0:16, :].unsqueeze(1), [16, 8, 128]),
                           bd16[0:16, :].rearrange("p (h i) -> p h i", h=8), ALU.mult, R=[kptb, k_bd16], W=[k_vall])
                        CP("act", uall[32:48, :], ptb[32:48, :], R=[kptb], W=[k_uall])
                    kb0, kb1 = "pbig:0", "pbig:1"
                    with atomic():
                        for half in range(2):
                            MM(PBIG[:, half * 512:(half + 1) * 512], uall[0:48, :], vall[0:48, half * 512:(half + 1) * 512], start=True, stop=False,
                               R=[k_uall, k_vall], W=[kb0 if half == 0 else kb1])
                            MM(PBIG[:, half * 512:(half + 1) * 512], ident, negm[:, d, :], start=False, stop=True, R=[k_ident, k_negm],
                               W=[kb0 if half == 0 else kb1])
                        for half in range(2):
                            ACT(LT[:, half * 512:(half + 1) * 512], PBIG[:, half * 512:(half + 1) * 512], AF.Exp,
                                R=[kb0 if half == 0 else kb1], W=[k_LT])
                    kpG = "pb5:g"
                    with atomic():
                        for g in range(2):
                            MM(P5[:, 32 + g * 128: 32 + (g + 1) * 128], fs.BTm[:, g, tsl], fs.CT[:, tsl], R=[fs.k_BTm, fs.k_CT], W=[kpG])
                        TT("dve", scT.rearrange("p (g h) i -> p g h i", g=2),
                           bc(P5[:, 32:288].rearrange("p (g i) -> p g i", g=2).unsqueeze(2), [128, 2, 4, 128]),
                           LT.rearrange("p (g h i) -> p g h i", g=2, h=4), ALU.mult, R=[kpG, k_LT], W=[k_scT])
                    TT("pool", xdt, fs.x[:, ti, :].rearrange("p (h q) -> p h q", h=8),
                       bc(fs.dt[:, ti, d * 8:(d + 1) * 8].unsqueeze(2), [128, 8, 64]), ALU.mult, R=[fs.k_x, fs.k_dt], W=[k_xdt])
                    with atomic():
                        for h in range(8):
                            MM(PBIG[:, h * 64:(h + 1) * 64], scT[:, h, :], xdt[:, h, :], R=[k_scT, k_xdt], W=[kb0])
                        MM(PBIG[:, 512:1024], fs.CT[:, tsl], ssb, R=[fs.k_CT, kssb], W=[kb1])
                        TT("dve", tmpi.rearrange("p (h q) -> p h q", h=8), PBIG[:, 512:1024].rearrange("p (h q) -> p h q", h=8),
                           bc(e3[:, 0, :].unsqueeze(2), [128, 8, 64]), ALU.mult, R=[kb1, k_e3], W=[k_tmpi])
                        if d == 1:
                            TT("dve", tmpi, tmpi, yot[:, 512:1024], ALU.add, R=[k_tmpi, kyo + ":s"], W=[k_tmpi])
                        TT("dve", yot[:, 512:1024], PBIG[:, 0:512], tmpi, ALU.add, R=[kb0, k_tmpi], W=[kyo + ":s"])
                TT("pool", xw, fs.x[:, ti, :].rearrange("p (h q) -> p h q", h=8), bc(wj.unsqueeze(2), [128, 8, 64]), ALU.mult,
                   R=[fs.k_x, k_wj], W=[k_xw])
                TT("pool" if (full and d == 1) else "dve", tmps.rearrange("p (h q) -> p h q", h=8), ssd_.rearrange("p (h q) -> p h q", h=8),
                   bc(e3[:, 2, :].unsqueeze(2), [128, 8, 64]), ALU.mult, R=[kss, k_e3], W=[k_tmps])
                with atomic():
                    MM(PBIG[:, 0:512], fs.Btm[:, ti, :], xw.rearrange("p h q -> p (h q)"), R=[fs.k_Btm, k_xw], W=["pbig:0"])
                    TT("dve", ssd_, tmps, PBIG[:, 0:512], ALU.add, R=[k_tmps, "pbig:0"], W=[kss])
                if os.environ.get("X_SSB"):
                    TT("pool", ssb.rearrange("p (h q) -> p h q", h=8), ssd_.rearrange("p (h q) -> p h q", h=8),
                       bc(bds.unsqueeze(2), [128, 8, 64]), ALU.mult, R=[kss, k_bds], W=[kssb])
                else:
                    for g in range(2):
                        TS("dve", ssb[:, g * 256:(g + 1) * 256], ssd_[:, g * 256:(g + 1) * 256], gmask[:, g:g + 1], ALU.mult,
                           R=[kss, k_gmask], W=[kssb])

            obs = [xsb[2], xsb[0]]

            def merge_gla(fs, ti, T):
                yot, kyo = yo[ti]
                ob_, kob = obs[T % 2]
                for h in range(4):
                    ACT(sq.bitcast(BF16)[:, h * 128:(h + 1) * 128], yot[:, h * 128:(h + 1) * 128], AF.Square, accum=mst[:, h:h + 1],
                        R=[kyo + ":g"], W=[k_sq + ":g", k_mst])
                ACT(mst[:, 0:4], mst[:, 0:4], AF.Ln, scale=1.0 / 128, bias=1e-6, R=[k_mst], W=[k_mst])
                ACT(mst[:, 0:4], mst[:, 0:4], AF.Exp, scale=-0.5, R=[k_mst], W=[k_mst])
                for h in range(4):
                    STT(ob_[:, h * 128:(h + 1) * 128], yot[:, h * 128:(h + 1) * 128], mst[:, h:h + 1], fs.r[:, ti, h * 128:(h + 1) * 128],
                        ALU.mult, ALU.mult, R=[kyo + ":g", k_mst, fs.k_r], W=[kob + ":g"])

            def merge_ssd(fs, ti, T):
                yot, kyo = yo[ti]
                ob_, kob = obs[T % 2]
                TT("pool", t2, fs.x[:, ti, :], dskb, ALU.mult, R=[fs.k_x, k_dskb], W=[k_t2])
                TT("dve", t2, t2, yot[:, 512:1024], ALU.add, R=[k_t2, kyo + ":s"], W=[k_t2])
                TT("pool", t2, t2, fs.z[:, ti, :], ALU.mult, R=[k_t2, fs.k_z], W=[k_t2])
                kmb = k_mst + "b"
                for g in range(2):
                    ACT(sq.bitcast(BF16)[:, 512 + g * 256: 512 + (g + 1) * 256], t2[:, g * 256:(g + 1) * 256], AF.Square, accum=mst[:, 4 + g:5 + g],
                        R=[k_t2], W=[k_sq + ":s", kmb])
                ACT(mst[:, 4:6], mst[:, 4:6], AF.Ln, scale=1.0 / 256, bias=1e-6, R=[kmb], W=[kmb])
                ACT(mst[:, 4:6], mst[:, 4:6], AF.Exp, scale=-0.5, R=[kmb], W=[kmb])
                for g in range(2):
                    ACT(ob_[:, 512 + g * 256: 512 + (g + 1) * 256], t2[:, g * 256:(g + 1) * 256], AF.Copy, scale=mst[:, 4 + g:5 + g],
                        R=[k_t2, kmb], W=[kob + ":s"])

            def merge_out(T):
                ob_, kob = obs[T % 2]
                ptb, kptb = PT_BF[0]
                DMA("sp", xo, x_d[T * 128:(T + 1) * 128, :], W=[k_xo])
                with atomic():
                    for kc in range(8):
                        TR(ptb[:, kc * 128:(kc + 1) * 128], ob_[:, kc * 128:(kc + 1) * 128], R=[kob + ":g", kob + ":s", k_ident], W=[kptb])
                    TT("dve", oT, ptb[:, 0:1024].rearrange("p (k t) -> p k t", k=8), bc(mixg.unsqueeze(2), [128, 8, 128]), ALU.mult,
                       R=[kptb, k_mixg], W=[k_oT])
                with atomic():
                    for nb in range(2):
                        pop, kpop = PB[nb]
                        for kc in range(8):
                            MM(pop[:, :], oT[:, kc, :], wout[:, kc, nb * 512:(nb + 1) * 512], start=(kc == 0), stop=(kc == 7),
                               R=[k_oT, k_wout], W=[kpop])
                        ACT(t1.bitcast(BF16)[:, nb * 512:(nb + 1) * 512], pop[:, :], AF.Square, accum=pst[:, nb:nb + 1],
                            R=[kpop], W=[k_t1, k_pst])
                    TT("dve", pst[:, 2:3], pst[:, 0:1], pst[:, 1:2], ALU.add, R=[k_pst], W=[k_pst])
                    ACT(pst[:, 2:3], pst[:, 2:3], AF.Ln, scale=1.0 / D, bias=1e-6, R=[k_pst], W=[k_pst])
                    ACT(pst[:, 3:4], pst[:, 2:3], AF.Exp, scale=-0.5, R=[k_pst], W=[k_pst])
                    for nb in range(2):
                        pop, kpop = PB[nb]
                        STT(tmo[:, nb * 512:(nb + 1) * 512], pop[:, :], pst[:, 3:4], gate_mix[:, nb * 512:(nb + 1) * 512],
                            ALU.mult, ALU.mult, R=[kpop, k_pst, k_gmix], W=[k_tmo])
                TT("dve", tmo[:, 0:512], tmo[:, 0:512], xo[:, 0:512], ALU.add, R=[k_tmo, k_xo], W=[k_tmo])
                TT("pool", tmo[:, 512:1024], tmo[:, 512:1024], xo[:, 512:1024], ALU.add, R=[k_tmo, k_xo], W=[k_tmo])
                DMA("pool", xn_scr[T * 128:(T + 1) * 128, :], tmo, R=[k_tmo], W=["xn_scr:%d" % T])

            fs_i = [0]

            def next_fs():
                fs_i[0] += 1
                return fsets[fs_i[0] % 2]

            FS_NAMES = ["qT", "kT", "ktm", "v", "g", "x", "Btm", "BTm", "CT", "dt", "a", "r", "z"]
            fs_scr = {}
            for nm in FS_NAMES:
                ap0 = getattr(fsets[0], nm)
                shp = [8] + [int(v) for v in ap0.shape]
                fs_scr[nm] = nc.dram_tensor("fsscr_" + nm, shp, ap0.dtype, kind="Internal").ap()

            def spill_fs(f, blk):
                for nm in FS_NAMES:
                    DMA("pool", fs_scr[nm][blk], getattr(f, nm), R=[getattr(f, "k_" + nm)], W=["fsscr:%s:%d" % (nm, blk)])

            def reload_fs(f, blk):
                for nm in FS_NAMES:
                    DMA("sp", getattr(f, nm), fs_scr[nm][blk], R=["fsscr:%s:%d" % (nm, blk)], W=[getattr(f, "k_" + nm)])

            seq = [(ctx_d, 0, True, CTX, "s2", [(0, False), (1, False)], -1)]
            for blk in range(15, 7, -1):
                seq.append((x_d, blk * 256, False, SEQ, "sB", [(1, False)], blk))
            for blk in range(0, 8):
                seq.append((x_d, blk * 256, False, SEQ, "full", [(0, True)], blk))
            fsl = []
            for j in range(len(seq)):
                fsl.append(next_fs())
            feats_norm(0, *seq[0][0:4])
            with Chain() as cn:
                feats_norm(1, *seq[1][0:4])
            with Chain() as cf:
                feats_proj(fsl[0], 0, *seq[0][0:5])
            interleave([cn, cf])
            for i, item in enumerate(seq):
                fcur = fsl[i]
                chains = []
                if i + 2 < len(seq):
                    with Chain() as cn:
                        feats_norm((i + 2) % 2, *seq[i + 2][0:4])
                    chains.append(cn)
                if i + 1 < len(seq):
                    with Chain() as cf:
                        feats_proj(fsl[i + 1], (i + 1) % 2, *seq[i + 1][0:5])
                    chains.append(cf)
                with Chain() as cg:
                    for (d, full) in item[5]:
                        gla_prep(fcur, d, full)
                        for ti in ([0, 1] if d == 0 else [1, 0]):
                            gla_tile(fcur, d, full, ti)
                with Chain() as cs:
                    for (d, full) in item[5]:
                        for ti in ([0, 1] if d == 0 else [1, 0]):
                            ssd_tile(fcur, d, full, ti)
                chains += [cg, cs]
                interleave(chains)
                if item[4] == "full":
                    blk = item[6]
                    for ti in range(2):
                        T = blk * 2 + ti
                        DMA("pool", yf_scr[T * 128:(T + 1) * 128, :], yo[ti][0], R=[yo[ti][1] + ":g", yo[ti][1] + ":s"], W=["yf_scr:%d" % T])
                    spill_fs(fcur, blk)
            S.barrier()
            A2 = Arena(arena_t, win_end)
            A2.off = win_off
            wout, k_wout = A2.alloc([8, D], BF16, "wout")
            t2, k_t2 = A.alloc([512], F32, "t2")
            wmbL = [A2.alloc([8, 256], F32, "wmbL%d" % i) for i in range(2)]
            wmbbL = [A2.alloc([8, 256], BF16, "wmbbL%d" % i) for i in range(2)]
            rowp, k_rowp = A2.alloc([256], F32, "rowp")
            bmodp, k_bmodp = A2.alloc([256], F32, "bmodp")
            nrowp, k_nrowp = A2.alloc([256], F32, "nrowp")
            ncolp, k_ncolp = A2.alloc([16], F32, "ncolp")
            DMA("sp", ncolp, ncol_d, W=[k_ncolp])
            scvP = scbP.rearrange("p (k j) -> p k j", j=2)

            def late_mod(k):
                c0 = 2 * D + 256 * k
                vec, q = k // 4, k % 4
                wt, kw_ = wmbL[k % 2]
                wb, kwb = wmbbL[k % 2]
                DMA("sp", wt, wm_view[:, :, c0:c0 + 256], W=[kw_])
                DMA("sp", bmodp[0:2, :], bmod_d[:, c0:c0 + 256], W=[k_bmodp])
                CP("act", wb[:, 0:4, :], wt[:, 0:4, :], R=[kw_], W=[kwb])
                CP("act", wb[:, 4:8, :], wt[:, 4:8, :], R=[kw_], W=[kwb])
                pt, kp = PB[k % 2]
                with atomic():
                    for kc in range(8):
                        MM(pt[0:2, 0:256], scvP[:, kc, :], wb[:, kc, :], start=(kc == 0), stop=(kc == 7), R=[k_scbP, kwb], W=[kp])
                    TT("dve", rowp[0:2, :], pt[0:2, 0:256], bmodp[0:2, :], ALU.add, R=[kp, k_bmodp], W=[k_rowp])
                if vec in (0, 3):
                    gt, kg = (gate_mix, k_gmix) if vec == 0 else (gate_ffn, k_gffn)
                    n0 = (0 if vec == 0 else D) + q * 256
                    DMA("sp", nrowp[0:1, :], nrow_d[0:1, n0:n0 + 256], W=[k_nrowp])
                    TT("dve", rowp[0:1, :], rowp[0:1, :], nrowp[0:1, :], ALU.mult, R=[k_rowp, k_nrowp], W=[k_rowp])
                    with atomic():
                        MM(pt[:, 0:256], onesf[0:1, :], rowp[0:1, :], R=[k_onesf, k_rowp], W=[kp])
                        CP("act", gt[:, q * 256:(q + 1) * 256], pt[:, 0:256], R=[kp], W=[kg])
                else:
                    base = SH_FFN if vec == 1 else G_FFN
                    with atomic():
                        for j in range(2):
                            MM(pt[:, 2 * j:2 * j + 2], rowp[0:1, j * 128:(j + 1) * 128], onesf[0:1, 0:2], R=[k_rowp, k_onesf], W=[kp])
                        CP("dve", mcols[:, base + 2 * q: base + 2 * q + 2], pt[:, 0:4].rearrange("p (c two) -> p c two", two=2)[:, :, 0],
                           R=[kp], W=[k_mcols])
                    if vec == 2:
                        sl = slice(base + 2 * q, base + 2 * q + 2)
                        TS("dve", mcols[:, sl], mcols[:, sl], 1.0, ALU.add, R=[k_mcols], W=[k_mcols])
                        TT("dve", mcols[:, sl], mcols[:, sl], ncolp[:, 8 + 2 * q: 8 + 2 * q + 2], ALU.mult, R=[k_mcols, k_ncolp], W=[k_mcols])
            wout_view = wout_d.rearrange("(k p) n -> p k n", p=128)
            for kc in range(8):
                DMA("pool", wout[:, kc, :], wout_view[:, kc, :], W=[k_wout], max_dma_last_dim=4096)
            st1 = None
            st2 = None
            for blk in range(7, -1, -1):
                if blk == 7:
                    f = fsets[fs_i[0] % 2]
                else:
                    f = next_fs()
                    reload_fs(f, blk)
                for ti in (1, 0):
                    T = blk * 2 + ti
                    if not (blk == 7):
                        DMA("sp", yo[ti][0], yf_scr[T * 128:(T + 1) * 128, :], R=["yf_scr:%d" % T], W=[yo[ti][1] + ":g", yo[ti][1] + ":s"])
                    with Chain() as cg:
                        if ti == 1:
                            gla_prep(f, 1, True)
                        gla_tile(f, 1, True, ti)
                    with Chain() as cs:
                        ssd_tile(f, 1, True, ti)
                    chains = [cg, cs]
                    if T >= 14:
                        late_ks = [2 * (15 - T), 2 * (15 - T) + 1]
                    elif T >= 2:
                        late_ks = [4 + (13 - T)]
                    else:
                        late_ks = []
                    if late_ks:
                        with Chain() as cl:
                            for k_ in late_ks:
                                late_mod(k_)
                        chains.append(cl)
                    if st2 is not None:
                        with Chain() as c3:
                            merge_out(st2)
                        chains.append(c3)
                    if st1 is not None:
                        with Chain() as c1:
                            merge_gla(*st1)
                        with Chain() as c2:
                            merge_ssd(*st1)
                        chains += [c1, c2]
                    interleave(chains)
                    st2 = st1[2] if st1 is not None else None
                    st1 = (f, ti, T)
            chains = []
            if st2 is not None:
                with Chain() as c3:
                    merge_out(st2)
                chains.append(c3)
            with Chain() as c1:
                merge_gla(*st1)
            with Chain() as c2:
                merge_ssd(*st1)
            interleave(chains + [c1, c2])
            merge_out(st1[2])

            S.barrier()
            A.off = win_off

            wdn, k_wdn = A.alloc([22, D], BF16, "wdn")
            wdn_view = wdown_d.rearrange("(k p) n -> p k n", p=128)
            wdn_pending = list(range(22))
            h2Ts = [A.alloc([8, 1024], BF16, "h2T%d" % i) for i in range(2)]
            actT, k_actT = A.alloc([22, 1024], BF16, "actT")
            wgb = [A.alloc([8, 512], BF16, "wgb%d" % i) for i in range(2)]
            wub = [A.alloc([8, 512], BF16, "wub%d" % i) for i in range(2)]
            xn2 = [A.alloc([D], F32, "xn2_%d" % i) for i in range(2)]
            xn3 = [A.alloc([D], F32, "xn3_%d" % i) for i in range(2)]
            xs2 = [A.alloc([D], BF16, "xs2_%d" % i) for i in range(2)]
            junk2, k_junk2 = A.alloc([D], BF16, "junk2")
            st2, k_st2 = A.alloc([2, 4], F32, "st2")
            sgt = [A.alloc([512], F32, "sgt%d" % i) for i in range(2)]
            tf, k_tf = A.alloc([D], F32, "tf")
            fst, k_fst = A.alloc([2, 4], F32, "fst")
            print("arena phase F bytes:", A.off)
            wg_view = wgate_d.rearrange("(k p) n -> p k n", p=128)
            wu_view = wup_d.rearrange("(k p) n -> p k n", p=128)

            def ffn_prenorm(sb):
                h2T, k_h2T = h2Ts[sb % 2]
                xnb = (xn2 + xn3) if sb == 0 else xn2
                for tl in range(8):
                    T = sb * 8 + tl
                    j = tl % 2
                    xnt, kxn = xnb[tl % len(xnb)]
                    kst_ = k_st2 + ":%d" % j
                    DMA("sp", xnt, xn_scr[T * 128:(T + 1) * 128, :], R=["xn_scr:%d" % T], W=[kxn])
                    ACT(junk2, xnt, AF.Square, accum=st2[:, j, 0:1], R=[kxn], W=[k_junk2, kst_])
                    ACT(st2[:, j, 1:2], st2[:, j, 0:1], AF.Ln, scale=1.0 / D, bias=1e-6, R=[kst_], W=[kst_])
                    ACT(st2[:, j, 3:4], st2[:, j, 1:2], AF.Exp, scale=-0.5, R=[kst_], W=[kst_])
                    xs_, kxs = xs2[j]
                    ACT(xs_, xnt, AF.Copy, scale=st2[:, j, 3:4], R=[kxn, kst_], W=[kxs])
                    ptb, kptb = PT_BF[j]
                    with atomic():
                        for kc in range(8):
                            TR(ptb[:, kc * 128:(kc + 1) * 128], xs_[:, kc * 128:(kc + 1) * 128], R=[kxs, k_ident], W=[kptb])
                        for kc in range(8):
                            TS("dve", h2T[:, kc, tl * 128:(tl + 1) * 128], ptb[:, kc * 128:(kc + 1) * 128],
                               mcols[:, G_FFN + kc: G_FFN + kc + 1], ALU.mult, mcols[:, SH_FFN + kc: SH_FFN + kc + 1], ALU.add,
                               R=[kptb, k_mcols], W=[k_h2T + ":%d" % tl])

            w_issued = set()

            def ffn_wload(sb, fg):
                if (sb, fg) in w_issued:
                    return
                w_issued.add((sb, fg))
                nf = 4 if fg < 5 else 2
                wgt, kwg = wgb[fg % 2]
                wut, kwu = wub[fg % 2]
                c0 = fg * 512
                DMA("pool", wgt[:, :, 0:nf * 128], wg_view[:, :, c0:c0 + nf * 128], W=[kwg])
                DMA("pool", wut[:, :, 0:nf * 128], wu_view[:, :, c0:c0 + nf * 128], W=[kwu])

            def ffn_gateup(sb):
                h2T, k_h2T = h2Ts[sb % 2]
                K_H2 = [k_h2T + ":%d" % tl for tl in range(8)]
                for fg in range(6):
                    nf = 4 if fg < 5 else 2
                    wgt, kwg = wgb[fg % 2]
                    wut, kwu = wub[fg % 2]
                    ffn_wload(sb, fg)
                    if fg >= 1:
                        for _ in range(6):
                            if wdn_pending:
                                fc_ = wdn_pending.pop(0)
                                DMA("pool", wdn[:, fc_, :], wdn_view[:, fc_, :], W=[k_wdn + ":%d" % fc_])
                    for fi in range(nf):
                        fc = fg * 4 + fi
                        for nbk in range(2):
                            pg, kpg = PB[(2 * nbk) % 4]
                            pu, kpu = PB[(2 * nbk + 1) % 4]
                            rk = K_H2[nbk * 4:(nbk + 1) * 4]
                            for kc in range(8):
                                MM(pg[:, :], wgt[:, kc, fi * 128:(fi + 1) * 128], h2T[:, kc, nbk * 512:(nbk + 1) * 512], start=(kc == 0), stop=(kc == 7),
                                   R=[kwg] + rk, W=[kpg])
                            for kc in range(8):
                                MM(pu[:, :], wut[:, kc, fi * 128:(fi + 1) * 128], h2T[:, kc, nbk * 512:(nbk + 1) * 512], start=(kc == 0), stop=(kc == 7),
                                   R=[kwu] + rk, W=[kpu])
                            sg_, ksg_ = sgt[nbk]
                            ACT(sg_, pg[:, :], AF.Silu, R=[kpg], W=[ksg_])
                            TT("dve", actT[:, fc, nbk * 512:(nbk + 1) * 512], pu[:, :], sg_, ALU.mult, R=[kpu, ksg_], W=[k_actT + ":%d" % nbk])

            def ffn_down(sb):
                djunk = sgt[0][0].bitcast(BF16)
                kdj = sgt[0][1]
                for tl in range(8):
                    T = sb * 8 + tl
                    nbk = tl // 4
                    j = tl % 2
                    halves = []
                    for nb in range(2):
                        if j == 0:
                            halves.append((psum2[:, nb * 512:(nb + 1) * 512], "pbig:%d" % nb))
                        else:
                            halves.append((PB[nb][0][:, :], PB[nb][1]))
                    xnt, kxn = xn3[j]
                    DMA("sp", xnt, xn_scr[T * 128:(T + 1) * 128, :], R=["xn_scr:%d" % T], W=[kxn])
                    kf = k_fst + ":%d" % j
                    with atomic():
                        for nb in range(2):
                            pf, kpf = halves[nb]
                            for fc in range(22):
                                MM(pf, actT[:, fc, tl * 128:(tl + 1) * 128], wdn[:, fc, nb * 512:(nb + 1) * 512],
                                   start=(fc == 0), stop=(fc == 21), R=[k_actT + ":%d" % nbk, k_wdn + ":%d" % fc], W=[kpf])
                            ACT(djunk[:, nb * 512:(nb + 1) * 512], pf, AF.Square, accum=fst[:, j, nb:nb + 1], R=[kpf], W=[kdj, kf])
                        TT("dve", fst[:, j, 2:3], fst[:, j, 0:1], fst[:, j, 1:2], ALU.add, R=[kf], W=[kf])
                        ACT(fst[:, j, 2:3], fst[:, j, 2:3], AF.Ln, scale=1.0 / D, bias=1e-6, R=[kf], W=[kf])
                        ACT(fst[:, j, 3:4], fst[:, j, 2:3], AF.Exp, scale=-0.5, R=[kf], W=[kf])
                        for nb in range(2):
                            pf, kpf = halves[nb]
                            STT(tf[:, nb * 512:(nb + 1) * 512], pf, fst[:, j, 3:4], gate_ffn[:, nb * 512:(nb + 1) * 512],
                                ALU.mult, ALU.mult, R=[kpf, kf, k_gffn], W=[k_tf])
                    TT("dve", xnt[:, 0:512], tf[:, 0:512], xnt[:, 0:512], ALU.add, R=[k_tf, kxn], W=[kxn])
                    TT("pool", xnt[:, 512:1024], tf[:, 512:1024], xnt[:, 512:1024], ALU.add, R=[k_tf, kxn], W=[kxn])
                    tok_holder = _sched("dma", "act", (lambda e, T=T, xnt=xnt: e.dma_start(out=y_d[T * 128:(T + 1) * 128, :], in_=xnt)),
                                        [kxn], ["y:%d" % T])
                    if tok_holder is not None:
                        out_toks.append(tok_holder)

            ffn_prenorm(0)
            ffn_gateup(0)
            ffn_wload(1, 0)
            ffn_wload(1, 1)
            with Chain() as cd:
                ffn_down(0)
            with Chain() as cp:
                ffn_prenorm(1)
            interleave([cd, cp])
            ffn_gateup(1)
            ffn_down(1)
        except _Stop:
            pass
        S.barrier()
        S.emit(st)
    return nc


_PROG = {}


def _consts():
    bf = ml_dtypes.bfloat16
    t = np.arange(128)
    TF = (t[:, None] <= t[None, :]).astype(np.float32)
    TB = (t[:, None] >= t[None, :]).astype(np.float32)
    UF = (t[:, None] > t[None, :]).astype(np.float32)
    UB = (t[:, None] < t[None, :]).astype(np.float32)
    tri = np.concatenate([TF, TB], axis=1)
    upp = np.concatenate([UF, UB], axis=1)
    negF = np.tile((1.0 - TF) * NEG, (1, 4))
    negB = np.tile((1.0 - TB) * NEG, (1, 4))
    negm = np.concatenate([negF, negB], axis=1)
    bd = np.zeros((16, 8, 128), np.float32)
    for s in range(2):
        for h in range(8):
            bd[s * 8 + h, h, :] = 1.0
    bd = bd.reshape(16, 1024)
    uall = np.zeros((48, 128), np.float32)
    uall[0:16] = 1.0
    vall = np.zeros((48, 1024), np.float32)
    vall[32:48] = bd
    bdg = np.zeros((128, 256), np.float32)
    bdg[0:64, 0:128] = 1.0
    bdg[64:128, 128:256] = 1.0
    bds = np.zeros((128, 8), np.float32)
    bds[0:64, 0:4] = 1.0
    bds[64:128, 4:8] = 1.0
    gmask = np.zeros((128, 2), np.float32)
    gmask[0:64, 0] = 1.0
    gmask[64:128, 1] = 1.0
    return {
        "ident": np.eye(128, dtype=np.float32).astype(bf),
        "tri_bf": tri.astype(bf), "tri_f": tri, "upp_f": upp,
        "ones_f": np.ones((128, 128), np.float32),
        "negm": negm.astype(bf), "uall_t": uall.astype(bf), "vall_t": vall.astype(bf),
        "bdg": bdg, "bds": bds, "gmask": gmask,
    }


def _col(v):
    return np.ascontiguousarray(v.reshape(8, 128).T)


def make_in_maps(x, c, ctx, c_ctx, w_mod, b_mod, norm_mix_pre, norm_mix_post, norm_ffn_pre, norm_ffn_post,
                 w_in, conv_w, conv_b, gla_wg_f, gla_bg_f, gla_wg_b, gla_bg_b, gla_norm,
                 a_log_f, a_log_b, dt_bias_f, dt_bias_b, d_skip, ssd_norm, w_out, w_gate, w_up, w_down):
    f32 = np.float32
    L = 0
    consts = _consts()
    w_in0 = np.asarray(w_in[L], f32)
    perm = np.arange(DIN)
    perm[1536:1552], perm[1552:1568] = np.arange(1552, 1568), np.arange(1536, 1552)
    perm[2848:2856], perm[2856:2864] = np.arange(2856, 2864), np.arange(2848, 2856)
    w_in1 = np.ascontiguousarray(w_in0[:, perm])
    shared = dict(consts)
    shared.update({
        "w_mod": np.asarray(w_mod[L], f32),
        "b_mod2": np.ascontiguousarray(np.broadcast_to(np.asarray(b_mod[L], f32)[None, :], (2, 6 * D))),
        "ncol": np.concatenate([_col(np.asarray(norm_mix_pre[L], f32)), _col(np.asarray(norm_ffn_pre[L], f32))], axis=1),
        "nrow": np.concatenate([np.asarray(norm_mix_post[L], f32), np.asarray(norm_ffn_post[L], f32)])[None, :],
        "cb": np.ascontiguousarray(np.asarray(conv_b[L], f32).reshape(6, 128).T),
        "mixg": np.concatenate([np.broadcast_to(np.asarray(gla_norm[L], f32)[:, None], (128, 4)),
                                np.asarray(ssd_norm[L], f32).reshape(4, 128).T], axis=1).astype(f32),
        "dsk": np.ascontiguousarray(np.broadcast_to(np.repeat(np.asarray(d_skip[L], f32), 64)[None, :], (128, 512))),
        "w_out": np.asarray(w_out[L], f32), "w_gate": np.asarray(w_gate[L], f32),
        "w_up": np.asarray(w_up[L], f32), "w_down": np.asarray(w_down[L], f32),
    })
    per_th = []
    for th in range(2):
        cw = np.asarray(conv_w[L], f32)
        if th == 1:
            cw = cw[::-1, ::-1, :]
        cwl = np.ascontiguousarray(cw.reshape(9, 6, 128).transpose(2, 1, 0).reshape(128, 54))
        wgf, wgb_ = np.asarray(gla_wg_f[L], f32), np.asarray(gla_wg_b[L], f32)
        bgf, bgb = np.asarray(gla_bg_f[L], f32), np.asarray(gla_bg_b[L], f32)
        alf, alb = np.asarray(a_log_f[L], f32), np.asarray(a_log_b[L], f32)
        dbf, dbb = np.asarray(dt_bias_f[L], f32), np.asarray(dt_bias_b[L], f32)
        if th == 1:
            wgf, wgb_, bgf, bgb = wgb_, wgf, bgb, bgf
            alf, alb, dbf, dbb = alb, alf, dbb, dbf
        per_th.append({
            "w_in": w_in0 if th == 0 else w_in1,
            "cw": cwl,
            "wg": np.ascontiguousarray(np.concatenate([wgf, wgb_], axis=1)),
            "bg": np.concatenate([bgf, bgb])[None, :].astype(f32),
            "alog": np.ascontiguousarray(np.broadcast_to(np.concatenate([alf, alb])[None, :], (128, 16))),
            "dtb": np.ascontiguousarray(np.broadcast_to(np.concatenate([dbf, dbb])[None, :], (128, 16))),
        })
    in_maps = []
    x = np.asarray(x, f32)
    ctx = np.asarray(ctx, f32)
    c = np.asarray(c, f32)
    c_ctx = np.asarray(c_ctx, f32)
    for core in range(8):
        b, th = core // 2, core % 2
        m = dict(shared)
        m.update(per_th[th])
        xb, cxb = x[b], ctx[b]
        if th == 1:
            xb, cxb = xb[::-1], cxb[::-1]
        m["x"] = np.ascontiguousarray(xb)
        m["ctx"] = np.ascontiguousarray(cxb)
        cc = np.stack([_col(c[b]), _col(c_ctx)], axis=2).reshape(128, 16)
        m["cc"] = np.ascontiguousarray(cc)
        in_maps.append(m)
    return in_maps


def kernel(**inputs):
    if "nc" not in _PROG:
        _PROG["nc"] = build_program()
    nc = _PROG["nc"]
    in_maps = make_in_maps(**inputs)
    res = run_bass_kernel_spmd(nc, in_maps, core_ids=list(range(8)))
    out = np.empty((4, SEQ, D), np.float32)
    for core in range(8):
        b, th = core // 2, core % 2
        y = np.asarray(res.results[core]["y"], np.float32)
        if th == 0:
            out[b, 0:OWN] = y
        else:
            out[b, OWN:SEQ] = y[::-1]
    return out
```

```python
import os
from contextlib import ExitStack

import ml_dtypes
import numpy as np

import concourse.bass as bass
import concourse.mybir as mybir
from concourse.bass_utils import run_bass_kernel_spmd

F32 = mybir.dt.float32
BF16 = mybir.dt.bfloat16
AF = mybir.ActivationFunctionType
ALU = mybir.AluOpType
AX = mybir.AxisListType

D = 1024
SEQ = 4096
CTX = 256
OWN = 2048
DIN = 2864
DFF = 2816
NEG = -30000.0


class Sched:
    def __init__(self, nc, n_dma_sems=32):
        self.nc = nc
        self.engs = ["pe", "act", "dve", "pool", "sp"]
        self.lists = {e: [] for e in self.engs}
        self.count = {e: 0 for e in self.engs}
        self.waited = {e: {} for e in self.engs}
        self.last_w = {}
        self.readers = {}
        self.n_dma_sems = n_dma_sems
        self.dma_cnt = [0] * n_dma_sems
        self.dma_rr = 0
        self.sems = {}

    def _need(self, eng, best):
        out = []
        for sk, val in best.items():
            if sk == eng and eng == "pe":
                continue
            if self.waited[eng].get(sk, 0) >= val:
                continue
            self.waited[eng][sk] = val
            out.append((sk, val))
        return out

    def _deps(self, reads, writes):
        best = {}

        def add(tok):
            if best.get(tok[0], 0) < tok[1]:
                best[tok[0]] = tok[1]

        for k in reads:
            if k in self.last_w:
                add(self.last_w[k])
        for k in writes:
            if k in self.last_w:
                add(self.last_w[k])
            for t in self.readers.get(k, ()):
                add(t)
        return best

    def _commit(self, tok, reads, writes):
        for k in reads:
            self.readers.setdefault(k, []).append(tok)
        for k in writes:
            self.last_w[k] = tok
            self.readers[k] = []

    def op(self, eng, fn, reads=(), writes=()):
        best = self._deps(reads, writes)
        waits = self._need(eng, best)
        self.count[eng] += 1
        tok = (eng, self.count[eng])
        self.lists[eng].append((waits, fn, (eng, 1)))
        self._commit(tok, reads, writes)
        return tok

    def dma(self, eng, fn, reads=(), writes=()):
        i = self.dma_rr
        self.dma_rr = (self.dma_rr + 1) % self.n_dma_sems
        sk = "dma%d" % i
        best = self._deps(reads, writes)
        if self.dma_cnt[i] > 0:
            best[sk] = max(best.get(sk, 0), self.dma_cnt[i])
        waits = self._need(eng, best)
        self.dma_cnt[i] += 16
        tok = (sk, self.dma_cnt[i])
        self.lists[eng].append((waits, fn, (sk, 16)))
        self._commit(tok, reads, writes)
        return tok

    def barrier(self):
        best = {}
        for e in self.engs:
            if self.count[e] > 0:
                best[e] = self.count[e]
        for i in range(self.n_dma_sems):
            if self.dma_cnt[i] > 0:
                best["dma%d" % i] = self.dma_cnt[i]
        for e in self.engs:
            b = {k: v for k, v in best.items() if k != e}
            waits = self._need(e, b)
            if waits:
                self.lists[e].append((waits, None, None))
        self.last_w = {}
        self.readers = {}

    def wait_all(self, eng, toks):
        best = {}
        for s, v in toks:
            best[s] = max(best.get(s, 0), v)
        waits = self._need(eng, best)
        self.lists[eng].append((waits, None, None))

    def emit(self, stack):
        nc = self.nc
        for e in self.engs:
            self.sems[e] = stack.enter_context(nc.semaphore("s_" + e))
        for i in range(self.n_dma_sems):
            self.sems["dma%d" % i] = stack.enter_context(nc.semaphore("s_dma%d" % i))
        block = stack.enter_context(nc.Block())
        sems = self.sems

        def run(engh, lst):
            for waits, fn, inc in lst:
                for sk, v in waits:
                    engh.wait_ge(sems[sk], v)
                if fn is not None:
                    ins = fn(engh)
                    ins.then_inc(sems[inc[0]], inc[1])

        @block.tensor
        def _(e):
            run(e, self.lists["pe"])

        @block.scalar
        def _(e):
            run(e, self.lists["act"])

        @block.vector
        def _(e):
            run(e, self.lists["dve"])

        @block.gpsimd
        def _(e):
            run(e, self.lists["pool"])

        @block.sync
        def _(e):
            run(e, self.lists["sp"])


class Arena:
    def __init__(self, tile, nbytes):
        self.t = tile
        self.cap = nbytes
        self.off = 0
        self.n = 0

    def alloc(self, shape, dtype, name=None):
        n = int(np.prod(shape))
        esz = 4 if dtype == F32 else 2
        sz = (n * esz + 63) // 64 * 64
        off = self.off
        assert off + sz <= self.cap, ('arena overflow', name, off, sz, self.cap)
        self.off += sz
        ap = self.t[:, off // 2:(off + n * esz) // 2]
        if dtype == F32:
            ap = ap.bitcast(F32)
        if len(shape) == 2:
            ap = ap.rearrange("p (a b) -> p a b", a=shape[0])
        elif len(shape) == 3:
            ap = ap.rearrange("p (a b c) -> p a b c", a=shape[0], b=shape[1])
        self.n += 1
        return ap, (name or "t") + "#%d" % self.n


class _Stop(Exception):
    pass


def build_program(debug=None, stop=None):
    nc = bass.Bass("TRN2", target_bir_lowering=False)
    di = {}

    def dram_in(name, shape, dt=F32):
        di[name] = nc.dram_tensor(name, list(shape), dt, kind="ExternalInput").ap()
        return di[name]

    x_d = dram_in("x", [SEQ, D])
    ctx_d = dram_in("ctx", [CTX, D])
    cc_d = dram_in("cc", [128, 16])
    wmod_d = dram_in("w_mod", [D, 6 * D])
    bmod_d = dram_in("b_mod2", [2, 6 * D])
    ncol_d = dram_in("ncol", [128, 16])
    nrow_d = dram_in("nrow", [1, 2 * D])
    win_d = dram_in("w_in", [D, DIN])
    cw_d = dram_in("cw", [128, 54])
    cb_d = dram_in("cb", [128, 6])
    wg_d = dram_in("wg", [16, 512])
    bg_d = dram_in("bg", [1, 512])
    mixg_d = dram_in("mixg", [128, 8])
    alog_d = dram_in("alog", [128, 16])
    dtb_d = dram_in("dtb", [128, 16])
    dsk_d = dram_in("dsk", [128, 512])
    wout_d = dram_in("w_out", [D, D])
    wgate_d = dram_in("w_gate", [D, DFF])
    wup_d = dram_in("w_up", [D, DFF])
    wdown_d = dram_in("w_down", [DFF, D])
    ident_d = dram_in("ident", [128, 128], BF16)
    tribf_d = dram_in("tri_bf", [128, 256], BF16)
    trif_d = dram_in("tri_f", [128, 256])
    uppf_d = dram_in("upp_f", [128, 256])
    onesf_d = dram_in("ones_f", [128, 128])
    negm_d = dram_in("negm", [128, 1024], BF16)
    uallt_d = dram_in("uall_t", [48, 128], BF16)
    vallt_d = dram_in("vall_t", [48, 1024], BF16)
    bdg_d = dram_in("bdg", [128, 256])
    bds_d = dram_in("bds", [128, 8])
    gmask_d = dram_in("gmask", [128, 2])
    y_d = nc.dram_tensor("y", [OWN, D], F32, kind="ExternalOutput").ap()
    yf_scr = nc.dram_tensor("yf_scr", [OWN, D], F32, kind="Internal").ap()
    xn_scr = nc.dram_tensor("xn_scr", [OWN, D], F32, kind="Internal").ap()
    dbg_out = {}
    if debug:
        for nm, shp in debug.items():
            dbg_out[nm] = nc.dram_tensor("dbg_" + nm, list(shp), F32, kind="ExternalOutput").ap()

    st = ExitStack()
    dbg_toks = []
    with st:
        ARENA_BYTES = 207 * 1024
        arena_t = st.enter_context(nc.sbuf_tensor("arena", [128, ARENA_BYTES // 2], BF16))
        psum = [st.enter_context(nc.psum_tensor("pb%d" % i, [128, 512], F32)) for i in range(6)]
        psum2 = st.enter_context(nc.psum_tensor("pbig", [128, 1024], F32))
        S = Sched(nc)
        A = Arena(arena_t, ARENA_BYTES)

        PB = [(psum[i], "pb%d" % i) for i in range(6)]
        PB2 = (psum2, "pbig")

        REC = [None]
        ATOM = [0]

        class Chain:
            def __init__(self):
                self.steps = []

            def __enter__(self):
                self.prev = REC[0]
                REC[0] = self.steps
                return self

            def __exit__(self, *a):
                REC[0] = self.prev

        class atomic:
            def __enter__(self):
                if REC[0] is not None and ATOM[0] == 0:
                    REC[0].append([])
                ATOM[0] += 1

            def __exit__(self, *a):
                ATOM[0] -= 1

        def _sched(kind, eng, fn, R, W):
            def thunk():
                if kind == "op":
                    return S.op(eng, fn, reads=R, writes=W)
                return S.dma(eng, fn, reads=R, writes=W)
            if REC[0] is None:
                return thunk()
            if ATOM[0] > 0:
                REC[0][-1].append(thunk)
            else:
                REC[0].append([thunk])
            return None

        def interleave(chains):
            chains = [c for c in chains if c is not None and len(c.steps) > 0]
            pos = [0] * len(chains)
            while True:
                best, bi = None, -1
                for i, c in enumerate(chains):
                    if pos[i] < len(c.steps):
                        fr = (pos[i] + 0.5) / len(c.steps)
                        if best is None or fr < best:
                            best, bi = fr, i
                if bi < 0:
                    break
                for th in chains[bi].steps[pos[bi]]:
                    th()
                pos[bi] += 1

        def MM(out, lhsT, rhs, start=True, stop=True, R=(), W=()):
            return _sched("op", "pe", lambda e: e.matmul(out, lhsT, rhs, start=start, stop=stop), R, W)

        def TR(out, in_, R=(), W=()):
            return _sched("op", "pe", lambda e: e.transpose(out, in_, ident[0:in_.shape[0], 0:in_.shape[0]]), R, W)

        def ACT(out, in_, func, bias=None, scale=None, accum=None, R=(), W=()):
            kw = {}
            if bias is not None:
                kw["bias"] = bias
            if scale is not None:
                kw["scale"] = scale
            if accum is not None:
                kw["accum_out"] = accum
            return _sched("op", "act", lambda e: e.activation(out=out, in_=in_, func=func, **kw), R, W)

        def TT(eng, out, in0, in1, op, R=(), W=()):
            return _sched("op", eng, lambda e: e.tensor_tensor(out=out, in0=in0, in1=in1, op=op), R, W)

        def TS(eng, out, in0, s1, op0, s2=None, op1=None, R=(), W=()):
            if op1 is None:
                return _sched("op", eng, lambda e: e.tensor_scalar(out=out, in0=in0, scalar1=s1, scalar2=None, op0=op0), R, W)
            return _sched("op", eng, lambda e: e.tensor_scalar(out=out, in0=in0, scalar1=s1, scalar2=s2, op0=op0, op1=op1), R, W)

        def STT(out, in0, scalar, in1, op0, op1, R=(), W=()):
            return _sched("op", "dve", lambda e: e.scalar_tensor_tensor(out=out, in0=in0, scalar=scalar, in1=in1, op0=op0, op1=op1), R, W)

        def CP(eng, out, in_, R=(), W=()):
            if eng == "act":
                return ACT(out, in_, AF.Copy, R=R, W=W)
            return _sched("op", eng, lambda e: e.tensor_copy(out=out, in_=in_), R, W)

        def RECIP(out, in_, R=(), W=()):
            return _sched("op", "dve", lambda e: e.reciprocal(out=out, in_=in_), R, W)

        def MSET(eng, ap, val, W=()):
            return _sched("op", eng, lambda e: e.memset(ap, val), (), W)

        def DMA(q, out, in_, R=(), W=(), **kw):
            return _sched("dma", q, lambda e: e.dma_start(out=out, in_=in_, **kw), R, W)

        def bc(ap, shape):
            return ap.broadcast_to(shape)

        def dump(name, ap, key):
            dt_ = ap.dtype
            shp = [int(v) for v in ap.shape]
            o = nc.dram_tensor("dbg_" + name, shp, dt_, kind="ExternalOutput").ap()
            dbg_toks.append(DMA("sp", o, ap, R=[key], W=["dbg_" + name]))

        out_toks = []
        try:
            ident, k_ident = A.alloc([128], BF16, "ident")
            tribf, k_tribf = A.alloc([2, 128], BF16, "tribf")
            trif, k_trif = A.alloc([2, 128], F32, "trif")
            uppf, k_uppf = A.alloc([2, 128], F32, "uppf")
            onesf, k_onesf = A.alloc([128], F32, "onesf")
            onesb, k_onesb = A.alloc([128], BF16, "onesb")
            negm, k_negm = A.alloc([2, 512], BF16, "negm")
            bdg, k_bdg = A.alloc([256], F32, "bdg")
            bds, k_bds = A.alloc([8], F32, "bds")
            gmask, k_gmask = A.alloc([2], F32, "gmask")
            mixg, k_mixg = A.alloc([8], F32, "mixg")
            dskb, k_dskb = A.alloc([512], F32, "dskb")
            cb, k_cb = A.alloc([6], F32, "cb")
            gate_mix, k_gmix = A.alloc([1024], F32, "gate_mix")
            gate_ffn, k_gffn = A.alloc([1024], F32, "gate_ffn")
            mcols, k_mcols = A.alloc([64], F32, "mcols")
            CONST_K = [k_ident, k_tribf, k_trif, k_uppf, k_onesf, k_onesb, k_negm, k_bdg, k_bds, k_gmask, k_mixg, k_dskb, k_cb]
            def load_consts():
                for (ap, k, src) in [(ident, k_ident, ident_d), (onesf, k_onesf, onesf_d), (bdg, k_bdg, bdg_d),
                                     (bds, k_bds, bds_d), (gmask, k_gmask, gmask_d), (mixg, k_mixg, mixg_d),
                                     (dskb, k_dskb, dsk_d), (cb, k_cb, cb_d)]:
                    DMA("act", ap, src, W=[k])
                DMA("act", tribf, tribf_d.rearrange("p (a b) -> p a b", a=2), W=[k_tribf])
                DMA("act", trif, trif_d.rearrange("p (a b) -> p a b", a=2), W=[k_trif])
                DMA("act", uppf, uppf_d.rearrange("p (a b) -> p a b", a=2), W=[k_uppf])
                DMA("act", negm, negm_d.rearrange("p (a b) -> p a b", a=2), W=[k_negm])
                CP("dve", onesb, onesf, R=[k_onesf], W=[k_onesb])
            win_off = A.off
            win, k_win = A.alloc([8, DIN], BF16, "win")
            win_end = A.off
            win_view = win_d.rearrange("(k p) n -> p k n", p=128)
            K_WIN = [k_win + ":%d" % kc for kc in range(8)]
            scbP, k_scbP = A.alloc([16], BF16, "scbP")
            cwt, k_cwt = A.alloc([54], F32, "cwt")
            cwd, k_cwd = A.alloc([6, 9, 128], BF16, "cwd")
            identf, k_identf = A.alloc([128], F32, "identf")
            wgs, k_wgs = A.alloc([512], BF16, "wgs")
            bgs, k_bgs = A.alloc([512], BF16, "bgs")
            aneg, k_aneg = A.alloc([16], F32, "aneg")
            dtb, k_dtb = A.alloc([16], F32, "dtb")

            def setup_small():
                DMA("act", cwt, cw_d, W=[k_cwt])
                DMA("pool", wgs[0:16, :], wg_d, W=[k_wgs])
                DMA("pool", bgs[0:1, :], bg_d, W=[k_bgs])
                DMA("act", aneg, alog_d, W=[k_aneg])
                DMA("act", dtb, dtb_d, W=[k_dtb])
                ACT(aneg, aneg, AF.Exp, R=[k_aneg], W=[k_aneg])
                TS("dve", aneg, aneg, -1.0, ALU.mult, R=[k_aneg], W=[k_aneg])
                CP("dve", identf, ident, R=[k_ident], W=[k_identf])
                for c6 in range(6):
                    for tap in range(9):
                        TS("dve", cwd[:, c6, tap, :], identf, cwt[:, c6 * 9 + tap: c6 * 9 + tap + 1], ALU.mult,
                           R=[k_identf, k_cwt], W=[k_cwd])
            persist_mark = A.off

            ccs, k_ccs = A.alloc([16], F32, "ccs")
            sc, k_sc = A.alloc([16], F32, "sc")
            mrow, k_mrow = A.alloc([6 * D], F32, "mrow")
            bmod, k_bmod = A.alloc([2 * D], F32, "bmod")
            mctx, k_mctx = A.alloc([2 * D], F32, "mctx")
            ncol, k_ncol = A.alloc([16], F32, "ncol")
            wmb = [A.alloc([8, 512], F32, "wmb%d" % i) for i in range(3)]
            wmbb = [A.alloc([8, 512], BF16, "wmbb%d" % i) for i in range(2)]
            scb, k_scb = A.alloc([16], BF16, "scb")
            wm_view = wmod_d.rearrange("(k p) n -> p k n", p=128)
            DMA("sp", ccs, cc_d, W=[k_ccs])
            for nb in range(3):
                DMA("sp", wmb[nb][0], wm_view[:, :, nb * 512:(nb + 1) * 512], W=[wmb[nb][1]])
            ACT(sc, ccs, AF.Silu, R=[k_ccs], W=[k_sc])
            CP("dve", scb, sc, R=[k_sc], W=[k_scb])
            scv = scb.rearrange("p (k j) -> p k j", j=2)
            CP("dve", scbP, sc, R=[k_sc], W=[k_scbP])
            DMA("act", bmod[0:2, :], bmod_d[:, 0:2 * D], W=[k_bmod])
            DMA("act", ncol, ncol_d, W=[k_ncol])
            load_consts()
            for kc in range(8):
                DMA("pool", win[:, kc, :], win_view[:, kc, :], W=[K_WIN[kc]], max_dma_last_dim=4096)
            if stop == "mod0":
                dump("sc", sc, k_sc)
                dump("tribf", tribf, k_tribf)
                dump("onesb", onesb, k_onesb)
                raise _Stop()
            for nb in range(4):
                wt, kw_ = wmb[nb % 3]
                wb, kwb = wmbb[nb % 2]
                CP("dve", wb[:, 0:4, :], wt[:, 0:4, :], R=[kw_], W=[kwb])
                CP("act", wb[:, 4:8, :], wt[:, 4:8, :], R=[kw_], W=[kwb])
                if nb + 3 < 4:
                    DMA("sp", wt, wm_view[:, :, (nb + 3) * 512:(nb + 4) * 512], W=[kw_])
                pt, kp = PB[nb % 2]
                for kc in range(8):
                    MM(pt[0:2, :], scv[:, kc, :], wb[:, kc, :], start=(kc == 0), stop=(kc == 7), R=[k_scb, kwb], W=[kp])
                TT("dve", mrow[0:2, nb * 512:(nb + 1) * 512], pt[0:2, :], bmod[0:2, nb * 512:(nb + 1) * 512], ALU.add,
                   R=[kp, k_bmod], W=[k_mrow])
            if stop == "mod1":
                dump("mrow", mrow[0:2, :], k_mrow)
                raise _Stop()
            setup_small()
            DMA("sp", mctx[0:1, :], mrow[1:2, 0:2 * D], R=[k_mrow], W=[k_mctx])
            if stop == "mod2":
                dump("mctx", mctx[0:1, :], k_mctx)
                raise _Stop()
            pcol, kpc = PB[2]
            col_srcs = [(0, mrow, 0), (1, mrow, D), (4, mctx, 0), (5, mctx, D)]
            for (vi, src, off) in col_srcs:
                for kc in range(8):
                    ci = vi * 8 + kc
                    MM(pcol[:, 2 * ci: 2 * ci + 2], src[0:1, off + kc * 128: off + (kc + 1) * 128], onesf[0:1, 0:2],
                       R=[k_mrow, k_mctx, k_onesf], W=[kpc])
            CP("dve", mcols[:, 0:16], pcol[:, 0:32].rearrange("p (c two) -> p c two", two=2)[:, :, 0], R=[kpc], W=[k_mcols])
            CP("dve", mcols[:, 32:48], pcol[:, 64:96].rearrange("p (c two) -> p c two", two=2)[:, :, 0], R=[kpc], W=[k_mcols])
            if stop == "mod3":
                dump("mcols", mcols[:, 0:48], k_mcols)
                raise _Stop()
            for (sl, g0) in [(slice(8, 16), 0), (slice(40, 48), 0)]:
                TS("dve", mcols[:, sl], mcols[:, sl], 1.0, ALU.add, R=[k_mcols], W=[k_mcols])
                TT("dve", mcols[:, sl], mcols[:, sl], ncol[:, g0:g0 + 8], ALU.mult, R=[k_mcols, k_ncol], W=[k_mcols])
            SH_LAT, G_LAT, SH_FFN, G_FFN, SH_CTX, G_CTX = 0, 8, 16, 24, 32, 40
            if stop == "mod":
                dump("mcols", mcols[:, 0:48], k_mcols)
                dump("gate_mix", gate_mix, k_gmix)
                dump("gate_ffn", gate_ffn, k_gffn)
                raise _Stop()
            if debug and "mcols" in dbg_out:
                DMA("sp", dbg_out["mcols"], mcols[:, 0:48], R=[k_mcols], W=["dbg_mcols"])
            if debug and "gate_mix" in dbg_out:
                DMA("sp", dbg_out["gate_mix"], gate_mix, R=[k_gmix], W=["dbg_gm"])
            S.barrier()
            A.off = persist_mark

            if stop == "A0":
                dump("win0", win[:, 0, 0:512], K_WIN[0])
                dump("win7", win[:, 7, 2352:2864], K_WIN[7])
                dump("cwd", cwd[:, 2, :, :], k_cwd)
                dump("aneg", aneg, k_aneg)
                dump("wgs", wgs[0:16, :], k_wgs)
                raise _Stop()
            class FS:
                pass

            def new_fs(tag):
                f = FS()
                f.tag = tag
                f.qT, f.k_qT = A.alloc([2, 256], BF16, "qT" + tag)
                f.kT, f.k_kT = A.alloc([2, 256], BF16, "kT" + tag)
                f.ktm, f.k_ktm = A.alloc([2, 256], BF16, "ktm" + tag)
                f.v, f.k_v = A.alloc([2, 512], BF16, "v" + tag)
                f.g, f.k_g = A.alloc([2, 2, 256], BF16, "g" + tag)
                f.x, f.k_x = A.alloc([2, 512], BF16, "x" + tag)
                f.Btm, f.k_Btm = A.alloc([2, 128], BF16, "Btm" + tag)
                f.BTm, f.k_BTm = A.alloc([2, 256], BF16, "BTm" + tag)
                f.CT, f.k_CT = A.alloc([256], BF16, "CT" + tag)
                f.dt, f.k_dt = A.alloc([2, 16], F32, "dt" + tag)
                f.a, f.k_a = A.alloc([2, 16], F32, "a" + tag)
                f.r, f.k_r = A.alloc([2, 512], BF16, "r" + tag)
                f.z, f.k_z = A.alloc([2, 512], BF16, "z" + tag)
                return f

            fsets = [new_fs("A"), new_fs("B")]

            xt = [A.alloc([D], F32, "xt%d" % i) for i in range(3)]
            xsb = [A.alloc([D], BF16, "xs%d" % i) for i in range(3)]
            stat, k_stat = A.alloc([3, 4], F32, "stat")
            hTs = [A.alloc([8, 384], BF16, "hT%d" % i) for i in range(2)]
            hT, k_hT = hTs[0]
            hT_rr = [0]
            pcl, k_pcl = A.alloc([6, 6, 66], BF16, "pcl")
            xa, k_xa = A.alloc([6, 256], BF16, "xa")
            lrT, k_lrT = A.alloc([2, 256], BF16, "lrT")
            gwe, k_gwe = A.alloc([512], F32, "gwe")
            gwl, k_gwl = gwe, k_gwe
            sile, k_sile = A.alloc([512], F32, "sile")
            cbn, k_cbn = A.alloc([6], F32, "cbn")
            TS("dve", cbn, cb, -1.0, ALU.mult, R=[k_cb], W=[k_cbn])
            dtw, k_dtw = A.alloc([16], F32, "dtw")
            MSET("pool", pcl, 0.0, W=[k_pcl])

            PT_BF = [(psum[4][:, :].bitcast(BF16), "pb4"), (psum[5][:, :].bitcast(BF16), "pb5")]

            evac_rr = [0]

            def evac_eng():
                evac_rr[0] += 1
                return "dve" if evac_rr[0] % 3 == 0 else "act"

            def feats_norm(hsel, src, tok0, is_ctx, nseq):
                hT, k_hT = hTs[hsel]
                ntok_ext = 256 if is_ctx else 384
                ntile = 2 if is_ctx else 3
                for ti in range(ntile):
                    xtt, kx = xt[ti]
                    if ti < 2:
                        DMA("sp", xtt, src[tok0 + ti * 128: tok0 + (ti + 1) * 128, :], W=[kx])
                    else:
                        lo = tok0 - 64 if tok0 >= 64 else 0
                        hi = tok0 + 256 if tok0 + 320 <= nseq else 0
                        DMA("sp", xtt[0:64, :], src[lo:lo + 64, :], W=[kx])
                        DMA("sp", xtt[64:128, :], src[hi:hi + 64, :], W=[kx])
                    ss = stat[:, ti, 0:1]
                    kst_ = k_stat + ":%d" % ti
                    ACT(xsb[ti][0], xtt, AF.Square, accum=ss, R=[kx], W=[xsb[ti][1], kst_])
                    ACT(stat[:, ti, 1:2], ss, AF.Ln, scale=1.0 / D, bias=1e-6, R=[kst_], W=[kst_])
                    ACT(stat[:, ti, 3:4], stat[:, ti, 1:2], AF.Exp, scale=-0.5, R=[kst_], W=[kst_])
                    xs_, kxs = xsb[ti]
                    TS("dve", xs_, xtt, stat[:, ti, 3:4], ALU.mult, R=[kx, kst_], W=[kxs])
                gofs = G_CTX if is_ctx else G_LAT
                sofs = SH_CTX if is_ctx else SH_LAT
                pt, kpt = PT_BF[0]
                for kp in range(4):
                    with atomic():
                        for j in range(2):
                            kc = kp * 2 + j
                            for ti in range(ntile):
                                TR(pt[:, j * 384 + ti * 128: j * 384 + (ti + 1) * 128], xsb[ti][0][:, kc * 128:(kc + 1) * 128],
                                   R=[xsb[ti][1], k_ident], W=[kpt])
                        for j in range(2):
                            kc = kp * 2 + j
                            TS("dve", hT[:, kc, 0:ntok_ext], pt[:, j * 384: j * 384 + ntok_ext],
                               mcols[:, gofs + kc: gofs + kc + 1], ALU.mult, mcols[:, sofs + kc: sofs + kc + 1], ALU.add,
                               R=[kpt, k_mcols], W=[k_hT + ":%d" % kc])

            prr = [0]

            def nextp():
                prr[0] += 1
                return PB[prr[0] % 2]

            def feats_proj(fs, hsel, src, tok0, is_ctx, nseq, mode="full"):
                hT, k_hT = hTs[hsel]
                ntok_ext = 256 if is_ctx else 384
                K_HT = [k_hT + ":%d" % kc for kc in range(8)]
                full = (mode == "full")
                dirs = [1] if mode == "sB" else [0, 1]
                nc6 = 6 if full else 5
                fm = []
                if full:
                    fm += [("q", 0, 128, 0), ("q", 128, 128, 1), ("k", 256, 128, 0), ("k", 384, 128, 1)]
                fm += [("lr", 1536 + 16 * d_, 16, d_) for d_ in dirs]
                for (nm, c0, M, idx) in fm:
                    pt, kpt = nextp()
                    with atomic():
                        for kc in range(8):
                            MM(pt[0:M, 0:256], win[:, kc, c0:c0 + M], hT[:, kc, 0:256], start=(kc == 0), stop=(kc == 7),
                               R=[K_WIN[kc], K_HT[kc]], W=[kpt])
                        if nm == "q":
                            ACT(fs.qT[:, idx, :], pt[:, 0:256], AF.Copy, scale=0.125, R=[kpt], W=[fs.k_qT])
                        elif nm == "k":
                            CP("act", fs.kT[:, idx, :], pt[:, 0:256], R=[kpt], W=[fs.k_kT])
                        else:
                            CP("act", lrT[0:16, idx, :], pt[0:16, 0:256], R=[kpt], W=[k_lrT])
                pc = pcx if is_ctx else pcl
                kpc_ = k_pcx if is_ctx else k_pcl
                for c6 in range(nc6):
                    pt, kpt = nextp()
                    with atomic():
                        for kc in range(8):
                            MM(pt[:, 0:ntok_ext], win[:, kc, 2080 + c6 * 128: 2080 + (c6 + 1) * 128], hT[:, kc, 0:ntok_ext],
                               start=(kc == 0), stop=(kc == 7), R=[K_WIN[kc], K_HT[kc]], W=[kpt])
                        eng = evac_eng()
                        if is_ctx:
                            CP(eng, pc[:, c6, 1, 1:257], pt[:, 0:256], R=[kpt], W=[kpc_])
                        else:
                            CP(eng, pc[:, c6, 1:5, 1:65], pt[:, 0:256].rearrange("p (r c) -> p r c", r=4), R=[kpt], W=[kpc_])
                            eng = evac_eng()
                            if tok0 >= 64 and tok0 + 320 <= nseq:
                                CP(eng, pc[:, c6, 0:6:5, 1:65], pt[:, 256:384].rearrange("p (r c) -> p r c", r=2), R=[kpt], W=[kpc_])
                            else:
                                if tok0 >= 64:
                                    CP(eng, pc[:, c6, 0, 1:65], pt[:, 256:320], R=[kpt], W=[kpc_])
                                else:
                                    MSET("pool", pc[:, c6, 0, 1:65], 0.0, W=[kpc_])
                                if tok0 + 320 <= nseq:
                                    CP(eng, pc[:, c6, 5, 1:65], pt[:, 320:384], R=[kpt], W=[kpc_])
                                else:
                                    MSET("pool", pc[:, c6, 5, 1:65], 0.0, W=[kpc_])
                for tpass in range(4):
                    ti = tpass % 2
                    if tpass < 2:
                        tml = [("v", 512, 512), ("k", 256, 256), ("dt", 2848, 16)]
                    else:
                        tml = [("r", 1024, 512), ("z", 1568, 512)] if full else []
                    for (nm, c0, N) in tml:
                        pt, kpt = nextp()
                        with atomic():
                            for kc in range(8):
                                MM(pt[:, 0:N], hT[:, kc, ti * 128:(ti + 1) * 128], win[:, kc, c0:c0 + N], start=(kc == 0), stop=(kc == 7),
                                   R=[K_WIN[kc], K_HT[kc]], W=[kpt])
                            if nm == "v":
                                CP("act", fs.v[:, ti, :], pt[:, :], R=[kpt], W=[fs.k_v])
                            elif nm == "r":
                                ACT(fs.r[:, ti, :], pt[:, :], AF.Silu, R=[kpt], W=[fs.k_r])
                            elif nm == "z":
                                ACT(fs.z[:, ti, :], pt[:, :], AF.Silu, R=[kpt], W=[fs.k_z])
                            elif nm == "k":
                                CP("act", fs.ktm[:, ti, :], pt[:, 0:256], R=[kpt], W=[fs.k_ktm])
                            else:
                                TT("dve", dtw, pt[:, 0:16], dtb, ALU.add, R=[kpt, k_dtb], W=[k_dtw])
                        if nm == "dt":
                            ACT(dtw, dtw, AF.Exp, R=[k_dtw], W=[k_dtw])
                            ACT(fs.dt[:, ti, :], dtw, AF.Ln, bias=1.0, R=[k_dtw], W=[fs.k_dt])
                            TT("dve", fs.a[:, ti, :], fs.dt[:, ti, :], aneg, ALU.mult, R=[fs.k_dt, k_aneg], W=[fs.k_a])
                    if tpass >= 2:
                        continue
                    pt, kpt = nextp()
                    gsl = slice(dirs[0] * 256, 512)
                    with atomic():
                        for d in dirs:
                            MM(pt[:, d * 256:(d + 1) * 256], lrT[0:16, d, ti * 128:(ti + 1) * 128], wgs[0:16, d * 256:(d + 1) * 256],
                               start=True, stop=False, R=[k_lrT, k_wgs], W=[kpt])
                            MM(pt[:, d * 256:(d + 1) * 256], onesb[0:1, :], bgs[0:1, d * 256:(d + 1) * 256],
                               start=False, stop=True, R=[k_onesb, k_bgs], W=[kpt])
                        ACT(gwe[:, gsl], pt[:, gsl], AF.Exp, scale=-1.0, R=[kpt], W=[k_gwe])
                    ACT(gwl[:, gsl], gwe[:, gsl], AF.Ln, bias=1.0, R=[k_gwe], W=[k_gwl])
                    if os.environ.get("X_G"):
                        TS("pool", fs.g[:, dirs[0]:2, ti, :], gwl[:, gsl].rearrange("p (d f) -> p d f", f=256), -1.0 / 16.0, ALU.mult,
                           R=[k_gwl], W=[fs.k_g])
                    else:
                        ACT(fs.g[:, dirs[0]:2, ti, :], gwl[:, gsl].rearrange("p (d f) -> p d f", f=256), AF.Copy, scale=-1.0 / 16.0,
                            R=[k_gwl], W=[fs.k_g])
                taps = [(0, dc) for dc in (-1, 0, 1)] if is_ctx else [(dr, dc) for dr in (-1, 0, 1) for dc in (-1, 0, 1)]
                for c6 in range(nc6):
                    pt, kpt = nextp()
                    with atomic():
                        for i, (dr, dc) in enumerate(taps):
                            tap = (dr + 1) * 3 + (dc + 1)
                            if is_ctx:
                                rhs = pc[:, c6, 1, 1 + dc: 257 + dc]
                            else:
                                rhs = pc[:, c6, 1 + dr: 5 + dr, 1 + dc: 65 + dc]
                            MM(pt[:, 0:256], cwd[:, c6, tap, :], rhs, start=(i == 0), stop=(i == len(taps) - 1), R=[k_cwd, kpc_], W=[kpt])
                        if c6 == 5:
                            cdst, kcd = fs.CT, fs.k_CT
                        else:
                            cdst, kcd = xa[:, c6, :], k_xa + ":%d" % c6
                        ACT(cdst, pt[:, 0:256], AF.Silu, bias=cb[:, c6:c6 + 1], R=[kpt, k_cb], W=[kcd])
                if full:
                    for g in range(2):
                        TS("dve", fs.BTm[:, g, :], xa[:, 4, :], gmask[:, g:g + 1], ALU.mult, R=[k_xa + ":4", k_gmask], W=[fs.k_BTm])
                pt, kpt = PT_BF[0]
                for ti in range(2):
                    with atomic():
                        for c6 in range(5):
                            TR(pt[:, c6 * 128:(c6 + 1) * 128], xa[:, c6, ti * 128:(ti + 1) * 128], R=[k_xa + ":%d" % c6, k_ident], W=[kpt])
                        CP("dve", fs.x[:, ti, :], pt[:, 0:512], R=[kpt], W=[fs.k_x])
                        CP("dve", fs.Btm[:, ti, :], pt[:, 512:640], R=[kpt], W=[fs.k_Btm])

            Sg = [A.alloc([2, 256], F32, "Sg%d" % d) for d in range(2)]
            Sgb = [A.alloc([2, 256], BF16, "Sgb%d" % d) for d in range(2)]
            Ss = [A.alloc([512], F32, "Ss%d" % d) for d in range(2)]
            Ssb = [A.alloc([512], BF16, "Ssb%d" % d) for d in range(2)]
            for d in range(2):
                MSET("pool", Sg[d][0], 0.0, W=[Sg[d][1]])
                MSET("pool", Sgb[d][0], 0.0, W=[Sgb[d][1]])
                MSET("pool", Ss[d][0], 0.0, W=[Ss[d][1]])
                MSET("pool", Ssb[d][0], 0.0, W=[Ssb[d][1]])
            eq, k_eq = A.alloc([2, 256], F32, "eq")
            ek, k_ek = A.alloc([2, 256], F32, "ek")
            qs, k_qs = A.alloc([2, 256], BF16, "qs")
            ks = [A.alloc([2, 256], BF16, "ks%d" % h) for h in range(2)]
            ekt, k_ekt = A.alloc([2, 256], F32, "ekt")
            kst, k_kst = A.alloc([2, 256], BF16, "kst")
            attm, k_attm = A.alloc([4, 128], BF16, "attm")
            tmpg, k_tmpg = A.alloc([2, 256], F32, "tmpg")
            _mark = A.off
            pcx, k_pcx = A.alloc([6, 3, 258], BF16, "pcx")
            MSET("dve", pcx, 0.0, W=[k_pcx])
            A.off = _mark
            yo = [A.alloc([1024], F32, "yo%d" % i) for i in range(2)]
            sq, k_sq = A.alloc([512], F32, "sq")
            t1, k_t1 = A.alloc([512], F32, "t1")
            yfr = yo
            e3, k_e3 = A.alloc([3, 8], F32, "e3")
            wj, k_wj = A.alloc([8], F32, "wj")
            p1s, k_p1s = A.alloc([8], F32, "p1s")
            cspb, k_cspb = A.alloc([48], BF16, "cspb")
            uall, k_uall = A.alloc([128], BF16, "uall")
            vall, k_vall = A.alloc([1024], BF16, "vall")
            LT, k_LT = A.alloc([1024], BF16, "LT")
            scT, k_scT = A.alloc([8, 128], BF16, "scT")
            xdt, k_xdt = A.alloc([8, 64], BF16, "xdt")
            xw, k_xw = A.alloc([8, 64], BF16, "xw")
            tmpi, k_tmpi = A.alloc([512], F32, "tmpi")
            tmps, k_tmps = A.alloc([512], F32, "tmps")
            MSET("pool", cspb, 0.0, W=[k_cspb])
            DMA("sp", uall[0:48, :], uallt_d, W=[k_uall])
            DMA("sp", vall[0:48, :], vallt_d, W=[k_vall])
            bd16, k_bd16 = A.alloc([1024], BF16, "bd16")
            DMA("sp", bd16[0:16, :], vallt_d[32:48, :], W=[k_bd16])

            mst, k_mst = A.alloc([16], F32, "mst")
            ob, k_ob = xsb[2]
            oT, k_oT = xsb[1][0].rearrange("p (k t) -> p k t", k=8), xsb[1][1]
            xo, k_xo = xt[2]
            tmo, k_tmo = xt[1]
            pst, k_pst = A.alloc([4], F32, "pst")
            print("arena phase A bytes:", A.off)

            PG_A, PG_B = PB[2], PB[3]
            P5 = psum[5]
            P5BF = psum[5][:, :].bitcast(BF16)
            PBIG = psum2

            def gla_prep(fs, d, full):
                pt, kpt = PG_A
                with atomic():
                    for ti in range(2):
                        for p in range(2):
                            MM(pt[:, p * 256 + ti * 128: p * 256 + (ti + 1) * 128], fs.g[:, d, ti, p * 128:(p + 1) * 128], tribf[:, d, :],
                               R=[fs.k_g, k_tribf], W=[kpt])
                    ptv = pt[:, :].rearrange("p (a b) -> p a b", a=2)
                    ACT(eq, ptv, AF.Exp, R=[kpt], W=[k_eq])
                    if full:
                        ACT(ek, ptv, AF.Exp, scale=-1.0, R=[kpt], W=[k_ek])
                if full:
                    TT("dve", qs, fs.qT, eq, ALU.mult, R=[fs.k_qT, k_eq], W=[k_qs])
                    for hh in range(2):
                        STT(ks[hh][0], fs.kT, gmask[:, hh:hh + 1], ek, ALU.mult, ALU.mult, R=[fs.k_kT, k_gmask, k_ek], W=[ks[hh][1]])
                pt2, kpt2 = PG_B
                with atomic():
                    for ti in range(2):
                        MM(pt2[:, ti * 256:(ti + 1) * 256], tribf[:, d, :], fs.g[:, d, ti, :], R=[fs.k_g, k_tribf], W=[kpt2])
                    ACT(ekt, pt2[:, :].rearrange("p (a b) -> p a b", a=2), AF.Exp, scale=-1.0, R=[kpt2], W=[k_ekt])
                TT("pool", kst, fs.ktm, ekt, ALU.mult, R=[fs.k_ktm, k_ekt], W=[k_kst])

            def gla_tile(fs, d, full, ti):
                last = 127 if d == 0 else 0
                sg, ksg = Sg[d]
                sgb, ksgb = Sgb[d]
                yot, kyo = yo[ti]
                tsl = slice(ti * 128, (ti + 1) * 128)
                if full:
                    pa, kpa = PG_A
                    with atomic():
                        for h in range(4):
                            p, hh = h // 2, h % 2
                            MM(pa[:, h * 128:(h + 1) * 128], ks[hh][0][:, p, tsl], qs[:, p, tsl], R=[ks[hh][1], k_qs], W=[kpa])
                        TT("dve", attm, pa[:, :].rearrange("p (h i) -> p h i", h=4), bc(trif[:, d, :].unsqueeze(1), [128, 4, 128]), ALU.mult,
                           R=[kpa, k_trif], W=[k_attm])
                    po, kpo = PG_B
                    with atomic():
                        for p in range(2):
                            MM(po[:, p * 256:(p + 1) * 256], qs[:, p, tsl], sgb[:, p, :], start=True, stop=False, R=[k_qs, ksgb], W=[kpo])
                            for hh in range(2):
                                h = 2 * p + hh
                                MM(po[:, h * 128:(h + 1) * 128], attm[:, h, :], fs.v[:, ti, h * 128:(h + 1) * 128], start=False, stop=(hh == 1),
                                   R=[k_attm, fs.k_v], W=[kpo])
                        if d == 0:
                            CP("act", yot[:, 0:512], po[:, :], R=[kpo], W=[kyo + ":g"])
                        else:
                            TT("dve", yot[:, 0:512], po[:, :], yot[:, 0:512], ALU.add, R=[kpo, kyo + ":g"], W=[kyo + ":g"])
                psg, kpsg = PG_A
                with atomic():
                    for p in range(2):
                        MM(psg[:, p * 256:(p + 1) * 256], kst[:, ti, p * 128:(p + 1) * 128], fs.v[:, ti, p * 256:(p + 1) * 256],
                           R=[k_kst, fs.k_v], W=[kpsg])
                    TT("dve", tmpg, sg, psg[:, :].rearrange("p (a b) -> p a b", a=2), ALU.add, R=[ksg, kpsg], W=[k_tmpg])
                for p in range(2):
                    STT(sg[:, p, :], tmpg[:, p, :], eq[:, p, ti * 128 + last: ti * 128 + last + 1], bdg, ALU.mult, ALU.mult,
                        R=[k_tmpg, k_eq, k_bdg], W=[ksg])
                CP("act", sgb, sg, R=[ksg], W=[ksgb])

            def ssd_tile(fs, d, full, ti):
                ssd_, kss = Ss[d]
                ssb, kssb = Ssb[d]
                yot, kyo = yo[ti]
                tsl = slice(ti * 128, (ti + 1) * 128)
                av = fs.a[:, ti, d * 8:(d + 1) * 8]
                kp1 = "pb5:a"
                with atomic():
                    MM(P5[:, 0:8], trif[:, d, :], av, R=[k_trif, fs.k_a], W=[kp1])
                    MM(P5[:, 8:16], uppf[:, d, :], av, R=[k_uppf, fs.k_a], W=[kp1])
                    MM(P5[:, 16:24], onesf, av, R=[k_onesf, fs.k_a], W=[kp1])
                    ACT(e3, P5[:, 0:24].rearrange("p (a b) -> p a b", a=3), AF.Exp, R=[kp1], W=[k_e3])
                    if full:
                        ACT(p1s, P5[:, 0:8], AF.Copy, R=[kp1], W=[k_p1s])
                if full:
                    CP("pool", cspb[:, 0:8], p1s, R=[k_p1s], W=[k_cspb])
                    TT("pool", cspb[:, 8:16], p1s, cspb[:, 0:8], ALU.subtract, R=[k_p1s, k_cspb], W=[k_cspb])
                    TS("pool", cspb[:, 32:48], cspb[:, 0:16], -1.0, ALU.mult, R=[k_cspb], W=[k_cspb])
                TT("pool", wj, e3[:, 1, :], fs.dt[:, ti, d * 8:(d + 1) * 8], ALU.mult, R=[k_e3, fs.k_dt], W=[k_wj])
                if full:
                    ptb = P5BF[:, 640:768]
                    kptb = "pb5:t"
                    with atomic():
                        TR(ptb[0:48, :], cspb, R=[k_cspb, k_ident], W=[kptb])
                        TT("dve", vall[0:16, :].rearrange("p (h i) -> p h i", h=8), bc(ptb[0:16, :].unsqueeze(1), [16, 8, 128]),
                           bd16[0:16, :].rearrange("p (h i) -> p h i", h=8), ALU.mult, R=[kptb, k_bd16], W=[k_vall])
                        CP("act", uall[32:48, :], ptb[32:48, :], R=[kptb], W=[k_uall])
                    kb0, kb1 = "pbig:0", "pbig:1"
                    with atomic():
                        for half in range(2):
                            MM(PBIG[:, half * 512:(half + 1) * 512], uall[0:48, :], vall[0:48, half * 512:(half + 1) * 512], start=True, stop=False,
                               R=[k_uall, k_vall], W=[kb0 if half == 0 else kb1])
                            MM(PBIG[:, half * 512:(half + 1) * 512], ident, negm[:, d, :], start=False, stop=True, R=[k_ident, k_negm],
                               W=[kb0 if half == 0 else kb1])
                        for half in range(2):
                            ACT(LT[:, half * 512:(half + 1) * 512], PBIG[:, half * 512:(half + 1) * 512], AF.Exp,
                                R=[kb0 if half == 0 else kb1], W=[k_LT])
                    kpG = "pb5:g"
                    with atomic():
                        for g in range(2):
                            MM(P5[:, 32 + g * 128: 32 + (g + 1) * 128], fs.BTm[:, g, tsl], fs.CT[:, tsl], R=[fs.k_BTm, fs.k_CT], W=[kpG])
                        TT("dve", scT.rearrange("p (g h) i -> p g h i", g=2),
                           bc(P5[:, 32:288].rearrange("p (g i) -> p g i", g=2).unsqueeze(2), [128, 2, 4, 128]),
                           LT.rearrange("p (g h i) -> p g h i", g=2, h=4), ALU.mult, R=[kpG, k_LT], W=[k_scT])
                    TT("pool", xdt, fs.x[:, ti, :].rearrange("p (h q) -> p h q", h=8),
                       bc(fs.dt[:, ti, d * 8:(d + 1) * 8].unsqueeze(2), [128, 8, 64]), ALU.mult, R=[fs.k_x, fs.k_dt], W=[k_xdt])
                    with atomic():
                        for h in range(8):
                            MM(PBIG[:, h * 64:(h + 1) * 64], scT[:, h, :], xdt[:, h, :], R=[k_scT, k_xdt], W=[kb0])
                        MM(PBIG[:, 512:1024], fs.CT[:, tsl], ssb, R=[fs.k_CT, kssb], W=[kb1])
                        TT("dve", tmpi.rearrange("p (h q) -> p h q", h=8), PBIG[:, 512:1024].rearrange("p (h q) -> p h q", h=8),
                           bc(e3[:, 0, :].unsqueeze(2), [128, 8, 64]), ALU.mult, R=[kb1, k_e3], W=[k_tmpi])
                        if d == 1:
                            TT("dve", tmpi, tmpi, yot[:, 512:1024], ALU.add, R=[k_tmpi, kyo + ":s"], W=[k_tmpi])
                        TT("dve", yot[:, 512:1024], PBIG[:, 0:512], tmpi, ALU.add, R=[kb0, k_tmpi], W=[kyo + ":s"])
                TT("pool", xw, fs.x[:, ti, :].rearrange("p (h q) -> p h q", h=8), bc(wj.unsqueeze(2), [128, 8, 64]), ALU.mult,
                   R=[fs.k_x, k_wj], W=[k_xw])
                TT("pool" if (full and d == 1) else "dve", tmps.rearrange("p (h q) -> p h q", h=8), ssd_.rearrange("p (h q) -> p h q", h=8),
                   bc(e3[:, 2, :].unsqueeze(2), [128, 8, 64]), ALU.mult, R=[kss, k_e3], W=[k_tmps])
                with atomic():
                    MM(PBIG[:, 0:512], fs.Btm[:, ti, :], xw.rearrange("p h q -> p (h q)"), R=[fs.k_Btm, k_xw], W=["pbig:0"])
                    TT("dve", ssd_, tmps, PBIG[:, 0:512], ALU.add, R=[k_tmps, "pbig:0"], W=[kss])
                if os.environ.get("X_SSB"):
                    TT("pool", ssb.rearrange("p (h q) -> p h q", h=8), ssd_.rearrange("p (h q) -> p h q", h=8),
                       bc(bds.unsqueeze(2), [128, 8, 64]), ALU.mult, R=[kss, k_bds], W=[kssb])
                else:
                    for g in range(2):
                        TS("dve", ssb[:, g * 256:(g + 1) * 256], ssd_[:, g * 256:(g + 1) * 256], gmask[:, g:g + 1], ALU.mult,
                           R=[kss, k_gmask], W=[kssb])

            obs = [xsb[2], xsb[0]]

            def merge_gla(fs, ti, T):
                yot, kyo = yo[ti]
                ob_, kob = obs[T % 2]
                for h in range(4):
                    ACT(sq.bitcast(BF16)[:, h * 128:(h + 1) * 128], yot[:, h * 128:(h + 1) * 128], AF.Square, accum=mst[:, h:h + 1],
                        R=[kyo + ":g"], W=[k_sq + ":g", k_mst])
                ACT(mst[:, 0:4], mst[:, 0:4], AF.Ln, scale=1.0 / 128, bias=1e-6, R=[k_mst], W=[k_mst])
                ACT(mst[:, 0:4], mst[:, 0:4], AF.Exp, scale=-0.5, R=[k_mst], W=[k_mst])
                for h in range(4):
                    STT(ob_[:, h * 128:(h + 1) * 128], yot[:, h * 128:(h + 1) * 128], mst[:, h:h + 1], fs.r[:, ti, h * 128:(h + 1) * 128],
                        ALU.mult, ALU.mult, R=[kyo + ":g", k_mst, fs.k_r], W=[kob + ":g"])

            def merge_ssd(fs, ti, T):
                yot, kyo = yo[ti]
                ob_, kob = obs[T % 2]
                TT("pool", t2, fs.x[:, ti, :], dskb, ALU.mult, R=[fs.k_x, k_dskb], W=[k_t2])
                TT("dve", t2, t2, yot[:, 512:1024], ALU.add, R=[k_t2, kyo + ":s"], W=[k_t2])
                TT("pool", t2, t2, fs.z[:, ti, :], ALU.mult, R=[k_t2, fs.k_z], W=[k_t2])
                kmb = k_mst + "b"
                for g in range(2):
                    ACT(sq.bitcast(BF16)[:, 512 + g * 256: 512 + (g + 1) * 256], t2[:, g * 256:(g + 1) * 256], AF.Square, accum=mst[:, 4 + g:5 + g],
                        R=[k_t2], W=[k_sq + ":s", kmb])
                ACT(mst[:, 4:6], mst[:, 4:6], AF.Ln, scale=1.0 / 256, bias=1e-6, R=[kmb], W=[kmb])
                ACT(mst[:, 4:6], mst[:, 4:6], AF.Exp, scale=-0.5, R=[kmb], W=[kmb])
                for g in range(2):
                    ACT(ob_[:, 512 + g * 256: 512 + (g + 1) * 256], t2[:, g * 256:(g + 1) * 256], AF.Copy, scale=mst[:, 4 + g:5 + g],
                        R=[k_t2, kmb], W=[kob + ":s"])

            def merge_out(T):
                ob_, kob = obs[T % 2]
                ptb, kptb = PT_BF[0]
                DMA("sp", xo, x_d[T * 128:(T + 1) * 128, :], W=[k_xo])
                with atomic():
                    for kc in range(8):
                        TR(ptb[:, kc * 128:(kc + 1) * 128], ob_[:, kc * 128:(kc + 1) * 128], R=[kob + ":g", kob + ":s", k_ident], W=[kptb])
                    TT("dve", oT, ptb[:, 0:1024].rearrange("p (k t) -> p k t", k=8), bc(mixg.unsqueeze(2), [128, 8, 128]), ALU.mult,
                       R=[kptb, k_mixg], W=[k_oT])
                with atomic():
                    for nb in range(2):
                        pop, kpop = PB[nb]
                        for kc in range(8):
                            MM(pop[:, :], oT[:, kc, :], wout[:, kc, nb * 512:(nb + 1) * 512], start=(kc == 0), stop=(kc == 7),
                               R=[k_oT, k_wout], W=[kpop])
                        ACT(t1.bitcast(BF16)[:, nb * 512:(nb + 1) * 512], pop[:, :], AF.Square, accum=pst[:, nb:nb + 1],
                            R=[kpop], W=[k_t1, k_pst])
                    TT("dve", pst[:, 2:3], pst[:, 0:1], pst[:, 1:2], ALU.add, R=[k_pst], W=[k_pst])
                    ACT(pst[:, 2:3], pst[:, 2:3], AF.Ln, scale=1.0 / D, bias=1e-6, R=[k_pst], W=[k_pst])
                    ACT(pst[:, 3:4], pst[:, 2:3], AF.Exp, scale=-0.5, R=[k_pst], W=[k_pst])
                    for nb in range(2):
                        pop, kpop = PB[nb]
                        STT(tmo[:, nb * 512:(nb + 1) * 512], pop[:, :], pst[:, 3:4], gate_mix[:, nb * 512:(nb + 1) * 512],
                            ALU.mult, ALU.mult, R=[kpop, k_pst, k_gmix], W=[k_tmo])
                TT("dve", tmo[:, 0:512], tmo[:, 0:512], xo[:, 0:512], ALU.add, R=[k_tmo, k_xo], W=[k_tmo])
                TT("pool", tmo[:, 512:1024], tmo[:, 512:1024], xo[:, 512:1024], ALU.add, R=[k_tmo, k_xo], W=[k_tmo])
                DMA("pool", xn_scr[T * 128:(T + 1) * 128, :], tmo, R=[k_tmo], W=["xn_scr:%d" % T])

            fs_i = [0]

            def next_fs():
                fs_i[0] += 1
                return fsets[fs_i[0] % 2]

            FS_NAMES = ["qT", "kT", "ktm", "v", "g", "x", "Btm", "BTm", "CT", "dt", "a", "r", "z"]
            fs_scr = {}
            for nm in FS_NAMES:
                ap0 = getattr(fsets[0], nm)
                shp = [8] + [int(v) for v in ap0.shape]
                fs_scr[nm] = nc.dram_tensor("fsscr_" + nm, shp, ap0.dtype, kind="Internal").ap()

            def spill_fs(f, blk):
                for nm in FS_NAMES:
                    DMA("pool", fs_scr[nm][blk], getattr(f, nm), R=[getattr(f, "k_" + nm)], W=["fsscr:%s:%d" % (nm, blk)])

            def reload_fs(f, blk):
                for nm in FS_NAMES:
                    DMA("sp", getattr(f, nm), fs_scr[nm][blk], R=["fsscr:%s:%d" % (nm, blk)], W=[getattr(f, "k_" + nm)])

            seq = [(ctx_d, 0, True, CTX, "s2", [(0, False), (1, False)], -1)]
            for blk in range(15, 7, -1):
                seq.append((x_d, blk * 256, False, SEQ, "sB", [(1, False)], blk))
            for blk in range(0, 8):
                seq.append((x_d, blk * 256, False, SEQ, "full", [(0, True)], blk))
            fsl = []
            for j in range(len(seq)):
                fsl.append(next_fs())
            feats_norm(0, *seq[0][0:4])
            with Chain() as cn:
                feats_norm(1, *seq[1][0:4])
            with Chain() as cf:
                feats_proj(fsl[0], 0, *seq[0][0:5])
            interleave([cn, cf])
            for i, item in enumerate(seq):
                fcur = fsl[i]
                chains = []
                if i + 2 < len(seq):
                    with Chain() as cn:
                        feats_norm((i + 2) % 2, *seq[i + 2][0:4])
                    chains.append(cn)
                if i + 1 < len(seq):
                    with Chain() as cf:
                        feats_proj(fsl[i + 1], (i + 1) % 2, *seq[i + 1][0:5])
                    chains.append(cf)
                with Chain() as cg:
                    for (d, full) in item[5]:
                        gla_prep(fcur, d, full)
                        for ti in ([0, 1] if d == 0 else [1, 0]):
                            gla_tile(fcur, d, full, ti)
                with Chain() as cs:
                    for (d, full) in item[5]:
                        for ti in ([0, 1] if d == 0 else [1, 0]):
                            ssd_tile(fcur, d, full, ti)
                chains += [cg, cs]
                interleave(chains)
                if item[4] == "full":
                    blk = item[6]
                    for ti in range(2):
                        T = blk * 2 + ti
                        DMA("pool", yf_scr[T * 128:(T + 1) * 128, :], yo[ti][0], R=[yo[ti][1] + ":g", yo[ti][1] + ":s"], W=["yf_scr:%d" % T])
                    spill_fs(fcur, blk)
            S.barrier()
            A2 = Arena(arena_t, win_end)
            A2.off = win_off
            wout, k_wout = A2.alloc([8, D], BF16, "wout")
            t2, k_t2 = A.alloc([512], F32, "t2")
            wmbL = [A2.alloc([8, 256], F32, "wmbL%d" % i) for i in range(2)]
            wmbbL = [A2.alloc([8, 256], BF16, "wmbbL%d" % i) for i in range(2)]
            rowp, k_rowp = A2.alloc([256], F32, "rowp")
            bmodp, k_bmodp = A2.alloc([256], F32, "bmodp")
            nrowp, k_nrowp = A2.alloc([256], F32, "nrowp")
            ncolp, k_ncolp = A2.alloc([16], F32, "ncolp")
            DMA("sp", ncolp, ncol_d, W=[k_ncolp])
            scvP = scbP.rearrange("p (k j) -> p k j", j=2)

            def late_mod(k):
                c0 = 2 * D + 256 * k
                vec, q = k // 4, k % 4
                wt, kw_ = wmbL[k % 2]
                wb, kwb = wmbbL[k % 2]
                DMA("act", wt, wm_view[:, :, c0:c0 + 256], W=[kw_])
                DMA("act", bmodp[0:2, :], bmod_d[:, c0:c0 + 256], W=[k_bmodp])
                CP("act", wb[:, 0:4, :], wt[:, 0:4, :], R=[kw_], W=[kwb])
                CP("act", wb[:, 4:8, :], wt[:, 4:8, :], R=[kw_], W=[kwb])
                pt, kp = PB[k % 2]
                with atomic():
                    for kc in range(8):
                        MM(pt[0:2, 0:256], scvP[:, kc, :], wb[:, kc, :], start=(kc == 0), stop=(kc == 7), R=[k_scbP, kwb], W=[kp])
                    TT("dve", rowp[0:2, :], pt[0:2, 0:256], bmodp[0:2, :], ALU.add, R=[kp, k_bmodp], W=[k_rowp])
                if vec in (0, 3):
                    gt, kg = (gate_mix, k_gmix) if vec == 0 else (gate_ffn, k_gffn)
                    n0 = (0 if vec == 0 else D) + q * 256
                    DMA("sp", nrowp[0:1, :], nrow_d[0:1, n0:n0 + 256], W=[k_nrowp])
                    TT("dve", rowp[0:1, :], rowp[0:1, :], nrowp[0:1, :], ALU.mult, R=[k_rowp, k_nrowp], W=[k_rowp])
                    with atomic():
                        MM(pt[:, 0:256], onesf[0:1, :], rowp[0:1, :], R=[k_onesf, k_rowp], W=[kp])
                        CP("act", gt[:, q * 256:(q + 1) * 256], pt[:, 0:256], R=[kp], W=[kg])
                else:
                    base = SH_FFN if vec == 1 else G_FFN
                    with atomic():
                        for j in range(2):
                            MM(pt[:, 2 * j:2 * j + 2], rowp[0:1, j * 128:(j + 1) * 128], onesf[0:1, 0:2], R=[k_rowp, k_onesf], W=[kp])
                        CP("dve", mcols[:, base + 2 * q: base + 2 * q + 2], pt[:, 0:4].rearrange("p (c two) -> p c two", two=2)[:, :, 0],
                           R=[kp], W=[k_mcols])
                    if vec == 2:
                        sl = slice(base + 2 * q, base + 2 * q + 2)
                        TS("dve", mcols[:, sl], mcols[:, sl], 1.0, ALU.add, R=[k_mcols], W=[k_mcols])
                        TT("dve", mcols[:, sl], mcols[:, sl], ncolp[:, 8 + 2 * q: 8 + 2 * q + 2], ALU.mult, R=[k_mcols, k_ncolp], W=[k_mcols])
            wout_view = wout_d.rearrange("(k p) n -> p k n", p=128)
            for kc in range(8):
                DMA("pool", wout[:, kc, :], wout_view[:, kc, :], W=[k_wout], max_dma_last_dim=4096)
            st1 = None
            st2 = None
            for blk in range(7, -1, -1):
                if blk == 7:
                    f = fsets[fs_i[0] % 2]
                else:
                    f = next_fs()
                    reload_fs(f, blk)
                for ti in (1, 0):
                    T = blk * 2 + ti
                    if not (blk == 7):
                        DMA("sp", yo[ti][0], yf_scr[T * 128:(T + 1) * 128, :], R=["yf_scr:%d" % T], W=[yo[ti][1] + ":g", yo[ti][1] + ":s"])
                    with Chain() as cg:
                        if ti == 1:
                            gla_prep(f, 1, True)
                        gla_tile(f, 1, True, ti)
                    with Chain() as cs:
                        ssd_tile(f, 1, True, ti)
                    chains = [cg, cs]
                    if T >= 14:
                        late_ks = [2 * (15 - T), 2 * (15 - T) + 1]
                    elif T >= 2:
                        late_ks = [4 + (13 - T)]
                    else:
                        late_ks = []
                    if late_ks:
                        with Chain() as cl:
                            for k_ in late_ks:
                                late_mod(k_)
                        chains.append(cl)
                    if st2 is not None:
                        with Chain() as c3:
                            merge_out(st2)
                        chains.append(c3)
                    if st1 is not None:
                        with Chain() as c1:
                            merge_gla(*st1)
                        with Chain() as c2:
                            merge_ssd(*st1)
                        chains += [c1, c2]
                    interleave(chains)
                    st2 = st1[2] if st1 is not None else None
                    st1 = (f, ti, T)
            chains = []
            if st2 is not None:
                with Chain() as c3:
                    merge_out(st2)
                chains.append(c3)
            with Chain() as c1:
                merge_gla(*st1)
            with Chain() as c2:
                merge_ssd(*st1)
            interleave(chains + [c1, c2])
            merge_out(st1[2])

            S.barrier()
            A.off = win_off

            wdn, k_wdn = A.alloc([22, D], BF16, "wdn")
            wdn_view = wdown_d.rearrange("(k p) n -> p k n", p=128)
            wdn_pending = list(range(22))
            h2Ts = [A.alloc([8, 1024], BF16, "h2T%d" % i) for i in range(2)]
            actT, k_actT = A.alloc([22, 1024], BF16, "actT")
            wgb = [A.alloc([8, 512], BF16, "wgb%d" % i) for i in range(2)]
            wub = [A.alloc([8, 512], BF16, "wub%d" % i) for i in range(2)]
            xn2 = [A.alloc([D], F32, "xn2_%d" % i) for i in range(2)]
            xn3 = [A.alloc([D], F32, "xn3_%d" % i) for i in range(2)]
            xs2 = [A.alloc([D], BF16, "xs2_%d" % i) for i in range(2)]
            junk2, k_junk2 = A.alloc([D], BF16, "junk2")
            st2, k_st2 = A.alloc([2, 4], F32, "st2")
            sgt = [A.alloc([512], F32, "sgt%d" % i) for i in range(2)]
            tf, k_tf = A.alloc([D], F32, "tf")
            fst, k_fst = A.alloc([2, 4], F32, "fst")
            print("arena phase F bytes:", A.off)
            wg_view = wgate_d.rearrange("(k p) n -> p k n", p=128)
            wu_view = wup_d.rearrange("(k p) n -> p k n", p=128)

            def ffn_prenorm(sb):
                h2T, k_h2T = h2Ts[sb % 2]
                xnb = (xn2 + xn3) if sb == 0 else xn2
                for tl in range(8):
                    T = sb * 8 + tl
                    j = tl % 2
                    xnt, kxn = xnb[tl % len(xnb)]
                    kst_ = k_st2 + ":%d" % j
                    DMA("sp", xnt, xn_scr[T * 128:(T + 1) * 128, :], R=["xn_scr:%d" % T], W=[kxn])
                    ACT(junk2, xnt, AF.Square, accum=st2[:, j, 0:1], R=[kxn], W=[k_junk2, kst_])
                    ACT(st2[:, j, 1:2], st2[:, j, 0:1], AF.Ln, scale=1.0 / D, bias=1e-6, R=[kst_], W=[kst_])
                    ACT(st2[:, j, 3:4], st2[:, j, 1:2], AF.Exp, scale=-0.5, R=[kst_], W=[kst_])
                    xs_, kxs = xs2[j]
                    ACT(xs_, xnt, AF.Copy, scale=st2[:, j, 3:4], R=[kxn, kst_], W=[kxs])
                    ptb, kptb = PT_BF[j]
                    with atomic():
                        for kc in range(8):
                            TR(ptb[:, kc * 128:(kc + 1) * 128], xs_[:, kc * 128:(kc + 1) * 128], R=[kxs, k_ident], W=[kptb])
                        for kc in range(8):
                            TS("dve", h2T[:, kc, tl * 128:(tl + 1) * 128], ptb[:, kc * 128:(kc + 1) * 128],
                               mcols[:, G_FFN + kc: G_FFN + kc + 1], ALU.mult, mcols[:, SH_FFN + kc: SH_FFN + kc + 1], ALU.add,
                               R=[kptb, k_mcols], W=[k_h2T + ":%d" % tl])

            w_issued = set()

            def ffn_wload(sb, fg):
                if (sb, fg) in w_issued:
                    return
                w_issued.add((sb, fg))
                nf = 4 if fg < 5 else 2
                wgt, kwg = wgb[fg % 2]
                wut, kwu = wub[fg % 2]
                c0 = fg * 512
                DMA("pool", wgt[:, :, 0:nf * 128], wg_view[:, :, c0:c0 + nf * 128], W=[kwg])
                DMA("pool", wut[:, :, 0:nf * 128], wu_view[:, :, c0:c0 + nf * 128], W=[kwu])

            def ffn_gateup(sb):
                h2T, k_h2T = h2Ts[sb % 2]
                K_H2 = [k_h2T + ":%d" % tl for tl in range(8)]
                for fg in range(6):
                    nf = 4 if fg < 5 else 2
                    wgt, kwg = wgb[fg % 2]
                    wut, kwu = wub[fg % 2]
                    ffn_wload(sb, fg)
                    if fg >= 1:
                        for _ in range(6):
                            if wdn_pending:
                                fc_ = wdn_pending.pop(0)
                                DMA("pool", wdn[:, fc_, :], wdn_view[:, fc_, :], W=[k_wdn + ":%d" % fc_])
                    for fi in range(nf):
                        fc = fg * 4 + fi
                        for nbk in range(2):
                            pg, kpg = PB[(2 * nbk) % 4]
                            pu, kpu = PB[(2 * nbk + 1) % 4]
                            rk = K_H2[nbk * 4:(nbk + 1) * 4]
                            for kc in range(8):
                                MM(pg[:, :], wgt[:, kc, fi * 128:(fi + 1) * 128], h2T[:, kc, nbk * 512:(nbk + 1) * 512], start=(kc == 0), stop=(kc == 7),
                                   R=[kwg] + rk, W=[kpg])
                            for kc in range(8):
                                MM(pu[:, :], wut[:, kc, fi * 128:(fi + 1) * 128], h2T[:, kc, nbk * 512:(nbk + 1) * 512], start=(kc == 0), stop=(kc == 7),
                                   R=[kwu] + rk, W=[kpu])
                            sg_, ksg_ = sgt[nbk]
                            ACT(sg_, pg[:, :], AF.Silu, R=[kpg], W=[ksg_])
                            TT("dve", actT[:, fc, nbk * 512:(nbk + 1) * 512], pu[:, :], sg_, ALU.mult, R=[kpu, ksg_], W=[k_actT + ":%d" % nbk])

            def ffn_down(sb):
                djunk = sgt[0][0].bitcast(BF16)
                kdj = sgt[0][1]
                for tl in range(8):
                    T = sb * 8 + tl
                    nbk = tl // 4
                    j = tl % 2
                    halves = []
                    for nb in range(2):
                        if j == 0:
                            halves.append((psum2[:, nb * 512:(nb + 1) * 512], "pbig:%d" % nb))
                        else:
                            halves.append((PB[nb][0][:, :], PB[nb][1]))
                    xnt, kxn = xn3[j]
                    DMA("sp", xnt, xn_scr[T * 128:(T + 1) * 128, :], R=["xn_scr:%d" % T], W=[kxn])
                    kf = k_fst + ":%d" % j
                    with atomic():
                        for nb in range(2):
                            pf, kpf = halves[nb]
                            for fc in range(22):
                                MM(pf, actT[:, fc, tl * 128:(tl + 1) * 128], wdn[:, fc, nb * 512:(nb + 1) * 512],
                                   start=(fc == 0), stop=(fc == 21), R=[k_actT + ":%d" % nbk, k_wdn + ":%d" % fc], W=[kpf])
                            ACT(djunk[:, nb * 512:(nb + 1) * 512], pf, AF.Square, accum=fst[:, j, nb:nb + 1], R=[kpf], W=[kdj, kf])
                        TT("dve", fst[:, j, 2:3], fst[:, j, 0:1], fst[:, j, 1:2], ALU.add, R=[kf], W=[kf])
                        ACT(fst[:, j, 2:3], fst[:, j, 2:3], AF.Ln, scale=1.0 / D, bias=1e-6, R=[kf], W=[kf])
                        ACT(fst[:, j, 3:4], fst[:, j, 2:3], AF.Exp, scale=-0.5, R=[kf], W=[kf])
                        for nb in range(2):
                            pf, kpf = halves[nb]
                            STT(tf[:, nb * 512:(nb + 1) * 512], pf, fst[:, j, 3:4], gate_ffn[:, nb * 512:(nb + 1) * 512],
                                ALU.mult, ALU.mult, R=[kpf, kf, k_gffn], W=[k_tf])
                    TT("dve", xnt[:, 0:512], tf[:, 0:512], xnt[:, 0:512], ALU.add, R=[k_tf, kxn], W=[kxn])
                    TT("pool", xnt[:, 512:1024], tf[:, 512:1024], xnt[:, 512:1024], ALU.add, R=[k_tf, kxn], W=[kxn])
                    tok_holder = _sched("dma", "act", (lambda e, T=T, xnt=xnt: e.dma_start(out=y_d[T * 128:(T + 1) * 128, :], in_=xnt)),
                                        [kxn], ["y:%d" % T])
                    if tok_holder is not None:
                        out_toks.append(tok_holder)

            ffn_prenorm(0)
            ffn_gateup(0)
            ffn_wload(1, 0)
            ffn_wload(1, 1)
            with Chain() as cd:
                ffn_down(0)
            with Chain() as cp:
                ffn_prenorm(1)
            interleave([cd, cp])
            ffn_gateup(1)
            ffn_down(1)
        except _Stop:
            pass
        S.barrier()
        S.emit(st)
    return nc


_PROG = {}


def _consts():
    bf = ml_dtypes.bfloat16
    t = np.arange(128)
    TF = (t[:, None] <= t[None, :]).astype(np.float32)
    TB = (t[:, None] >= t[None, :]).astype(np.float32)
    UF = (t[:, None] > t[None, :]).astype(np.float32)
    UB = (t[:, None] < t[None, :]).astype(np.float32)
    tri = np.concatenate([TF, TB], axis=1)
    upp = np.concatenate([UF, UB], axis=1)
    negF = np.tile((1.0 - TF) * NEG, (1, 4))
    negB = np.tile((1.0 - TB) * NEG, (1, 4))
    negm = np.concatenate([negF, negB], axis=1)
    bd = np.zeros((16, 8, 128), np.float32)
    for s in range(2):
        for h in range(8):
            bd[s * 8 + h, h, :] = 1.0
    bd = bd.reshape(16, 1024)
    uall = np.zeros((48, 128), np.float32)
    uall[0:16] = 1.0
    vall = np.zeros((48, 1024), np.float32)
    vall[32:48] = bd
    bdg = np.zeros((128, 256), np.float32)
    bdg[0:64, 0:128] = 1.0
    bdg[64:128, 128:256] = 1.0
    bds = np.zeros((128, 8), np.float32)
    bds[0:64, 0:4] = 1.0
    bds[64:128, 4:8] = 1.0
    gmask = np.zeros((128, 2), np.float32)
    gmask[0:64, 0] = 1.0
    gmask[64:128, 1] = 1.0
    return {
        "ident": np.eye(128, dtype=np.float32).astype(bf),
        "tri_bf": tri.astype(bf), "tri_f": tri, "upp_f": upp,
        "ones_f": np.ones((128, 128), np.float32),
        "negm": negm.astype(bf), "uall_t": uall.astype(bf), "vall_t": vall.astype(bf),
        "bdg": bdg, "bds": bds, "gmask": gmask,
    }


def _col(v):
    return np.ascontiguousarray(v.reshape(8, 128).T)


def make_in_maps(x, c, ctx, c_ctx, w_mod, b_mod, norm_mix_pre, norm_mix_post, norm_ffn_pre, norm_ffn_post,
                 w_in, conv_w, conv_b, gla_wg_f, gla_bg_f, gla_wg_b, gla_bg_b, gla_norm,
                 a_log_f, a_log_b, dt_bias_f, dt_bias_b, d_skip, ssd_norm, w_out, w_gate, w_up, w_down):
    f32 = np.float32
    L = 0
    consts = _consts()
    w_in0 = np.asarray(w_in[L], f32)
    perm = np.arange(DIN)
    perm[1536:1552], perm[1552:1568] = np.arange(1552, 1568), np.arange(1536, 1552)
    perm[2848:2856], perm[2856:2864] = np.arange(2856, 2864), np.arange(2848, 2856)
    w_in1 = np.ascontiguousarray(w_in0[:, perm])
    shared = dict(consts)
    shared.update({
        "w_mod": np.asarray(w_mod[L], f32),
        "b_mod2": np.ascontiguousarray(np.broadcast_to(np.asarray(b_mod[L], f32)[None, :], (2, 6 * D))),
        "ncol": np.concatenate([_col(np.asarray(norm_mix_pre[L], f32)), _col(np.asarray(norm_ffn_pre[L], f32))], axis=1),
        "nrow": np.concatenate([np.asarray(norm_mix_post[L], f32), np.asarray(norm_ffn_post[L], f32)])[None, :],
        "cb": np.ascontiguousarray(np.asarray(conv_b[L], f32).reshape(6, 128).T),
        "mixg": np.concatenate([np.broadcast_to(np.asarray(gla_norm[L], f32)[:, None], (128, 4)),
                                np.asarray(ssd_norm[L], f32).reshape(4, 128).T], axis=1).astype(f32),
        "dsk": np.ascontiguousarray(np.broadcast_to(np.repeat(np.asarray(d_skip[L], f32), 64)[None, :], (128, 512))),
        "w_out": np.asarray(w_out[L], f32), "w_gate": np.asarray(w_gate[L], f32),
        "w_up": np.asarray(w_up[L], f32), "w_down": np.asarray(w_down[L], f32),
    })
    per_th = []
    for th in range(2):
        cw = np.asarray(conv_w[L], f32)
        if th == 1:
            cw = cw[::-1, ::-1, :]
        cwl = np.ascontiguousarray(cw.reshape(9, 6, 128).transpose(2, 1, 0).reshape(128, 54))
        wgf, wgb_ = np.asarray(gla_wg_f[L], f32), np.asarray(gla_wg_b[L], f32)
        bgf, bgb = np.asarray(gla_bg_f[L], f32), np.asarray(gla_bg_b[L], f32)
        alf, alb = np.asarray(a_log_f[L], f32), np.asarray(a_log_b[L], f32)
        dbf, dbb = np.asarray(dt_bias_f[L], f32), np.asarray(dt_bias_b[L], f32)
        if th == 1:
            wgf, wgb_, bgf, bgb = wgb_, wgf, bgb, bgf
            alf, alb, dbf, dbb = alb, alf, dbb, dbf
        per_th.append({
            "w_in": w_in0 if th == 0 else w_in1,
            "cw": cwl,
            "wg": np.ascontiguousarray(np.concatenate([wgf, wgb_], axis=1)),
            "bg": np.concatenate([bgf, bgb])[None, :].astype(f32),
            "alog": np.ascontiguousarray(np.broadcast_to(np.concatenate([alf, alb])[None, :], (128, 16))),
            "dtb": np.ascontiguousarray(np.broadcast_to(np.concatenate([dbf, dbb])[None, :], (128, 16))),
        })
    in_maps = []
    x = np.asarray(x, f32)
    ctx = np.asarray(ctx, f32)
    c = np.asarray(c, f32)
    c_ctx = np.asarray(c_ctx, f32)
    for core in range(8):
        b, th = core // 2, core % 2
        m = dict(shared)
        m.update(per_th[th])
        xb, cxb = x[b], ctx[b]
        if th == 1:
            xb, cxb = xb[::-1], cxb[::-1]
        m["x"] = np.ascontiguousarray(xb)
        m["ctx"] = np.ascontiguousarray(cxb)
        cc = np.stack([_col(c[b]), _col(c_ctx)], axis=2).reshape(128, 16)
        m["cc"] = np.ascontiguousarray(cc)
        in_maps.append(m)
    return in_maps


def kernel(**inputs):
    if "nc" not in _PROG:
        _PROG["nc"] = build_program()
    nc = _PROG["nc"]
    in_maps = make_in_maps(**inputs)
    res = run_bass_kernel_spmd(nc, in_maps, core_ids=list(range(8)))
    out = np.empty((4, SEQ, D), np.float32)
    for core in range(8):
        b, th = core // 2, core % 2
        y = np.asarray(res.results[core]["y"], np.float32)
        if th == 0:
            out[b, 0:OWN] = y
        else:
            out[b, OWN:SEQ] = y[::-1]
    return out
```

```python
import os
from contextlib import ExitStack

import ml_dtypes
import numpy as np

import concourse.bass as bass
import concourse.mybir as mybir
from concourse.bass_utils import run_bass_kernel_spmd

F32 = mybir.dt.float32
BF16 = mybir.dt.bfloat16
AF = mybir.ActivationFunctionType
ALU = mybir.AluOpType
AX = mybir.AxisListType

D = 1024
SEQ = 4096
CTX = 256
OWN = 2048
DIN = 2864
DFF = 2816
NEG = -30000.0


class Sched:
    def __init__(self, nc, n_dma_sems=32):
        self.nc = nc
        self.engs = ["pe", "act", "dve", "pool", "sp"]
        self.lists = {e: [] for e in self.engs}
        self.count = {e: 0 for e in self.engs}
        self.waited = {e: {} for e in self.engs}
        self.last_w = {}
        self.readers = {}
        self.n_dma_sems = n_dma_sems
        self.dma_cnt = [0] * n_dma_sems
        self.dma_rr = 0
        self.sems = {}

    def _need(self, eng, best):
        out = []
        for sk, val in best.items():
            if sk == eng and eng == "pe":
                continue
            if self.waited[eng].get(sk, 0) >= val:
                continue
            self.waited[eng][sk] = val
            out.append((sk, val))
        return out

    def _deps(self, reads, writes):
        best = {}

        def add(tok):
            if best.get(tok[0], 0) < tok[1]:
                best[tok[0]] = tok[1]

        for k in reads:
            if k in self.last_w:
                add(self.last_w[k])
        for k in writes:
            if k in self.last_w:
                add(self.last_w[k])
            for t in self.readers.get(k, ()):
                add(t)
        return best

    def _commit(self, tok, reads, writes):
        for k in reads:
            self.readers.setdefault(k, []).append(tok)
        for k in writes:
            self.last_w[k] = tok
            self.readers[k] = []

    def op(self, eng, fn, reads=(), writes=()):
        best = self._deps(reads, writes)
        waits = self._need(eng, best)
        self.count[eng] += 1
        tok = (eng, self.count[eng])
        self.lists[eng].append((waits, fn, (eng, 1)))
        self._commit(tok, reads, writes)
        return tok

    def dma(self, eng, fn, reads=(), writes=()):
        i = self.dma_rr
        self.dma_rr = (self.dma_rr + 1) % self.n_dma_sems
        sk = "dma%d" % i
        best = self._deps(reads, writes)
        if self.dma_cnt[i] > 0:
            best[sk] = max(best.get(sk, 0), self.dma_cnt[i])
        waits = self._need(eng, best)
        self.dma_cnt[i] += 16
        tok = (sk, self.dma_cnt[i])
        self.lists[eng].append((waits, fn, (sk, 16)))
        self._commit(tok, reads, writes)
        return tok

    def barrier(self):
        best = {}
        for e in self.engs:
            if self.count[e] > 0:
                best[e] = self.count[e]
        for i in range(self.n_dma_sems):
            if self.dma_cnt[i] > 0:
                best["dma%d" % i] = self.dma_cnt[i]
        for e in self.engs:
            b = {k: v for k, v in best.items() if k != e}
            waits = self._need(e, b)
            if waits:
                self.lists[e].append((waits, None, None))
        self.last_w = {}
        self.readers = {}

    def wait_all(self, eng, toks):
        best = {}
        for s, v in toks:
            best[s] = max(best.get(s, 0), v)
        waits = self._need(eng, best)
        self.lists[eng].append((waits, None, None))

    def emit(self, stack):
        nc = self.nc
        for e in self.engs:
            self.sems[e] = stack.enter_context(nc.semaphore("s_" + e))
        for i in range(self.n_dma_sems):
            self.sems["dma%d" % i] = stack.enter_context(nc.semaphore("s_dma%d" % i))
        block = stack.enter_context(nc.Block())
        sems = self.sems

        def run(engh, lst):
            for waits, fn, inc in lst:
                for sk, v in waits:
                    engh.wait_ge(sems[sk], v)
                if fn is not None:
                    ins = fn(engh)
                    ins.then_inc(sems[inc[0]], inc[1])

        @block.tensor
        def _(e):
            run(e, self.lists["pe"])

        @block.scalar
        def _(e):
            run(e, self.lists["act"])

        @block.vector
        def _(e):
            run(e, self.lists["dve"])

        @block.gpsimd
        def _(e):
            run(e, self.lists["pool"])

        @block.sync
        def _(e):
            run(e, self.lists["sp"])


class Arena:
    def __init__(self, tile, nbytes):
        self.t = tile
        self.cap = nbytes
        self.off = 0
        self.n = 0

    def alloc(self, shape, dtype, name=None):
        n = int(np.prod(shape))
        esz = 4 if dtype == F32 else 2
        sz = (n * esz + 63) // 64 * 64
        off = self.off
        assert off + sz <= self.cap, ('arena overflow', name, off, sz, self.cap)
        self.off += sz
        ap = self.t[:, off // 2:(off + n * esz) // 2]
        if dtype == F32:
            ap = ap.bitcast(F32)
        if len(shape) == 2:
            ap = ap.rearrange("p (a b) -> p a b", a=shape[0])
        elif len(shape) == 3:
            ap = ap.rearrange("p (a b c) -> p a b c", a=shape[0], b=shape[1])
        self.n += 1
        return ap, (name or "t") + "#%d" % self.n


class _Stop(Exception):
    pass


def build_program(debug=None, stop=None):
    nc = bass.Bass("TRN2", target_bir_lowering=False)
    di = {}

    def dram_in(name, shape, dt=F32):
        di[name] = nc.dram_tensor(name, list(shape), dt, kind="ExternalInput").ap()
        return di[name]

    x_d = dram_in("x", [SEQ, D])
    ctx_d = dram_in("ctx", [CTX, D])
    cc_d = dram_in("cc", [128, 16])
    wmod_d = dram_in("w_mod", [D, 6 * D])
    bmod_d = dram_in("b_mod2", [2, 6 * D])
    ncol_d = dram_in("ncol", [128, 16])
    nrow_d = dram_in("nrow", [1, 2 * D])
    win_d = dram_in("w_in", [D, DIN])
    cw_d = dram_in("cw", [128, 54])
    cb_d = dram_in("cb", [128, 6])
    wg_d = dram_in("wg", [16, 512])
    bg_d = dram_in("bg", [1, 512])
    mixg_d = dram_in("mixg", [128, 8])
    alog_d = dram_in("alog", [128, 16])
    dtb_d = dram_in("dtb", [128, 16])
    dsk_d = dram_in("dsk", [128, 512])
    wout_d = dram_in("w_out", [D, D])
    wgate_d = dram_in("w_gate", [D, DFF])
    wup_d = dram_in("w_up", [D, DFF])
    wdown_d = dram_in("w_down", [DFF, D])
    ident_d = dram_in("ident", [128, 128], BF16)
    tribf_d = dram_in("tri_bf", [128, 256], BF16)
    trif_d = dram_in("tri_f", [128, 256])
    uppf_d = dram_in("upp_f", [128, 256])
    onesf_d = dram_in("ones_f", [128, 128])
    negm_d = dram_in("negm", [128, 1024], BF16)
    uallt_d = dram_in("uall_t", [48, 128], BF16)
    vallt_d = dram_in("vall_t", [48, 1024], BF16)
    bdg_d = dram_in("bdg", [128, 256])
    bds_d = dram_in("bds", [128, 8])
    gmask_d = dram_in("gmask", [128, 2])
    y_d = nc.dram_tensor("y", [OWN, D], F32, kind="ExternalOutput").ap()
    yf_scr = nc.dram_tensor("yf_scr", [OWN, D], F32, kind="Internal").ap()
    xn_scr = nc.dram_tensor("xn_scr", [OWN, D], F32, kind="Internal").ap()
    dbg_out = {}
    if debug:
        for nm, shp in debug.items():
            dbg_out[nm] = nc.dram_tensor("dbg_" + nm, list(shp), F32, kind="ExternalOutput").ap()

    st = ExitStack()
    dbg_toks = []
    with st:
        ARENA_BYTES = 207 * 1024
        arena_t = st.enter_context(nc.sbuf_tensor("arena", [128, ARENA_BYTES // 2], BF16))
        psum = [st.enter_context(nc.psum_tensor("pb%d" % i, [128, 512], F32)) for i in range(6)]
        psum2 = st.enter_context(nc.psum_tensor("pbig", [128, 1024], F32))
        S = Sched(nc)
        A = Arena(arena_t, ARENA_BYTES)

        PB = [(psum[i], "pb%d" % i) for i in range(6)]
        PB2 = (psum2, "pbig")

        REC = [None]
        ATOM = [0]

        class Chain:
            def __init__(self):
                self.steps = []

            def __enter__(self):
                self.prev = REC[0]
                REC[0] = self.steps
                return self

            def __exit__(self, *a):
                REC[0] = self.prev

        class atomic:
            def __enter__(self):
                if REC[0] is not None and ATOM[0] == 0:
                    REC[0].append([])
                ATOM[0] += 1

            def __exit__(self, *a):
                ATOM[0] -= 1

        def _sched(kind, eng, fn, R, W):
            def thunk():
                if kind == "op":
                    return S.op(eng, fn, reads=R, writes=W)
                return S.dma(eng, fn, reads=R, writes=W)
            if REC[0] is None:
                return thunk()
            if ATOM[0] > 0:
                REC[0][-1].append(thunk)
            else:
                REC[0].append([thunk])
            return None

        def interleave(chains):
            chains = [c for c in chains if c is not None and len(c.steps) > 0]
            pos = [0] * len(chains)
            while True:
                best, bi = None, -1
                for i, c in enumerate(chains):
                    if pos[i] < len(c.steps):
                        fr = (pos[i] + 0.5) / len(c.steps)
                        if best is None or fr < best:
                            best, bi = fr, i
                if bi < 0:
                    break
                for th in chains[bi].steps[pos[bi]]:
                    th()
                pos[bi] += 1

        def MM(out, lhsT, rhs, start=True, stop=True, R=(), W=()):
            return _sched("op", "pe", lambda e: e.matmul(out, lhsT, rhs, start=start, stop=stop), R, W)

        def TR(out, in_, R=(), W=()):
            return _sched("op", "pe", lambda e: e.transpose(out, in_, ident[0:in_.shape[0], 0:in_.shape[0]]), R, W)

        def ACT(out, in_, func, bias=None, scale=None, accum=None, R=(), W=()):
            kw = {}
            if bias is not None:
                kw["bias"] = bias
            if scale is not None:
                kw["scale"] = scale
            if accum is not None:
                kw["accum_out"] = accum
            return _sched("op", "act", lambda e: e.activation(out=out, in_=in_, func=func, **kw), R, W)

        def TT(eng, out, in0, in1, op, R=(), W=()):
            return _sched("op", eng, lambda e: e.tensor_tensor(out=out, in0=in0, in1=in1, op=op), R, W)

        def TS(eng, out, in0, s1, op0, s2=None, op1=None, R=(), W=()):
            if op1 is None:
                return _sched("op", eng, lambda e: e.tensor_scalar(out=out, in0=in0, scalar1=s1, scalar2=None, op0=op0), R, W)
            return _sched("op", eng, lambda e: e.tensor_scalar(out=out, in0=in0, scalar1=s1, scalar2=s2, op0=op0, op1=op1), R, W)

        def STT(out, in0, scalar, in1, op0, op1, R=(), W=()):
            return _sched("op", "dve", lambda e: e.scalar_tensor_tensor(out=out, in0=in0, scalar=scalar, in1=in1, op0=op0, op1=op1), R, W)

        def CP(eng, out, in_, R=(), W=()):
            if eng == "act":
                return ACT(out, in_, AF.Copy, R=R, W=W)
            return _sched("op", eng, lambda e: e.tensor_copy(out=out, in_=in_), R, W)

        def RECIP(out, in_, R=(), W=()):
            return _sched("op", "dve", lambda e: e.reciprocal(out=out, in_=in_), R, W)

        def MSET(eng, ap, val, W=()):
            return _sched("op", eng, lambda e: e.memset(ap, val), (), W)

        def DMA(q, out, in_, R=(), W=(), **kw):
            return _sched("dma", q, lambda e: e.dma_start(out=out, in_=in_, **kw), R, W)

        def bc(ap, shape):
            return ap.broadcast_to(shape)

        def dump(name, ap, key):
            dt_ = ap.dtype
            shp = [int(v) for v in ap.shape]
            o = nc.dram_tensor("dbg_" + name, shp, dt_, kind="ExternalOutput").ap()
            dbg_toks.append(DMA("sp", o, ap, R=[key], W=["dbg_" + name]))

        out_toks = []
        try:
            ident, k_ident = A.alloc([128], BF16, "ident")
            tribf, k_tribf = A.alloc([2, 128], BF16, "tribf")
            trif, k_trif = A.alloc([2, 128], F32, "trif")
            uppf, k_uppf = A.alloc([2, 128], F32, "uppf")
            onesf, k_onesf = A.alloc([128], F32, "onesf")
            onesb, k_onesb = A.alloc([128], BF16, "onesb")
            negm, k_negm = A.alloc([2, 512], BF16, "negm")
            bdg, k_bdg = A.alloc([256], F32, "bdg")
            bds, k_bds = A.alloc([8], F32, "bds")
            gmask, k_gmask = A.alloc([2], F32, "gmask")
            mixg, k_mixg = A.alloc([8], F32, "mixg")
            dskb, k_dskb = A.alloc([512], F32, "dskb")
            cb, k_cb = A.alloc([6], F32, "cb")
            gate_mix, k_gmix = A.alloc([1024], F32, "gate_mix")
            gate_ffn, k_gffn = A.alloc([1024], F32, "gate_ffn")
            mcols, k_mcols = A.alloc([64], F32, "mcols")
            CONST_K = [k_ident, k_tribf, k_trif, k_uppf, k_onesf, k_onesb, k_negm, k_bdg, k_bds, k_gmask, k_mixg, k_dskb, k_cb]
            def load_consts():
                for (ap, k, src) in [(ident, k_ident, ident_d), (onesf, k_onesf, onesf_d), (bdg, k_bdg, bdg_d),
                                     (bds, k_bds, bds_d), (gmask, k_gmask, gmask_d), (mixg, k_mixg, mixg_d),
                                     (dskb, k_dskb, dsk_d), (cb, k_cb, cb_d)]:
                    DMA("act", ap, src, W=[k])
                DMA("act", tribf, tribf_d.rearrange("p (a b) -> p a b", a=2), W=[k_tribf])
                DMA("act", trif, trif_d.rearrange("p (a b) -> p a b", a=2), W=[k_trif])
                DMA("act", uppf, uppf_d.rearrange("p (a b) -> p a b", a=2), W=[k_uppf])
                DMA("act", negm, negm_d.rearrange("p (a b) -> p a b", a=2), W=[k_negm])
                CP("dve", onesb, onesf, R=[k_onesf], W=[k_onesb])
            win_off = A.off
            win, k_win = A.alloc([8, DIN], BF16, "win")
            win_end = A.off
            win_view = win_d.rearrange("(k p) n -> p k n", p=128)
            K_WIN = [k_win + ":%d" % kc for kc in range(8)]
            scbP, k_scbP = A.alloc([16], BF16, "scbP")
            cwt, k_cwt = A.alloc([54], F32, "cwt")
            cwd, k_cwd = A.alloc([6, 9, 128], BF16, "cwd")
            identf, k_identf = A.alloc([128], F32, "identf")
            wgs, k_wgs = A.alloc([512], BF16, "wgs")
            bgs, k_bgs = A.alloc([512], BF16, "bgs")
            aneg, k_aneg = A.alloc([16], F32, "aneg")
            dtb, k_dtb = A.alloc([16], F32, "dtb")

            def setup_small():
                DMA("act", cwt, cw_d, W=[k_cwt])
                DMA("pool", wgs[0:16, :], wg_d, W=[k_wgs])
                DMA("pool", bgs[0:1, :], bg_d, W=[k_bgs])
                DMA("act", aneg, alog_d, W=[k_aneg])
                DMA("act", dtb, dtb_d, W=[k_dtb])
                ACT(aneg, aneg, AF.Exp, R=[k_aneg], W=[k_aneg])
                TS("dve", aneg, aneg, -1.0, ALU.mult, R=[k_aneg], W=[k_aneg])
                CP("dve", identf, ident, R=[k_ident], W=[k_identf])
                for c6 in range(6):
                    for tap in range(9):
                        TS("dve", cwd[:, c6, tap, :], identf, cwt[:, c6 * 9 + tap: c6 * 9 + tap + 1], ALU.mult,
                           R=[k_identf, k_cwt], W=[k_cwd])
            persist_mark = A.off

            ccs, k_ccs = A.alloc([16], F32, "ccs")
            sc, k_sc = A.alloc([16], F32, "sc")
            mrow, k_mrow = A.alloc([6 * D], F32, "mrow")
            bmod, k_bmod = A.alloc([2 * D], F32, "bmod")
            mctx, k_mctx = A.alloc([2 * D], F32, "mctx")
            ncol, k_ncol = A.alloc([16], F32, "ncol")
            wmb = [A.alloc([8, 512], F32, "wmb%d" % i) for i in range(3)]
            wmbb = [A.alloc([8, 512], BF16, "wmbb%d" % i) for i in range(2)]
            scb, k_scb = A.alloc([16], BF16, "scb")
            wm_view = wmod_d.rearrange("(k p) n -> p k n", p=128)
            DMA("sp", ccs, cc_d, W=[k_ccs])
            for nb in range(3):
                DMA("sp", wmb[nb][0], wm_view[:, :, nb * 512:(nb + 1) * 512], W=[wmb[nb][1]])
            ACT(sc, ccs, AF.Silu, R=[k_ccs], W=[k_sc])
            CP("dve", scb, sc, R=[k_sc], W=[k_scb])
            scv = scb.rearrange("p (k j) -> p k j", j=2)
            CP("dve", scbP, sc, R=[k_sc], W=[k_scbP])
            DMA("act", bmod[0:2, :], bmod_d[:, 0:2 * D], W=[k_bmod])
            DMA("act", ncol, ncol_d, W=[k_ncol])
            load_consts()
            for kc in range(8):
                DMA("pool", win[:, kc, :], win_view[:, kc, :], W=[K_WIN[kc]], max_dma_last_dim=4096)
            if stop == "mod0":
                dump("sc", sc, k_sc)
                dump("tribf", tribf, k_tribf)
                dump("onesb", onesb, k_onesb)
                raise _Stop()
            for nb in range(4):
                wt, kw_ = wmb[nb % 3]
                wb, kwb = wmbb[nb % 2]
                CP("dve", wb[:, 0:4, :], wt[:, 0:4, :], R=[kw_], W=[kwb])
                CP("act", wb[:, 4:8, :], wt[:, 4:8, :], R=[kw_], W=[kwb])
                if nb + 3 < 4:
                    DMA("sp", wt, wm_view[:, :, (nb + 3) * 512:(nb + 4) * 512], W=[kw_])
                pt, kp = PB[nb % 2]
                for kc in range(8):
                    MM(pt[0:2, :], scv[:, kc, :], wb[:, kc, :], start=(kc == 0), stop=(kc == 7), R=[k_scb, kwb], W=[kp])
                TT("dve", mrow[0:2, nb * 512:(nb + 1) * 512], pt[0:2, :], bmod[0:2, nb * 512:(nb + 1) * 512], ALU.add,
                   R=[kp, k_bmod], W=[k_mrow])
            if stop == "mod1":
                dump("mrow", mrow[0:2, :], k_mrow)
                raise _Stop()
            setup_small()
            DMA("sp", mctx[0:1, :], mrow[1:2, 0:2 * D], R=[k_mrow], W=[k_mctx])
            if stop == "mod2":
                dump("mctx", mctx[0:1, :], k_mctx)
                raise _Stop()
            pcol, kpc = PB[2]
            col_srcs = [(0, mrow, 0), (1, mrow, D), (4, mctx, 0), (5, mctx, D)]
            for (vi, src, off) in col_srcs:
                for kc in range(8):
                    ci = vi * 8 + kc
                    MM(pcol[:, 2 * ci: 2 * ci + 2], src[0:1, off + kc * 128: off + (kc + 1) * 128], onesf[0:1, 0:2],
                       R=[k_mrow, k_mctx, k_onesf], W=[kpc])
            CP("dve", mcols[:, 0:16], pcol[:, 0:32].rearrange("p (c two) -> p c two", two=2)[:, :, 0], R=[kpc], W=[k_mcols])
            CP("dve", mcols[:, 32:48], pcol[:, 64:96].rearrange("p (c two) -> p c two", two=2)[:, :, 0], R=[kpc], W=[k_mcols])
            if stop == "mod3":
                dump("mcols", mcols[:, 0:48], k_mcols)
                raise _Stop()
            for (sl, g0) in [(slice(8, 16), 0), (slice(40, 48), 0)]:
                TS("dve", mcols[:, sl], mcols[:, sl], 1.0, ALU.add, R=[k_mcols], W=[k_mcols])
                TT("dve", mcols[:, sl], mcols[:, sl], ncol[:, g0:g0 + 8], ALU.mult, R=[k_mcols, k_ncol], W=[k_mcols])
            SH_LAT, G_LAT, SH_FFN, G_FFN, SH_CTX, G_CTX = 0, 8, 16, 24, 32, 40
            if stop == "mod":
                dump("mcols", mcols[:, 0:48], k_mcols)
                dump("gate_mix", gate_mix, k_gmix)
                dump("gate_ffn", gate_ffn, k_gffn)
                raise _Stop()
            if debug and "mcols" in dbg_out:
                DMA("sp", dbg_out["mcols"], mcols[:, 0:48], R=[k_mcols], W=["dbg_mcols"])
            if debug and "gate_mix" in dbg_out:
                DMA("sp", dbg_out["gate_mix"], gate_mix, R=[k_gmix], W=["dbg_gm"])
            S.barrier()
            A.off = persist_mark

            if stop == "A0":
                dump("win0", win[:, 0, 0:512], K_WIN[0])
                dump("win7", win[:, 7, 2352:2864], K_WIN[7])
                dump("cwd", cwd[:, 2, :, :], k_cwd)
                dump("aneg", aneg, k_aneg)
                dump("wgs", wgs[0:16, :], k_wgs)
                raise _Stop()
            class FS:
                pass

            def new_fs(tag):
                f = FS()
                f.tag = tag
                f.qT, f.k_qT = A.alloc([2, 256], BF16, "qT" + tag)
                f.kT, f.k_kT = A.alloc([2, 256], BF16, "kT" + tag)
                f.ktm, f.k_ktm = A.alloc([2, 256], BF16, "ktm" + tag)
                f.v, f.k_v = A.alloc([2, 512], BF16, "v" + tag)
                f.g, f.k_g = A.alloc([2, 2, 256], BF16, "g" + tag)
                f.x, f.k_x = A.alloc([2, 512], BF16, "x" + tag)
                f.Btm, f.k_Btm = A.alloc([2, 128], BF16, "Btm" + tag)
                f.BTm, f.k_BTm = A.alloc([2, 256], BF16, "BTm" + tag)
                f.CT, f.k_CT = A.alloc([256], BF16, "CT" + tag)
                f.dt, f.k_dt = A.alloc([2, 16], F32, "dt" + tag)
                f.a, f.k_a = A.alloc([2, 16], F32, "a" + tag)
                f.r, f.k_r = A.alloc([2, 512], BF16, "r" + tag)
                f.z, f.k_z = A.alloc([2, 512], BF16, "z" + tag)
                return f

            fsets = [new_fs("A"), new_fs("B")]

            xt = [A.alloc([D], F32, "xt%d" % i) for i in range(3)]
            xsb = [A.alloc([D], BF16, "xs%d" % i) for i in range(3)]
            stat, k_stat = A.alloc([3, 4], F32, "stat")
            hTs = [A.alloc([8, 384], BF16, "hT%d" % i) for i in range(2)]
            hT, k_hT = hTs[0]
            hT_rr = [0]
            pcl, k_pcl = A.alloc([6, 6, 66], BF16, "pcl")
            xa, k_xa = A.alloc([6, 256], BF16, "xa")
            lrT, k_lrT = A.alloc([2, 256], BF16, "lrT")
            gwe, k_gwe = A.alloc([512], F32, "gwe")
            gwl, k_gwl = gwe, k_gwe
            sile, k_sile = A.alloc([512], F32, "sile")
            cbn, k_cbn = A.alloc([6], F32, "cbn")
            TS("dve", cbn, cb, -1.0, ALU.mult, R=[k_cb], W=[k_cbn])
            dtw, k_dtw = A.alloc([16], F32, "dtw")
            MSET("pool", pcl, 0.0, W=[k_pcl])

            PT_BF = [(psum[4][:, :].bitcast(BF16), "pb4"), (psum[5][:, :].bitcast(BF16), "pb5")]

            evac_rr = [0]

            def evac_eng():
                evac_rr[0] += 1
                return "dve" if evac_rr[0] % 3 == 0 else "act"

            def feats_norm(hsel, src, tok0, is_ctx, nseq):
                hT, k_hT = hTs[hsel]
                ntok_ext = 256 if is_ctx else 384
                ntile = 2 if is_ctx else 3
                for ti in range(ntile):
                    xtt, kx = xt[ti]
                    if ti < 2:
                        DMA("sp", xtt, src[tok0 + ti * 128: tok0 + (ti + 1) * 128, :], W=[kx])
                    else:
                        lo = tok0 - 64 if tok0 >= 64 else 0
                        hi = tok0 + 256 if tok0 + 320 <= nseq else 0
                        DMA("sp", xtt[0:64, :], src[lo:lo + 64, :], W=[kx])
                        DMA("sp", xtt[64:128, :], src[hi:hi + 64, :], W=[kx])
                    ss = stat[:, ti, 0:1]
                    kst_ = k_stat + ":%d" % ti
                    ACT(xsb[ti][0], xtt, AF.Square, accum=ss, R=[kx], W=[xsb[ti][1], kst_])
                    ACT(stat[:, ti, 1:2], ss, AF.Ln, scale=1.0 / D, bias=1e-6, R=[kst_], W=[kst_])
                    ACT(stat[:, ti, 3:4], stat[:, ti, 1:2], AF.Exp, scale=-0.5, R=[kst_], W=[kst_])
                    xs_, kxs = xsb[ti]
                    TS("dve", xs_, xtt, stat[:, ti, 3:4], ALU.mult, R=[kx, kst_], W=[kxs])
                gofs = G_CTX if is_ctx else G_LAT
                sofs = SH_CTX if is_ctx else SH_LAT
                pt, kpt = PT_BF[0]
                for kp in range(4):
                    with atomic():
                        for j in range(2):
                            kc = kp * 2 + j
                            for ti in range(ntile):
                                TR(pt[:, j * 384 + ti * 128: j * 384 + (ti + 1) * 128], xsb[ti][0][:, kc * 128:(kc + 1) * 128],
                                   R=[xsb[ti][1], k_ident], W=[kpt])
                        for j in range(2):
                            kc = kp * 2 + j
                            TS("dve", hT[:, kc, 0:ntok_ext], pt[:, j * 384: j * 384 + ntok_ext],
                               mcols[:, gofs + kc: gofs + kc + 1], ALU.mult, mcols[:, sofs + kc: sofs + kc + 1], ALU.add,
                               R=[kpt, k_mcols], W=[k_hT + ":%d" % kc])

            prr = [0]

            def nextp():
                prr[0] += 1
                return PB[prr[0] % 2]

            def feats_proj(fs, hsel, src, tok0, is_ctx, nseq, mode="full"):
                hT, k_hT = hTs[hsel]
                ntok_ext = 256 if is_ctx else 384
                K_HT = [k_hT + ":%d" % kc for kc in range(8)]
                full = (mode == "full")
                dirs = [1] if mode == "sB" else [0, 1]
                nc6 = 6 if full else 5
                fm = []
                if full:
                    fm += [("q", 0, 128, 0), ("q", 128, 128, 1), ("k", 256, 128, 0), ("k", 384, 128, 1)]
                fm += [("lr", 1536 + 16 * d_, 16, d_) for d_ in dirs]
                for (nm, c0, M, idx) in fm:
                    pt, kpt = nextp()
                    with atomic():
                        for kc in range(8):
                            MM(pt[0:M, 0:256], win[:, kc, c0:c0 + M], hT[:, kc, 0:256], start=(kc == 0), stop=(kc == 7),
                               R=[K_WIN[kc], K_HT[kc]], W=[kpt])
                        if nm == "q":
                            ACT(fs.qT[:, idx, :], pt[:, 0:256], AF.Copy, scale=0.125, R=[kpt], W=[fs.k_qT])
                        elif nm == "k":
                            CP("act", fs.kT[:, idx, :], pt[:, 0:256], R=[kpt], W=[fs.k_kT])
                        else:
                            CP("act", lrT[0:16, idx, :], pt[0:16, 0:256], R=[kpt], W=[k_lrT])
                pc = pcx if is_ctx else pcl
                kpc_ = k_pcx if is_ctx else k_pcl
                for c6 in range(nc6):
                    pt, kpt = nextp()
                    with atomic():
                        for kc in range(8):
                            MM(pt[:, 0:ntok_ext], win[:, kc, 2080 + c6 * 128: 2080 + (c6 + 1) * 128], hT[:, kc, 0:ntok_ext],
                               start=(kc == 0), stop=(kc == 7), R=[K_WIN[kc], K_HT[kc]], W=[kpt])
                        eng = evac_eng()
                        if is_ctx:
                            CP(eng, pc[:, c6, 1, 1:257], pt[:, 0:256], R=[kpt], W=[kpc_])
                        else:
                            CP(eng, pc[:, c6, 1:5, 1:65], pt[:, 0:256].rearrange("p (r c) -> p r c", r=4), R=[kpt], W=[kpc_])
                            eng = evac_eng()
                            if tok0 >= 64 and tok0 + 320 <= nseq:
                                CP(eng, pc[:, c6, 0:6:5, 1:65], pt[:, 256:384].rearrange("p (r c) -> p r c", r=2), R=[kpt], W=[kpc_])
                            else:
                                if tok0 >= 64:
                                    CP(eng, pc[:, c6, 0, 1:65], pt[:, 256:320], R=[kpt], W=[kpc_])
                                else:
                                    MSET("pool", pc[:, c6, 0, 1:65], 0.0, W=[kpc_])
                                if tok0 + 320 <= nseq:
                                    CP(eng, pc[:, c6, 5, 1:65], pt[:, 320:384], R=[kpt], W=[kpc_])
                                else:
                                    MSET("pool", pc[:, c6, 5, 1:65], 0.0, W=[kpc_])
                for tpass in range(4):
                    ti = tpass % 2
                    if tpass < 2:
                        tml = [("v", 512, 512), ("k", 256, 256), ("dt", 2848, 16)]
                    else:
                        tml = [("r", 1024, 512), ("z", 1568, 512)] if full else []
                    for (nm, c0, N) in tml:
                        pt, kpt = nextp()
                        with atomic():
                            for kc in range(8):
                                MM(pt[:, 0:N], hT[:, kc, ti * 128:(ti + 1) * 128], win[:, kc, c0:c0 + N], start=(kc == 0), stop=(kc == 7),
                                   R=[K_WIN[kc], K_HT[kc]], W=[kpt])
                            if nm == "v":
                                CP("act", fs.v[:, ti, :], pt[:, :], R=[kpt], W=[fs.k_v])
                            elif nm == "r":
                                ACT(fs.r[:, ti, :], pt[:, :], AF.Silu, R=[kpt], W=[fs.k_r])
                            elif nm == "z":
                                ACT(fs.z[:, ti, :], pt[:, :], AF.Silu, R=[kpt], W=[fs.k_z])
                            elif nm == "k":
                                CP("act", fs.ktm[:, ti, :], pt[:, 0:256], R=[kpt], W=[fs.k_ktm])
                            else:
                                TT("dve", dtw, pt[:, 0:16], dtb, ALU.add, R=[kpt, k_dtb], W=[k_dtw])
                        if nm == "dt":
                            ACT(dtw, dtw, AF.Exp, R=[k_dtw], W=[k_dtw])
                            ACT(fs.dt[:, ti, :], dtw, AF.Ln, bias=1.0, R=[k_dtw], W=[fs.k_dt])
                            TT("dve", fs.a[:, ti, :], fs.dt[:, ti, :], aneg, ALU.mult, R=[fs.k_dt, k_aneg], W=[fs.k_a])
                    if tpass >= 2:
                        continue
                    pt, kpt = nextp()
                    gsl = slice(dirs[0] * 256, 512)
                    with atomic():
                        for d in dirs:
                            MM(pt[:, d * 256:(d + 1) * 256], lrT[0:16, d, ti * 128:(ti + 1) * 128], wgs[0:16, d * 256:(d + 1) * 256],
                               start=True, stop=False, R=[k_lrT, k_wgs], W=[kpt])
                            MM(pt[:, d * 256:(d + 1) * 256], onesb[0:1, :], bgs[0:1, d * 256:(d + 1) * 256],
                               start=False, stop=True, R=[k_onesb, k_bgs], W=[kpt])
                        ACT(gwe[:, gsl], pt[:, gsl], AF.Exp, scale=-1.0, R=[kpt], W=[k_gwe])
                    ACT(gwl[:, gsl], gwe[:, gsl], AF.Ln, bias=1.0, R=[k_gwe], W=[k_gwl])
                    if os.environ.get("X_G"):
                        TS("pool", fs.g[:, dirs[0]:2, ti, :], gwl[:, gsl].rearrange("p (d f) -> p d f", f=256), -1.0 / 16.0, ALU.mult,
                           R=[k_gwl], W=[fs.k_g])
                    else:
                        ACT(fs.g[:, dirs[0]:2, ti, :], gwl[:, gsl].rearrange("p (d f) -> p d f", f=256), AF.Copy, scale=-1.0 / 16.0,
                            R=[k_gwl], W=[fs.k_g])
                taps = [(0, dc) for dc in (-1, 0, 1)] if is_ctx else [(dr, dc) for dr in (-1, 0, 1) for dc in (-1, 0, 1)]
                for c6 in range(nc6):
                    pt, kpt = nextp()
                    with atomic():
                        for i, (dr, dc) in enumerate(taps):
                            tap = (dr + 1) * 3 + (dc + 1)
                            if is_ctx:
                                rhs = pc[:, c6, 1, 1 + dc: 257 + dc]
                            else:
                                rhs = pc[:, c6, 1 + dr: 5 + dr, 1 + dc: 65 + dc]
                            MM(pt[:, 0:256], cwd[:, c6, tap, :], rhs, start=(i == 0), stop=(i == len(taps) - 1), R=[k_cwd, kpc_], W=[kpt])
                        if c6 == 5:
                            cdst, kcd = fs.CT, fs.k_CT
                        else:
                            cdst, kcd = xa[:, c6, :], k_xa + ":%d" % c6
                        ACT(cdst, pt[:, 0:256], AF.Silu, bias=cb[:, c6:c6 + 1], R=[kpt, k_cb], W=[kcd])
                if full:
                    for g in range(2):
                        TS("dve", fs.BTm[:, g, :], xa[:, 4, :], gmask[:, g:g + 1], ALU.mult, R=[k_xa + ":4", k_gmask], W=[fs.k_BTm])
                pt, kpt = PT_BF[0]
                for ti in range(2):
                    with atomic():
                        for c6 in range(5):
                            TR(pt[:, c6 * 128:(c6 + 1) * 128], xa[:, c6, ti * 128:(ti + 1) * 128], R=[k_xa + ":%d" % c6, k_ident], W=[kpt])
                        CP("dve", fs.x[:, ti, :], pt[:, 0:512], R=[kpt], W=[fs.k_x])
                        CP("dve", fs.Btm[:, ti, :], pt[:, 512:640], R=[kpt], W=[fs.k_Btm])

            Sg = [A.alloc([2, 256], F32, "Sg%d" % d) for d in range(2)]
            Sgb = [A.alloc([2, 256], BF16, "Sgb%d" % d) for d in range(2)]
            Ss = [A.alloc([512], F32, "Ss%d" % d) for d in range(2)]
            Ssb = [A.alloc([512], BF16, "Ssb%d" % d) for d in range(2)]
            for d in range(2):
                MSET("pool", Sg[d][0], 0.0, W=[Sg[d][1]])
                MSET("pool", Sgb[d][0], 0.0, W=[Sgb[d][1]])
                MSET("pool", Ss[d][0], 0.0, W=[Ss[d][1]])
                MSET("pool", Ssb[d][0], 0.0, W=[Ssb[d][1]])
            eq, k_eq = A.alloc([2, 256], F32, "eq")
            ek, k_ek = A.alloc([2, 256], F32, "ek")
            qs, k_qs = A.alloc([2, 256], BF16, "qs")
            ks = [A.alloc([2, 256], BF16, "ks%d" % h) for h in range(2)]
            ekt, k_ekt = A.alloc([2, 256], F32, "ekt")
            kst, k_kst = A.alloc([2, 256], BF16, "kst")
            attm, k_attm = A.alloc([4, 128], BF16, "attm")
            tmpg, k_tmpg = A.alloc([2, 256], F32, "tmpg")
            _mark = A.off
            pcx, k_pcx = A.alloc([6, 3, 258], BF16, "pcx")
            MSET("dve", pcx, 0.0, W=[k_pcx])
            A.off = _mark
            yo = [A.alloc([1024], F32, "yo%d" % i) for i in range(2)]
            sq, k_sq = A.alloc([512], F32, "sq")
            t1, k_t1 = A.alloc([512], F32, "t1")
            yfr = yo
            e3, k_e3 = A.alloc([3, 8], F32, "e3")
            wj, k_wj = A.alloc([8], F32, "wj")
            p1s, k_p1s = A.alloc([8], F32, "p1s")
            cspb, k_cspb = A.alloc([48], BF16, "cspb")
            uall, k_uall = A.alloc([128], BF16, "uall")
            vall, k_vall = A.alloc([1024], BF16, "vall")
            LT, k_LT = A.alloc([1024], BF16, "LT")
            scT, k_scT = A.alloc([8, 128], BF16, "scT")
            xdt, k_xdt = A.alloc([8, 64], BF16, "xdt")
            xw, k_xw = A.alloc([8, 64], BF16, "xw")
            tmpi, k_tmpi = A.alloc([512], F32, "tmpi")
            tmps, k_tmps = A.alloc([512], F32, "tmps")
            MSET("pool", cspb, 0.0, W=[k_cspb])
            DMA("sp", uall[0:48, :], uallt_d, W=[k_uall])
            DMA("sp", vall[0:48, :], vallt_d, W=[k_vall])
            bd16, k_bd16 = A.alloc([1024], BF16, "bd16")
            DMA("sp", bd16[0:16, :], vallt_d[32:48, :], W=[k_bd16])

            mst, k_mst = A.alloc([16], F32, "mst")
            ob, k_ob = xsb[2]
            oT, k_oT = xsb[1][0].rearrange("p (k t) -> p k t", k=8), xsb[1][1]
            xo, k_xo = xt[2]
            tmo, k_tmo = xt[1]
            pst, k_pst = A.alloc([4], F32, "pst")
            print("arena phase A bytes:", A.off)

            PG_A, PG_B = PB[2], PB[3]
            P5 = psum[5]
            P5BF = psum[5][:, :].bitcast(BF16)
            PBIG = psum2

            def gla_prep(fs, d, full):
                pt, kpt = PG_A
                with atomic():
                    for ti in range(2):
                        for p in range(2):
                            MM(pt[:, p * 256 + ti * 128: p * 256 + (ti + 1) * 128], fs.g[:, d, ti, p * 128:(p + 1) * 128], tribf[:, d, :],
                               R=[fs.k_g, k_tribf], W=[kpt])
                    ptv = pt[:, :].rearrange("p (a b) -> p a b", a=2)
                    ACT(eq, ptv, AF.Exp, R=[kpt], W=[k_eq])
                    if full:
                        ACT(ek, ptv, AF.Exp, scale=-1.0, R=[kpt], W=[k_ek])
                if full:
                    TT("dve", qs, fs.qT, eq, ALU.mult, R=[fs.k_qT, k_eq], W=[k_qs])
                    for hh in range(2):
                        STT(ks[hh][0], fs.kT, gmask[:, hh:hh + 1], ek, ALU.mult, ALU.mult, R=[fs.k_kT, k_gmask, k_ek], W=[ks[hh][1]])
                pt2, kpt2 = PG_B
                with atomic():
                    for ti in range(2):
                        MM(pt2[:, ti * 256:(ti + 1) * 256], tribf[:, d, :], fs.g[:, d, ti, :], R=[fs.k_g, k_tribf], W=[kpt2])
                    ACT(ekt, pt2[:, :].rearrange("p (a b) -> p a b", a=2), AF.Exp, scale=-1.0, R=[kpt2], W=[k_ekt])
                TT("pool", kst, fs.ktm, ekt, ALU.mult, R=[fs.k_ktm, k_ekt], W=[k_kst])

            def gla_tile(fs, d, full, ti):
                last = 127 if d == 0 else 0
                sg, ksg = Sg[d]
                sgb, ksgb = Sgb[d]
                yot, kyo = yo[ti]
                tsl = slice(ti * 128, (ti + 1) * 128)
                if full:
                    pa, kpa = PG_A
                    with atomic():
                        for h in range(4):
                            p, hh = h // 2, h % 2
                            MM(pa[:, h * 128:(h + 1) * 128], ks[hh][0][:, p, tsl], qs[:, p, tsl], R=[ks[hh][1], k_qs], W=[kpa])
                        TT("dve", attm, pa[:, :].rearrange("p (h i) -> p h i", h=4), bc(trif[:, d, :].unsqueeze(1), [128, 4, 128]), ALU.mult,
                           R=[kpa, k_trif], W=[k_attm])
                    po, kpo = PG_B
                    with atomic():
                        for p in range(2):
                            MM(po[:, p * 256:(p + 1) * 256], qs[:, p, tsl], sgb[:, p, :], start=True, stop=False, R=[k_qs, ksgb], W=[kpo])
                            for hh in range(2):
                                h = 2 * p + hh
                                MM(po[:, h * 128:(h + 1) * 128], attm[:, h, :], fs.v[:, ti, h * 128:(h + 1) * 128], start=False, stop=(hh == 1),
                                   R=[k_attm, fs.k_v], W=[kpo])
                        if d == 0:
                            CP("act", yot[:, 0:512], po[:, :], R=[kpo], W=[kyo + ":g"])
                        else:
                            TT("dve", yot[:, 0:512], po[:, :], yot[:, 0:512], ALU.add, R=[kpo, kyo + ":g"], W=[kyo + ":g"])
                psg, kpsg = PG_A
                with atomic():
                    for p in range(2):
                        MM(psg[:, p * 256:(p + 1) * 256], kst[:, ti, p * 128:(p + 1) * 128], fs.v[:, ti, p * 256:(p + 1) * 256],
                           R=[k_kst, fs.k_v], W=[kpsg])
                    TT("dve", tmpg, sg, psg[:, :].rearrange("p (a b) -> p a b", a=2), ALU.add, R=[ksg, kpsg], W=[k_tmpg])
                for p in range(2):
                    STT(sg[:, p, :], tmpg[:, p, :], eq[:, p, ti * 128 + last: ti * 128 + last + 1], bdg, ALU.mult, ALU.mult,
                        R=[k_tmpg, k_eq, k_bdg], W=[ksg])
                CP("act", sgb, sg, R=[ksg], W=[ksgb])

            def ssd_tile(fs, d, full, ti):
                ssd_, kss = Ss[d]
                ssb, kssb = Ssb[d]
                yot, kyo = yo[ti]
                tsl = slice(ti * 128, (ti + 1) * 128)
                av = fs.a[:, ti, d * 8:(d + 1) * 8]
                kp1 = "pb5:a"
                with atomic():
                    MM(P5[:, 0:8], trif[:, d, :], av, R=[k_trif, fs.k_a], W=[kp1])
                    MM(P5[:, 8:16], uppf[:, d, :], av, R=[k_uppf, fs.k_a], W=[kp1])
                    MM(P5[:, 16:24], onesf, av, R=[k_onesf, fs.k_a], W=[kp1])
                    ACT(e3, P5[:, 0:24].rearrange("p (a b) -> p a b", a=3), AF.Exp, R=[kp1], W=[k_e3])
                    if full:
                        ACT(p1s, P5[:, 0:8], AF.Copy, R=[kp1], W=[k_p1s])
                if full:
                    CP("pool", cspb[:, 0:8], p1s, R=[k_p1s], W=[k_cspb])
                    TT("pool", cspb[:, 8:16], p1s, cspb[:, 0:8], ALU.subtract, R=[k_p1s, k_cspb], W=[k_cspb])
                    TS("pool", cspb[:, 32:48], cspb[:, 0:16], -1.0, ALU.mult, R=[k_cspb], W=[k_cspb])
                TT("pool", wj, e3[:, 1, :], fs.dt[:, ti, d * 8:(d + 1) * 8], ALU.mult, R=[k_e3, fs.k_dt], W=[k_wj])
                if full:
                    ptb = P5BF[:, 640:768]
                    kptb = "pb5:t"
                    with atomic():
                        TR(ptb[0:48, :], cspb, R=[k_cspb, k_ident], W=[kptb])
                        TT("dve", vall[0:16, :].rearrange("p (h i) -> p h i", h=8), bc(ptb[0:16, :].unsqueeze(1), [16, 8, 128]),
                           bd16[0:16, :].rearrange("p (h i) -> p h i", h=8), ALU.mult, R=[kptb, k_bd16], W=[k_vall])
                        CP("act", uall[32:48, :], ptb[32:48, :], R=[kptb], W=[k_uall])
                    kb0, kb1 = "pbig:0", "pbig:1"
                    with atomic():
                        for half in range(2):
                            MM(PBIG[:, half * 512:(half + 1) * 512], uall[0:48, :], vall[0:48, half * 512:(half + 1) * 512], start=True, stop=False,
                               R=[k_uall, k_vall], W=[kb0 if half == 0 else kb1])
                            MM(PBIG[:, half * 512:(half + 1) * 512], ident, negm[:, d, :], start=False, stop=True, R=[k_ident, k_negm],
                               W=[kb0 if half == 0 else kb1])
                        for half in range(2):
                            ACT(LT[:, half * 512:(half + 1) * 512], PBIG[:, half * 512:(half + 1) * 512], AF.Exp,
                                R=[kb0 if half == 0 else kb1], W=[k_LT])
                    kpG = "pb5:g"
                    with atomic():
                        for g in range(2):
                            MM(P5[:, 32 + g * 128: 32 + (g + 1) * 128], fs.BTm[:, g, tsl], fs.CT[:, tsl], R=[fs.k_BTm, fs.k_CT], W=[kpG])
                        TT("dve", scT.rearrange("p (g h) i -> p g h i", g=2),
                           bc(P5[:, 32:288].rearrange("p (g i) -> p g i", g=2).unsqueeze(2), [128, 2, 4, 128]),
                           LT.rearrange("p (g h i) -> p g h i", g=2, h=4), ALU.mult, R=[kpG, k_LT], W=[k_scT])
                    TT("pool", xdt, fs.x[:, ti, :].rearrange("p (h q) -> p h q", h=8),
                       bc(fs.dt[:, ti, d * 8:(d + 1) * 8].unsqueeze(2), [128, 8, 64]), ALU.mult, R=[fs.k_x, fs.k_dt], W=[k_xdt])
                    with atomic():
                        for h in range(8):
                            MM(PBIG[:, h * 64:(h + 1) * 64], scT[:, h, :], xdt[:, h, :], R=[k_scT, k_xdt], W=[kb0])
                        MM(PBIG[:, 512:1024], fs.CT[:, tsl], ssb, R=[fs.k_CT, kssb], W=[kb1])
                        TT("dve", tmpi.rearrange("p (h q) -> p h q", h=8), PBIG[:, 512:1024].rearrange("p (h q) -> p h q", h=8),
                           bc(e3[:, 0, :].unsqueeze(2), [128, 8, 64]), ALU.mult, R=[kb1, k_e3], W=[k_tmpi])
                        if d == 1:
                            TT("dve", tmpi, tmpi, yot[:, 512:1024], ALU.add, R=[k_tmpi, kyo + ":s"], W=[k_tmpi])
                        TT("dve", yot[:, 512:1024], PBIG[:, 0:512], tmpi, ALU.add, R=[kb0, k_tmpi], W=[kyo + ":s"])
                TT("pool", xw, fs.x[:, ti, :].rearrange("p (h q) -> p h q", h=8), bc(wj.unsqueeze(2), [128, 8, 64]), ALU.mult,
                   R=[fs.k_x, k_wj], W=[k_xw])
                TT("pool" if (full and d == 1) else "dve", tmps.rearrange("p (h q) -> p h q", h=8), ssd_.rearrange("p (h q) -> p h q", h=8),
                   bc(e3[:, 2, :].unsqueeze(2), [128, 8, 64]), ALU.mult, R=[kss, k_e3], W=[k_tmps])
                with atomic():
                    MM(PBIG[:, 0:512], fs.Btm[:, ti, :], xw.rearrange("p h q -> p (h q)"), R=[fs.k_Btm, k_xw], W=["pbig:0"])
                    TT("dve", ssd_, tmps, PBIG[:, 0:512], ALU.add, R=[k_tmps, "pbig:0"], W=[kss])
                if os.environ.get("X_SSB"):
                    TT("pool", ssb.rearrange("p (h q) -> p h q", h=8), ssd_.rearrange("p (h q) -> p h q", h=8),
                       bc(bds.unsqueeze(2), [128, 8, 64]), ALU.mult, R=[kss, k_bds], W=[kssb])
                else:
                    for g in range(2):
                        TS("dve", ssb[:, g * 256:(g + 1) * 256], ssd_[:, g * 256:(g + 1) * 256], gmask[:, g:g + 1], ALU.mult,
                           R=[kss, k_gmask], W=[kssb])

            obs = [xsb[2], xsb[0]]

            def merge_gla(fs, ti, T):
                yot, kyo = yo[ti]
                ob_, kob = obs[T % 2]
                for h in range(4):
                    ACT(sq.bitcast(BF16)[:, h * 128:(h + 1) * 128], yot[:, h * 128:(h + 1) * 128], AF.Square, accum=mst[:, h:h + 1],
                        R=[kyo + ":g"], W=[k_sq + ":g", k_mst])
                ACT(mst[:, 0:4], mst[:, 0:4], AF.Ln, scale=1.0 / 128, bias=1e-6, R=[k_mst], W=[k_mst])
                ACT(mst[:, 0:4], mst[:, 0:4], AF.Exp, scale=-0.5, R=[k_mst], W=[k_mst])
                for h in range(4):
                    STT(ob_[:, h * 128:(h + 1) * 128], yot[:, h * 128:(h + 1) * 128], mst[:, h:h + 1], fs.r[:, ti, h * 128:(h + 1) * 128],
                        ALU.mult, ALU.mult, R=[kyo + ":g", k_mst, fs.k_r], W=[kob + ":g"])

            def merge_ssd(fs, ti, T):
                yot, kyo = yo[ti]
                ob_, kob = obs[T % 2]
                TT("pool", t2, fs.x[:, ti, :], dskb, ALU.mult, R=[fs.k_x, k_dskb], W=[k_t2])
                TT("dve", t2, t2, yot[:, 512:1024], ALU.add, R=[k_t2, kyo + ":s"], W=[k_t2])
                TT("pool", t2, t2, fs.z[:, ti, :], ALU.mult, R=[k_t2, fs.k_z], W=[k_t2])
                kmb = k_mst + "b"
                for g in range(2):
                    ACT(sq.bitcast(BF16)[:, 512 + g * 256: 512 + (g + 1) * 256], t2[:, g * 256:(g + 1) * 256], AF.Square, accum=mst[:, 4 + g:5 + g],
                        R=[k_t2], W=[k_sq + ":s", kmb])
                ACT(mst[:, 4:6], mst[:, 4:6], AF.Ln, scale=1.0 / 256, bias=1e-6, R=[kmb], W=[kmb])
                ACT(mst[:, 4:6], mst[:, 4:6], AF.Exp, scale=-0.5, R=[kmb], W=[kmb])
                for g in range(2):
                    ACT(ob_[:, 512 + g * 256: 512 + (g + 1) * 256], t2[:, g * 256:(g + 1) * 256], AF.Copy, scale=mst[:, 4 + g:5 + g],
                        R=[k_t2, kmb], W=[kob + ":s"])

            def merge_out(T):
                ob_, kob = obs[T % 2]
                ptb, kptb = PT_BF[0]
                DMA("sp", xo, x_d[T * 128:(T + 1) * 128, :], W=[k_xo])
                with atomic():
                    for kc in range(8):
                        TR(ptb[:, kc * 128:(kc + 1) * 128], ob_[:, kc * 128:(kc + 1) * 128], R=[kob + ":g", kob + ":s", k_ident], W=[kptb])
                    TT("dve", oT, ptb[:, 0:1024].rearrange("p (k t) -> p k t", k=8), bc(mixg.unsqueeze(2), [128, 8, 128]), ALU.mult,
                       R=[kptb, k_mixg], W=[k_oT])
                with atomic():
                    for nb in range(2):
                        pop, kpop = PB[nb]
                        for kc in range(8):
                            MM(pop[:, :], oT[:, kc, :], wout[:, kc, nb * 512:(nb + 1) * 512], start=(kc == 0), stop=(kc == 7),
                               R=[k_oT, k_wout], W=[kpop])
                        ACT(t1.bitcast(BF16)[:, nb * 512:(nb + 1) * 512], pop[:, :], AF.Square, accum=pst[:, nb:nb + 1],
                            R=[kpop], W=[k_t1, k_pst])
                    TT("dve", pst[:, 2:3], pst[:, 0:1], pst[:, 1:2], ALU.add, R=[k_pst], W=[k_pst])
                    ACT(pst[:, 2:3], pst[:, 2:3], AF.Ln, scale=1.0 / D, bias=1e-6, R=[k_pst], W=[k_pst])
                    ACT(pst[:, 3:4], pst[:, 2:3], AF.Exp, scale=-0.5, R=[k_pst], W=[k_pst])
                    for nb in range(2):
                        pop, kpop = PB[nb]
                        STT(tmo[:, nb * 512:(nb + 1) * 512], pop[:, :], pst[:, 3:4], gate_mix[:, nb * 512:(nb + 1) * 512],
                            ALU.mult, ALU.mult, R=[kpop, k_pst, k_gmix], W=[k_tmo])
                TT("dve", tmo[:, 0:512], tmo[:, 0:512], xo[:, 0:512], ALU.add, R=[k_tmo, k_xo], W=[k_tmo])
                TT("dve", tmo[:, 512:1024], tmo[:, 512:1024], xo[:, 512:1024], ALU.add, R=[k_tmo, k_xo], W=[k_tmo])
                DMA("pool", xn_scr[T * 128:(T + 1) * 128, :], tmo, R=[k_tmo], W=["xn_scr:%d" % T])

            fs_i = [0]

            def next_fs():
                fs_i[0] += 1
                return fsets[fs_i[0] % 2]

            FS_NAMES = ["qT", "kT", "ktm", "v", "g", "x", "Btm", "BTm", "CT", "dt", "a", "r", "z"]
            fs_scr = {}
            for nm in FS_NAMES:
                ap0 = getattr(fsets[0], nm)
                shp = [8] + [int(v) for v in ap0.shape]
                fs_scr[nm] = nc.dram_tensor("fsscr_" + nm, shp, ap0.dtype, kind="Internal").ap()

            def spill_fs(f, blk):
                for nm in FS_NAMES:
                    DMA("pool", fs_scr[nm][blk], getattr(f, nm), R=[getattr(f, "k_" + nm)], W=["fsscr:%s:%d" % (nm, blk)])

            def reload_fs(f, blk):
                for nm in FS_NAMES:
                    DMA("sp", getattr(f, nm), fs_scr[nm][blk], R=["fsscr:%s:%d" % (nm, blk)], W=[getattr(f, "k_" + nm)])

            seq = [(ctx_d, 0, True, CTX, "s2", [(0, False), (1, False)], -1)]
            for blk in range(15, 7, -1):
                seq.append((x_d, blk * 256, False, SEQ, "sB", [(1, False)], blk))
            for blk in range(0, 8):
                seq.append((x_d, blk * 256, False, SEQ, "full", [(0, True)], blk))
            fsl = []
            for j in range(len(seq)):
                fsl.append(next_fs())
            feats_norm(0, *seq[0][0:4])
            with Chain() as cn:
                feats_norm(1, *seq[1][0:4])
            with Chain() as cf:
                feats_proj(fsl[0], 0, *seq[0][0:5])
            interleave([cn, cf])
            for i, item in enumerate(seq):
                fcur = fsl[i]
                chains = []
                if i + 2 < len(seq):
                    with Chain() as cn:
                        feats_norm((i + 2) % 2, *seq[i + 2][0:4])
                    chains.append(cn)
                if i + 1 < len(seq):
                    with Chain() as cf:
                        feats_proj(fsl[i + 1], (i + 1) % 2, *seq[i + 1][0:5])
                    chains.append(cf)
                with Chain() as cg:
                    for (d, full) in item[5]:
                        gla_prep(fcur, d, full)
                        for ti in ([0, 1] if d == 0 else [1, 0]):
                            gla_tile(fcur, d, full, ti)
                with Chain() as cs:
                    for (d, full) in item[5]:
                        for ti in ([0, 1] if d == 0 else [1, 0]):
                            ssd_tile(fcur, d, full, ti)
                chains += [cg, cs]
                interleave(chains)
                if item[4] == "full":
                    blk = item[6]
                    for ti in range(2):
                        T = blk * 2 + ti
                        DMA("pool", yf_scr[T * 128:(T + 1) * 128, :], yo[ti][0], R=[yo[ti][1] + ":g", yo[ti][1] + ":s"], W=["yf_scr:%d" % T])
                    spill_fs(fcur, blk)
            S.barrier()
            A2 = Arena(arena_t, win_end)
            A2.off = win_off
            wout, k_wout = A2.alloc([8, D], BF16, "wout")
            t2, k_t2 = A.alloc([512], F32, "t2")
            wmbL = [A2.alloc([8, 256], F32, "wmbL%d" % i) for i in range(2)]
            wmbbL = [A2.alloc([8, 256], BF16, "wmbbL%d" % i) for i in range(2)]
            rowp, k_rowp = A2.alloc([256], F32, "rowp")
            bmodp, k_bmodp = A2.alloc([256], F32, "bmodp")
            nrowp, k_nrowp = A2.alloc([256], F32, "nrowp")
            ncolp, k_ncolp = A2.alloc([16], F32, "ncolp")
            DMA("sp", ncolp, ncol_d, W=[k_ncolp])
            scvP = scbP.rearrange("p (k j) -> p k j", j=2)

            def late_mod(k):
                c0 = 2 * D + 256 * k
                vec, q = k // 4, k % 4
                wt, kw_ = wmbL[k % 2]
                wb, kwb = wmbbL[k % 2]
                DMA("sp", wt, wm_view[:, :, c0:c0 + 256], W=[kw_])
                DMA("sp", bmodp[0:2, :], bmod_d[:, c0:c0 + 256], W=[k_bmodp])
                CP("act", wb[:, 0:4, :], wt[:, 0:4, :], R=[kw_], W=[kwb])
                CP("act", wb[:, 4:8, :], wt[:, 4:8, :], R=[kw_], W=[kwb])
                pt, kp = PB[k % 2]
                with atomic():
                    for kc in range(8):
                        MM(pt[0:2, 0:256], scvP[:, kc, :], wb[:, kc, :], start=(kc == 0), stop=(kc == 7), R=[k_scbP, kwb], W=[kp])
                    TT("dve", rowp[0:2, :], pt[0:2, 0:256], bmodp[0:2, :], ALU.add, R=[kp, k_bmodp], W=[k_rowp])
                if vec in (0, 3):
                    gt, kg = (gate_mix, k_gmix) if vec == 0 else (gate_ffn, k_gffn)
                    n0 = (0 if vec == 0 else D) + q * 256
                    DMA("sp", nrowp[0:1, :], nrow_d[0:1, n0:n0 + 256], W=[k_nrowp])
                    TT("dve", rowp[0:1, :], rowp[0:1, :], nrowp[0:1, :], ALU.mult, R=[k_rowp, k_nrowp], W=[k_rowp])
                    with atomic():
                        MM(pt[:, 0:256], onesf[0:1, :], rowp[0:1, :], R=[k_onesf, k_rowp], W=[kp])
                        CP("act", gt[:, q * 256:(q + 1) * 256], pt[:, 0:256], R=[kp], W=[kg])
                else:
                    base = SH_FFN if vec == 1 else G_FFN
                    with atomic():
                        for j in range(2):
                            MM(pt[:, 2 * j:2 * j + 2], rowp[0:1, j * 128:(j + 1) * 128], onesf[0:1, 0:2], R=[k_rowp, k_onesf], W=[kp])
                        CP("dve", mcols[:, base + 2 * q: base + 2 * q + 2], pt[:, 0:4].rearrange("p (c two) -> p c two", two=2)[:, :, 0],
                           R=[kp], W=[k_mcols])
                    if vec == 2:
                        sl = slice(base + 2 * q, base + 2 * q + 2)
                        TS("dve", mcols[:, sl], mcols[:, sl], 1.0, ALU.add, R=[k_mcols], W=[k_mcols])
                        TT("dve", mcols[:, sl], mcols[:, sl], ncolp[:, 8 + 2 * q: 8 + 2 * q + 2], ALU.mult, R=[k_mcols, k_ncolp], W=[k_mcols])
            wout_view = wout_d.rearrange("(k p) n -> p k n", p=128)
            for kc in range(8):
                DMA("pool", wout[:, kc, :], wout_view[:, kc, :], W=[k_wout], max_dma_last_dim=4096)
            st1 = None
            st2 = None
            for blk in range(7, -1, -1):
                if blk == 7:
                    f = fsets[fs_i[0] % 2]
                else:
                    f = next_fs()
                    reload_fs(f, blk)
                for ti in (1, 0):
                    T = blk * 2 + ti
                    if not (blk == 7):
                        DMA("sp", yo[ti][0], yf_scr[T * 128:(T + 1) * 128, :], R=["yf_scr:%d" % T], W=[yo[ti][1] + ":g", yo[ti][1] + ":s"])
                    with Chain() as cg:
                        if ti == 1:
                            gla_prep(f, 1, True)
                        gla_tile(f, 1, True, ti)
                    with Chain() as cs:
                        ssd_tile(f, 1, True, ti)
                    chains = [cg, cs]
                    if T >= 14:
                        late_ks = [2 * (15 - T), 2 * (15 - T) + 1]
                    elif T >= 2:
                        late_ks = [4 + (13 - T)]
                    else:
                        late_ks = []
                    if late_ks:
                        with Chain() as cl:
                            for k_ in late_ks:
                                late_mod(k_)
                        chains.append(cl)
                    if st2 is not None:
                        with Chain() as c3:
                            merge_out(st2)
                        chains.append(c3)
                    if st1 is not None:
                        with Chain() as c1:
                            merge_gla(*st1)
                        with Chain() as c2:
                            merge_ssd(*st1)
                        chains += [c1, c2]
                    interleave(chains)
                    st2 = st1[2] if st1 is not None else None
                    st1 = (f, ti, T)
            chains = []
            if st2 is not None:
                with Chain() as c3:
                    merge_out(st2)
                chains.append(c3)
            with Chain() as c1:
                merge_gla(*st1)
            with Chain() as c2:
                merge_ssd(*st1)
            interleave(chains + [c1, c2])
            merge_out(st1[2])

            S.barrier()
            A.off = win_off

            wdn, k_wdn = A.alloc([22, D], BF16, "wdn")
            wdn_view = wdown_d.rearrange("(k p) n -> p k n", p=128)
            wdn_pending = list(range(22))
            h2Ts = [A.alloc([8, 1024], BF16, "h2T%d" % i) for i in range(2)]
            actT, k_actT = A.alloc([22, 1024], BF16, "actT")
            wgb = [A.alloc([8, 512], BF16, "wgb%d" % i) for i in range(2)]
            wub = [A.alloc([8, 512], BF16, "wub%d" % i) for i in range(2)]
            xn2 = [A.alloc([D], F32, "xn2_%d" % i) for i in range(2)]
            xn3 = [A.alloc([D], F32, "xn3_%d" % i) for i in range(2)]
            xs2 = [A.alloc([D], BF16, "xs2_%d" % i) for i in range(2)]
            junk2, k_junk2 = A.alloc([D], BF16, "junk2")
            st2, k_st2 = A.alloc([2, 4], F32, "st2")
            sgt = [A.alloc([512], F32, "sgt%d" % i) for i in range(2)]
            tf, k_tf = A.alloc([D], F32, "tf")
            fst, k_fst = A.alloc([2, 4], F32, "fst")
            print("arena phase F bytes:", A.off)
            wg_view = wgate_d.rearrange("(k p) n -> p k n", p=128)
            wu_view = wup_d.rearrange("(k p) n -> p k n", p=128)

            def ffn_prenorm(sb):
                h2T, k_h2T = h2Ts[sb % 2]
                xnb = (xn2 + xn3) if sb == 0 else xn2
                for tl in range(8):
                    T = sb * 8 + tl
                    j = tl % 2
                    xnt, kxn = xnb[tl % len(xnb)]
                    kst_ = k_st2 + ":%d" % j
                    DMA("sp", xnt, xn_scr[T * 128:(T + 1) * 128, :], R=["xn_scr:%d" % T], W=[kxn])
                    ACT(junk2, xnt, AF.Square, accum=st2[:, j, 0:1], R=[kxn], W=[k_junk2, kst_])
                    ACT(st2[:, j, 1:2], st2[:, j, 0:1], AF.Ln, scale=1.0 / D, bias=1e-6, R=[kst_], W=[kst_])
                    ACT(st2[:, j, 3:4], st2[:, j, 1:2], AF.Exp, scale=-0.5, R=[kst_], W=[kst_])
                    xs_, kxs = xs2[j]
                    ACT(xs_, xnt, AF.Copy, scale=st2[:, j, 3:4], R=[kxn, kst_], W=[kxs])
                    ptb, kptb = PT_BF[j]
                    with atomic():
                        for kc in range(8):
                            TR(ptb[:, kc * 128:(kc + 1) * 128], xs_[:, kc * 128:(kc + 1) * 128], R=[kxs, k_ident], W=[kptb])
                        for kc in range(8):
                            TS("dve", h2T[:, kc, tl * 128:(tl + 1) * 128], ptb[:, kc * 128:(kc + 1) * 128],
                               mcols[:, G_FFN + kc: G_FFN + kc + 1], ALU.mult, mcols[:, SH_FFN + kc: SH_FFN + kc + 1], ALU.add,
                               R=[kptb, k_mcols], W=[k_h2T + ":%d" % tl])

            w_issued = set()

            def ffn_wload(sb, fg):
                if (sb, fg) in w_issued:
                    return
                w_issued.add((sb, fg))
                nf = 4 if fg < 5 else 2
                wgt, kwg = wgb[fg % 2]
                wut, kwu = wub[fg % 2]
                c0 = fg * 512
                DMA("pool", wgt[:, :, 0:nf * 128], wg_view[:, :, c0:c0 + nf * 128], W=[kwg])
                DMA("pool", wut[:, :, 0:nf * 128], wu_view[:, :, c0:c0 + nf * 128], W=[kwu])

            def ffn_gateup(sb):
                h2T, k_h2T = h2Ts[sb % 2]
                K_H2 = [k_h2T + ":%d" % tl for tl in range(8)]
                for fg in range(6):
                    nf = 4 if fg < 5 else 2
                    wgt, kwg = wgb[fg % 2]
                    wut, kwu = wub[fg % 2]
                    ffn_wload(sb, fg)
                    if fg >= 1:
                        for _ in range(6):
                            if wdn_pending:
                                fc_ = wdn_pending.pop(0)
                                DMA("pool", wdn[:, fc_, :], wdn_view[:, fc_, :], W=[k_wdn + ":%d" % fc_])
                    for fi in range(nf):
                        fc = fg * 4 + fi
                        for nbk in range(2):
                            pg, kpg = PB[(2 * nbk) % 4]
                            pu, kpu = PB[(2 * nbk + 1) % 4]
                            rk = K_H2[nbk * 4:(nbk + 1) * 4]
                            for kc in range(8):
                                MM(pg[:, :], wgt[:, kc, fi * 128:(fi + 1) * 128], h2T[:, kc, nbk * 512:(nbk + 1) * 512], start=(kc == 0), stop=(kc == 7),
                                   R=[kwg] + rk, W=[kpg])
                            for kc in range(8):
                                MM(pu[:, :], wut[:, kc, fi * 128:(fi + 1) * 128], h2T[:, kc, nbk * 512:(nbk + 1) * 512], start=(kc == 0), stop=(kc == 7),
                                   R=[kwu] + rk, W=[kpu])
                            sg_, ksg_ = sgt[nbk]
                            ACT(sg_, pg[:, :], AF.Silu, R=[kpg], W=[ksg_])
                            TT("dve", actT[:, fc, nbk * 512:(nbk + 1) * 512], pu[:, :], sg_, ALU.mult, R=[kpu, ksg_], W=[k_actT + ":%d" % nbk])

            def ffn_down(sb):
                djunk = sgt[0][0].bitcast(BF16)
                kdj = sgt[0][1]
                for tl in range(8):
                    T = sb * 8 + tl
                    nbk = tl // 4
                    j = tl % 2
                    halves = []
                    for nb in range(2):
                        if j == 0:
                            halves.append((psum2[:, nb * 512:(nb + 1) * 512], "pbig:%d" % nb))
                        else:
                            halves.append((PB[nb][0][:, :], PB[nb][1]))
                    xnt, kxn = xn3[j]
                    DMA("sp", xnt, xn_scr[T * 128:(T + 1) * 128, :], R=["xn_scr:%d" % T], W=[kxn])
                    kf = k_fst + ":%d" % j
                    with atomic():
                        for nb in range(2):
                            pf, kpf = halves[nb]
                            for fc in range(22):
                                MM(pf, actT[:, fc, tl * 128:(tl + 1) * 128], wdn[:, fc, nb * 512:(nb + 1) * 512],
                                   start=(fc == 0), stop=(fc == 21), R=[k_actT + ":%d" % nbk, k_wdn + ":%d" % fc], W=[kpf])
                            ACT(djunk[:, nb * 512:(nb + 1) * 512], pf, AF.Square, accum=fst[:, j, nb:nb + 1], R=[kpf], W=[kdj, kf])
                        TT("dve", fst[:, j, 2:3], fst[:, j, 0:1], fst[:, j, 1:2], ALU.add, R=[kf], W=[kf])
                        ACT(fst[:, j, 2:3], fst[:, j, 2:3], AF.Ln, scale=1.0 / D, bias=1e-6, R=[kf], W=[kf])
                        ACT(fst[:, j, 3:4], fst[:, j, 2:3], AF.Exp, scale=-0.5, R=[kf], W=[kf])
                        for nb in range(2):
                            pf, kpf = halves[nb]
                            STT(tf[:, nb * 512:(nb + 1) * 512], pf, fst[:, j, 3:4], gate_ffn[:, nb * 512:(nb + 1) * 512],
                                ALU.mult, ALU.mult, R=[kpf, kf, k_gffn], W=[k_tf])
                    TT("dve", xnt[:, 0:512], tf[:, 0:512], xnt[:, 0:512], ALU.add, R=[k_tf, kxn], W=[kxn])
                    TT("pool", xnt[:, 512:1024], tf[:, 512:1024], xnt[:, 512:1024], ALU.add, R=[k_tf, kxn], W=[kxn])
                    tok_holder = _sched("dma", "act", (lambda e, T=T, xnt=xnt: e.dma_start(out=y_d[T * 128:(T + 1) * 128, :], in_=xnt)),
                                        [kxn], ["y:%d" % T])
                    if tok_holder is not None:
                        out_toks.append(tok_holder)

            ffn_prenorm(0)
            ffn_gateup(0)
            ffn_wload(1, 0)
            ffn_wload(1, 1)
            with Chain() as cd:
                ffn_down(0)
            with Chain() as cp:
                ffn_prenorm(1)
            interleave([cd, cp])
            ffn_gateup(1)
            ffn_down(1)
        except _Stop:
            pass
        S.barrier()
        S.emit(st)
    return nc


_PROG = {}


def _consts():
    bf = ml_dtypes.bfloat16
    t = np.arange(128)
    TF = (t[:, None] <= t[None, :]).astype(np.float32)
    TB = (t[:, None] >= t[None, :]).astype(np.float32)
    UF = (t[:, None] > t[None, :]).astype(np.float32)
    UB = (t[:, None] < t[None, :]).astype(np.float32)
    tri = np.concatenate([TF, TB], axis=1)
    upp = np.concatenate([UF, UB], axis=1)
    negF = np.tile((1.0 - TF) * NEG, (1, 4))
    negB = np.tile((1.0 - TB) * NEG, (1, 4))
    negm = np.concatenate([negF, negB], axis=1)
    bd = np.zeros((16, 8, 128), np.float32)
    for s in range(2):
        for h in range(8):
            bd[s * 8 + h, h, :] = 1.0
    bd = bd.reshape(16, 1024)
    uall = np.zeros((48, 128), np.float32)
    uall[0:16] = 1.0
    vall = np.zeros((48, 1024), np.float32)
    vall[32:48] = bd
    bdg = np.zeros((128, 256), np.float32)
    bdg[0:64, 0:128] = 1.0
    bdg[64:128, 128:256] = 1.0
    bds = np.zeros((128, 8), np.float32)
    bds[0:64, 0:4] = 1.0
    bds[64:128, 4:8] = 1.0
    gmask = np.zeros((128, 2), np.float32)
    gmask[0:64, 0] = 1.0
    gmask[64:128, 1] = 1.0
    return {
        "ident": np.eye(128, dtype=np.float32).astype(bf),
        "tri_bf": tri.astype(bf), "tri_f": tri, "upp_f": upp,
        "ones_f": np.ones((128, 128), np.float32),
        "negm": negm.astype(bf), "uall_t": uall.astype(bf), "vall_t": vall.astype(bf),
        "bdg": bdg, "bds": bds, "gmask": gmask,
    }


def _col(v):
    return np.ascontiguousarray(v.reshape(8, 128).T)


def make_in_maps(x, c, ctx, c_ctx, w_mod, b_mod, norm_mix_pre, norm_mix_post, norm_ffn_pre, norm_ffn_post,
                 w_in, conv_w, conv_b, gla_wg_f, gla_bg_f, gla_wg_b, gla_bg_b, gla_norm,
                 a_log_f, a_log_b, dt_bias_f, dt_bias_b, d_skip, ssd_norm, w_out, w_gate, w_up, w_down):
    f32 = np.float32
    L = 0
    consts = _consts()
    w_in0 = np.asarray(w_in[L], f32)
    perm = np.arange(DIN)
    perm[1536:1552], perm[1552:1568] = np.arange(1552, 1568), np.arange(1536, 1552)
    perm[2848:2856], perm[2856:2864] = np.arange(2856, 2864), np.arange(2848, 2856)
    w_in1 = np.ascontiguousarray(w_in0[:, perm])
    shared = dict(consts)
    shared.update({
        "w_mod": np.asarray(w_mod[L], f32),
        "b_mod2": np.ascontiguousarray(np.broadcast_to(np.asarray(b_mod[L], f32)[None, :], (2, 6 * D))),
        "ncol": np.concatenate([_col(np.asarray(norm_mix_pre[L], f32)), _col(np.asarray(norm_ffn_pre[L], f32))], axis=1),
        "nrow": np.concatenate([np.asarray(norm_mix_post[L], f32), np.asarray(norm_ffn_post[L], f32)])[None, :],
        "cb": np.ascontiguousarray(np.asarray(conv_b[L], f32).reshape(6, 128).T),
        "mixg": np.concatenate([np.broadcast_to(np.asarray(gla_norm[L], f32)[:, None], (128, 4)),
                                np.asarray(ssd_norm[L], f32).reshape(4, 128).T], axis=1).astype(f32),
        "dsk": np.ascontiguousarray(np.broadcast_to(np.repeat(np.asarray(d_skip[L], f32), 64)[None, :], (128, 512))),
        "w_out": np.asarray(w_out[L], f32), "w_gate": np.asarray(w_gate[L], f32),
        "w_up": np.asarray(w_up[L], f32), "w_down": np.asarray(w_down[L], f32),
    })
    per_th = []
    for th in range(2):
        cw = np.asarray(conv_w[L], f32)
        if th == 1:
            cw = cw[::-1, ::-1, :]
        cwl = np.ascontiguousarray(cw.reshape(9, 6, 128).transpose(2, 1, 0).reshape(128, 54))
        wgf, wgb_ = np.asarray(gla_wg_f[L], f32), np.asarray(gla_wg_b[L], f32)
        bgf, bgb = np.asarray(gla_bg_f[L], f32), np.asarray(gla_bg_b[L], f32)
        alf, alb = np.asarray(a_log_f[L], f32), np.asarray(a_log_b[L], f32)
        dbf, dbb = np.asarray(dt_bias_f[L], f32), np.asarray(dt_bias_b[L], f32)
        if th == 1:
            wgf, wgb_, bgf, bgb = wgb_, wgf, bgb, bgf
            alf, alb, dbf, dbb = alb, alf, dbb, dbf
        per_th.append({
            "w_in": w_in0 if th == 0 else w_in1,
            "cw": cwl,
            "wg": np.ascontiguousarray(np.concatenate([wgf, wgb_], axis=1)),
            "bg": np.concatenate([bgf, bgb])[None, :].astype(f32),
            "alog": np.ascontiguousarray(np.broadcast_to(np.concatenate([alf, alb])[None, :], (128, 16))),
            "dtb": np.ascontiguousarray(np.broadcast_to(np.concatenate([dbf, dbb])[None, :], (128, 16))),
        })
    in_maps = []
    x = np.asarray(x, f32)
    ctx = np.asarray(ctx, f32)
    c = np.asarray(c, f32)
    c_ctx = np.asarray(c_ctx, f32)
    for core in range(8):
        b, th = core // 2, core % 2
        m = dict(shared)
        m.update(per_th[th])
        xb, cxb = x[b], ctx[b]
        if th == 1:
            xb, cxb = xb[::-1], cxb[::-1]
        m["x"] = np.ascontiguousarray(xb)
        m["ctx"] = np.ascontiguousarray(cxb)
        cc = np.stack([_col(c[b]), _col(c_ctx)], axis=2).reshape(128, 16)
        m["cc"] = np.ascontiguousarray(cc)
        in_maps.append(m)
    return in_maps


def kernel(**inputs):
    if "nc" not in _PROG:
        _PROG["nc"] = build_program()
    nc = _PROG["nc"]
    in_maps = make_in_maps(**inputs)
    res = run_bass_kernel_spmd(nc, in_maps, core_ids=list(range(8)))
    out = np.empty((4, SEQ, D), np.float32)
    for core in range(8):
        b, th = core // 2, core % 2
        y = np.asarray(res.results[core]["y"], np.float32)
        if th == 0:
            out[b, 0:OWN] = y
        else:
            out[b, OWN:SEQ] = y[::-1]
    return out
```

```python
import os
from contextlib import ExitStack

import ml_dtypes
import numpy as np

import concourse.bass as bass
import concourse.mybir as mybir
from concourse.bass_utils import run_bass_kernel_spmd

F32 = mybir.dt.float32
BF16 = mybir.dt.bfloat16
AF = mybir.ActivationFunctionType
ALU = mybir.AluOpType
AX = mybir.AxisListType

D = 1024
SEQ = 4096
CTX = 256
OWN = 2048
DIN = 2864
DFF = 2816
NEG = -30000.0


class Sched:
    def __init__(self, nc, n_dma_sems=32):
        self.nc = nc
        self.engs = ["pe", "act", "dve", "pool", "sp"]
        self.lists = {e: [] for e in self.engs}
        self.count = {e: 0 for e in self.engs}
        self.waited = {e: {} for e in self.engs}
        self.last_w = {}
        self.readers = {}
        self.n_dma_sems = n_dma_sems
        self.dma_cnt = [0] * n_dma_sems
        self.dma_rr = 0
        self.sems = {}

    def _need(self, eng, best):
        out = []
        for sk, val in best.items():
            if sk == eng and eng == "pe":
                continue
            if self.waited[eng].get(sk, 0) >= val:
                continue
            self.waited[eng][sk] = val
            out.append((sk, val))
        return out

    def _deps(self, reads, writes):
        best = {}

        def add(tok):
            if best.get(tok[0], 0) < tok[1]:
                best[tok[0]] = tok[1]

        for k in reads:
            if k in self.last_w:
                add(self.last_w[k])
        for k in writes:
            if k in self.last_w:
                add(self.last_w[k])
            for t in self.readers.get(k, ()):
                add(t)
        return best

    def _commit(self, tok, reads, writes):
        for k in reads:
            self.readers.setdefault(k, []).append(tok)
        for k in writes:
            self.last_w[k] = tok
            self.readers[k] = []

    def op(self, eng, fn, reads=(), writes=()):
        best = self._deps(reads, writes)
        waits = self._need(eng, best)
        self.count[eng] += 1
        tok = (eng, self.count[eng])
        self.lists[eng].append((waits, fn, (eng, 1)))
        self._commit(tok, reads, writes)
        return tok

    def dma(self, eng, fn, reads=(), writes=()):
        i = self.dma_rr
        self.dma_rr = (self.dma_rr + 1) % self.n_dma_sems
        sk = "dma%d" % i
        best = self._deps(reads, writes)
        if self.dma_cnt[i] > 0:
            best[sk] = max(best.get(sk, 0), self.dma_cnt[i])
        waits = self._need(eng, best)
        self.dma_cnt[i] += 16
        tok = (sk, self.dma_cnt[i])
        self.lists[eng].append((waits, fn, (sk, 16)))
        self._commit(tok, reads, writes)
        return tok

    def barrier(self):
        best = {}
        for e in self.engs:
            if self.count[e] > 0:
                best[e] = self.count[e]
        for i in range(self.n_dma_sems):
            if self.dma_cnt[i] > 0:
                best["dma%d" % i] = self.dma_cnt[i]
        for e in self.engs:
            b = {k: v for k, v in best.items() if k != e}
            waits = self._need(e, b)
            if waits:
                self.lists[e].append((waits, None, None))
        self.last_w = {}
        self.readers = {}

    def wait_all(self, eng, toks):
        best = {}
        for s, v in toks:
            best[s] = max(best.get(s, 0), v)
        waits = self._need(eng, best)
        self.lists[eng].append((waits, None, None))

    def emit(self, stack):
        nc = self.nc
        for e in self.engs:
            self.sems[e] = stack.enter_context(nc.semaphore("s_" + e))
        for i in range(self.n_dma_sems):
            self.sems["dma%d" % i] = stack.enter_context(nc.semaphore("s_dma%d" % i))
        block = stack.enter_context(nc.Block())
        sems = self.sems

        def run(engh, lst):
            for waits, fn, inc in lst:
                for sk, v in waits:
                    engh.wait_ge(sems[sk], v)
                if fn is not None:
                    ins = fn(engh)
                    ins.then_inc(sems[inc[0]], inc[1])

        @block.tensor
        def _(e):
            run(e, self.lists["pe"])

        @block.scalar
        def _(e):
            run(e, self.lists["act"])

        @block.vector
        def _(e):
            run(e, self.lists["dve"])

        @block.gpsimd
        def _(e):
            run(e, self.lists["pool"])

        @block.sync
        def _(e):
            run(e, self.lists["sp"])


class Arena:
    def __init__(self, tile, nbytes):
        self.t = tile
        self.cap = nbytes
        self.off = 0
        self.n = 0

    def alloc(self, shape, dtype, name=None):
        n = int(np.prod(shape))
        esz = 4 if dtype == F32 else 2
        sz = (n * esz + 63) // 64 * 64
        off = self.off
        assert off + sz <= self.cap, ('arena overflow', name, off, sz, self.cap)
        self.off += sz
        ap = self.t[:, off // 2:(off + n * esz) // 2]
        if dtype == F32:
            ap = ap.bitcast(F32)
        if len(shape) == 2:
            ap = ap.rearrange("p (a b) -> p a b", a=shape[0])
        elif len(shape) == 3:
            ap = ap.rearrange("p (a b c) -> p a b c", a=shape[0], b=shape[1])
        self.n += 1
        return ap, (name or "t") + "#%d" % self.n


class _Stop(Exception):
    pass


def build_program(debug=None, stop=None):
    nc = bass.Bass("TRN2", target_bir_lowering=False)
    di = {}

    def dram_in(name, shape, dt=F32):
        di[name] = nc.dram_tensor(name, list(shape), dt, kind="ExternalInput").ap()
        return di[name]

    x_d = dram_in("x", [SEQ, D])
    ctx_d = dram_in("ctx", [CTX, D])
    cc_d = dram_in("cc", [128, 16])
    wmod_d = dram_in("w_mod", [D, 6 * D])
    bmod_d = dram_in("b_mod2", [2, 6 * D])
    ncol_d = dram_in("ncol", [128, 16])
    nrow_d = dram_in("nrow", [1, 2 * D])
    win_d = dram_in("w_in", [D, DIN])
    cw_d = dram_in("cw", [128, 54])
    cb_d = dram_in("cb", [128, 6])
    wg_d = dram_in("wg", [16, 512])
    bg_d = dram_in("bg", [1, 512])
    mixg_d = dram_in("mixg", [128, 8])
    alog_d = dram_in("alog", [128, 16])
    dtb_d = dram_in("dtb", [128, 16])
    dsk_d = dram_in("dsk", [128, 512])
    wout_d = dram_in("w_out", [D, D])
    wgate_d = dram_in("w_gate", [D, DFF])
    wup_d = dram_in("w_up", [D, DFF])
    wdown_d = dram_in("w_down", [DFF, D])
    ident_d = dram_in("ident", [128, 128], BF16)
    tribf_d = dram_in("tri_bf", [128, 256], BF16)
    trif_d = dram_in("tri_f", [128, 256])
    uppf_d = dram_in("upp_f", [128, 256])
    onesf_d = dram_in("ones_f", [128, 128])
    negm_d = dram_in("negm", [128, 1024], BF16)
    uallt_d = dram_in("uall_t", [48, 128], BF16)
    vallt_d = dram_in("vall_t", [48, 1024], BF16)
    bdg_d = dram_in("bdg", [128, 256])
    bds_d = dram_in("bds", [128, 8])
    gmask_d = dram_in("gmask", [128, 2])
    y_d = nc.dram_tensor("y", [OWN, D], F32, kind="ExternalOutput").ap()
    yf_scr = nc.dram_tensor("yf_scr", [OWN, D], F32, kind="Internal").ap()
    xn_scr = nc.dram_tensor("xn_scr", [OWN, D], F32, kind="Internal").ap()
    dbg_out = {}
    if debug:
        for nm, shp in debug.items():
            dbg_out[nm] = nc.dram_tensor("dbg_" + nm, list(shp), F32, kind="ExternalOutput").ap()

    st = ExitStack()
    dbg_toks = []
    with st:
        ARENA_BYTES = 207 * 1024
        arena_t = st.enter_context(nc.sbuf_tensor("arena", [128, ARENA_BYTES // 2], BF16))
        psum = [st.enter_context(nc.psum_tensor("pb%d" % i, [128, 512], F32)) for i in range(6)]
        psum2 = st.enter_context(nc.psum_tensor("pbig", [128, 1024], F32))
        S = Sched(nc)
        A = Arena(arena_t, ARENA_BYTES)

        PB = [(psum[i], "pb%d" % i) for i in range(6)]
        PB2 = (psum2, "pbig")

        REC = [None]
        ATOM = [0]

        class Chain:
            def __init__(self):
                self.steps = []

            def __enter__(self):
                self.prev = REC[0]
                REC[0] = self.steps
                return self

            def __exit__(self, *a):
                REC[0] = self.prev

        class atomic:
            def __enter__(self):
                if REC[0] is not None and ATOM[0] == 0:
                    REC[0].append([])
                ATOM[0] += 1

            def __exit__(self, *a):
                ATOM[0] -= 1

        def _sched(kind, eng, fn, R, W):
            def thunk():
                if kind == "op":
                    return S.op(eng, fn, reads=R, writes=W)
                return S.dma(eng, fn, reads=R, writes=W)
            if REC[0] is None:
                return thunk()
            if ATOM[0] > 0:
                REC[0][-1].append(thunk)
            else:
                REC[0].append([thunk])
            return None

        def interleave(chains):
            chains = [c for c in chains if c is not None and len(c.steps) > 0]
            pos = [0] * len(chains)
            while True:
                best, bi = None, -1
                for i, c in enumerate(chains):
                    if pos[i] < len(c.steps):
                        fr = (pos[i] + 0.5) / len(c.steps)
                        if best is None or fr < best:
                            best, bi = fr, i
                if bi < 0:
                    break
                for th in chains[bi].steps[pos[bi]]:
                    th()
                pos[bi] += 1

        def MM(out, lhsT, rhs, start=True, stop=True, R=(), W=()):
            return _sched("op", "pe", lambda e: e.matmul(out, lhsT, rhs, start=start, stop=stop), R, W)

        def TR(out, in_, R=(), W=()):
            return _sched("op", "pe", lambda e: e.transpose(out, in_, ident[0:in_.shape[0], 0:in_.shape[0]]), R, W)

        def ACT(out, in_, func, bias=None, scale=None, accum=None, R=(), W=()):
            kw = {}
            if bias is not None:
                kw["bias"] = bias
            if scale is not None:
                kw["scale"] = scale
            if accum is not None:
                kw["accum_out"] = accum
            return _sched("op", "act", lambda e: e.activation(out=out, in_=in_, func=func, **kw), R, W)

        def TT(eng, out, in0, in1, op, R=(), W=()):
            return _sched("op", eng, lambda e: e.tensor_tensor(out=out, in0=in0, in1=in1, op=op), R, W)

        def TS(eng, out, in0, s1, op0, s2=None, op1=None, R=(), W=()):
            if op1 is None:
                return _sched("op", eng, lambda e: e.tensor_scalar(out=out, in0=in0, scalar1=s1, scalar2=None, op0=op0), R, W)
            return _sched("op", eng, lambda e: e.tensor_scalar(out=out, in0=in0, scalar1=s1, scalar2=s2, op0=op0, op1=op1), R, W)

        def STT(out, in0, scalar, in1, op0, op1, R=(), W=()):
            return _sched("op", "dve", lambda e: e.scalar_tensor_tensor(out=out, in0=in0, scalar=scalar, in1=in1, op0=op0, op1=op1), R, W)

        def CP(eng, out, in_, R=(), W=()):
            if eng == "act":
                return ACT(out, in_, AF.Copy, R=R, W=W)
            return _sched("op", eng, lambda e: e.tensor_copy(out=out, in_=in_), R, W)

        def RECIP(out, in_, R=(), W=()):
            return _sched("op", "dve", lambda e: e.reciprocal(out=out, in_=in_), R, W)

        def MSET(eng, ap, val, W=()):
            return _sched("op", eng, lambda e: e.memset(ap, val), (), W)

        def DMA(q, out, in_, R=(), W=(), **kw):
            return _sched("dma", q, lambda e: e.dma_start(out=out, in_=in_, **kw), R, W)

        def bc(ap, shape):
            return ap.broadcast_to(shape)

        def dump(name, ap, key):
            dt_ = ap.dtype
            shp = [int(v) for v in ap.shape]
            o = nc.dram_tensor("dbg_" + name, shp, dt_, kind="ExternalOutput").ap()
            dbg_toks.append(DMA("sp", o, ap, R=[key], W=["dbg_" + name]))

        out_toks = []
        try:
            ident, k_ident = A.alloc([128], BF16, "ident")
            tribf, k_tribf = A.alloc([2, 128], BF16, "tribf")
            trif, k_trif = A.alloc([2, 128], F32, "trif")
            uppf, k_uppf = A.alloc([2, 128], F32, "uppf")
            onesf, k_onesf = A.alloc([128], F32, "onesf")
            onesb, k_onesb = A.alloc([128], BF16, "onesb")
            negm, k_negm = A.alloc([2, 512], BF16, "negm")
            bdg, k_bdg = A.alloc([256], F32, "bdg")
            bds, k_bds = A.alloc([8], F32, "bds")
            gmask, k_gmask = A.alloc([2], F32, "gmask")
            mixg, k_mixg = A.alloc([8], F32, "mixg")
            dskb, k_dskb = A.alloc([512], F32, "dskb")
            cb, k_cb = A.alloc([6], F32, "cb")
            gate_mix, k_gmix = A.alloc([1024], F32, "gate_mix")
            gate_ffn, k_gffn = A.alloc([1024], F32, "gate_ffn")
            mcols, k_mcols = A.alloc([64], F32, "mcols")
            CONST_K = [k_ident, k_tribf, k_trif, k_uppf, k_onesf, k_onesb, k_negm, k_bdg, k_bds, k_gmask, k_mixg, k_dskb, k_cb]
            def load_consts():
                for (ap, k, src) in [(ident, k_ident, ident_d), (onesf, k_onesf, onesf_d), (bdg, k_bdg, bdg_d),
                                     (bds, k_bds, bds_d), (gmask, k_gmask, gmask_d), (mixg, k_mixg, mixg_d),
                                     (dskb, k_dskb, dsk_d), (cb, k_cb, cb_d)]:
                    DMA("act", ap, src, W=[k])
                DMA("act", tribf, tribf_d.rearrange("p (a b) -> p a b", a=2), W=[k_tribf])
                DMA("act", trif, trif_d.rearrange("p (a b) -> p a b", a=2), W=[k_trif])
                DMA("act", uppf, uppf_d.rearrange("p (a b) -> p a b", a=2), W=[k_uppf])
                DMA("act", negm, negm_d.rearrange("p (a b) -> p a b", a=2), W=[k_negm])
                CP("dve", onesb, onesf, R=[k_onesf], W=[k_onesb])
            win_off = A.off
            win, k_win = A.alloc([8, DIN], BF16, "win")
            win_end = A.off
            win_view = win_d.rearrange("(k p) n -> p k n", p=128)
            K_WIN = [k_win + ":%d" % kc for kc in range(8)]
            scbP, k_scbP = A.alloc([16], BF16, "scbP")
            cwt, k_cwt = A.alloc([54], F32, "cwt")
            cwd, k_cwd = A.alloc([6, 9, 128], BF16, "cwd")
            identf, k_identf = A.alloc([128], F32, "identf")
            wgs, k_wgs = A.alloc([512], BF16, "wgs")
            bgs, k_bgs = A.alloc([512], BF16, "bgs")
            aneg, k_aneg = A.alloc([16], F32, "aneg")
            dtb, k_dtb = A.alloc([16], F32, "dtb")

            def setup_small():
                DMA("act", cwt, cw_d, W=[k_cwt])
                DMA("pool", wgs[0:16, :], wg_d, W=[k_wgs])
                DMA("pool", bgs[0:1, :], bg_d, W=[k_bgs])
                DMA("act", aneg, alog_d, W=[k_aneg])
                DMA("act", dtb, dtb_d, W=[k_dtb])
                ACT(aneg, aneg, AF.Exp, R=[k_aneg], W=[k_aneg])
                TS("dve", aneg, aneg, -1.0, ALU.mult, R=[k_aneg], W=[k_aneg])
                CP("dve", identf, ident, R=[k_ident], W=[k_identf])
                for c6 in range(6):
                    for tap in range(9):
                        TS("dve", cwd[:, c6, tap, :], identf, cwt[:, c6 * 9 + tap: c6 * 9 + tap + 1], ALU.mult,
                           R=[k_identf, k_cwt], W=[k_cwd])
            persist_mark = A.off

            ccs, k_ccs = A.alloc([16], F32, "ccs")
            sc, k_sc = A.alloc([16], F32, "sc")
            mrow, k_mrow = A.alloc([6 * D], F32, "mrow")
            bmod, k_bmod = A.alloc([2 * D], F32, "bmod")
            mctx, k_mctx = A.alloc([2 * D], F32, "mctx")
            ncol, k_ncol = A.alloc([16], F32, "ncol")
            wmb = [A.alloc([8, 512], F32, "wmb%d" % i) for i in range(3)]
            wmbb = [A.alloc([8, 512], BF16, "wmbb%d" % i) for i in range(2)]
            scb, k_scb = A.alloc([16], BF16, "scb")
            wm_view = wmod_d.rearrange("(k p) n -> p k n", p=128)
            DMA("sp", ccs, cc_d, W=[k_ccs])
            for nb in range(3):
                DMA("sp", wmb[nb][0], wm_view[:, :, nb * 512:(nb + 1) * 512], W=[wmb[nb][1]])
            ACT(sc, ccs, AF.Silu, R=[k_ccs], W=[k_sc])
            CP("dve", scb, sc, R=[k_sc], W=[k_scb])
            scv = scb.rearrange("p (k j) -> p k j", j=2)
            CP("dve", scbP, sc, R=[k_sc], W=[k_scbP])
            DMA("act", bmod[0:2, :], bmod_d[:, 0:2 * D], W=[k_bmod])
            DMA("act", ncol, ncol_d, W=[k_ncol])
            load_consts()
            for kc in range(8):
                DMA("pool", win[:, kc, :], win_view[:, kc, :], W=[K_WIN[kc]], max_dma_last_dim=4096)
            if stop == "mod0":
                dump("sc", sc, k_sc)
                dump("tribf", tribf, k_tribf)
                dump("onesb", onesb, k_onesb)
                raise _Stop()
            for nb in range(4):
                wt, kw_ = wmb[nb % 3]
                wb, kwb = wmbb[nb % 2]
                CP("dve", wb[:, 0:4, :], wt[:, 0:4, :], R=[kw_], W=[kwb])
                CP("act", wb[:, 4:8, :], wt[:, 4:8, :], R=[kw_], W=[kwb])
                if nb + 3 < 4:
                    DMA("sp", wt, wm_view[:, :, (nb + 3) * 512:(nb + 4) * 512], W=[kw_])
                pt, kp = PB[nb % 2]
                for kc in range(8):
                    MM(pt[0:2, :], scv[:, kc, :], wb[:, kc, :], start=(kc == 0), stop=(kc == 7), R=[k_scb, kwb], W=[kp])
                TT("dve", mrow[0:2, nb * 512:(nb + 1) * 512], pt[0:2, :], bmod[0:2, nb * 512:(nb + 1) * 512], ALU.add,
                   R=[kp, k_bmod], W=[k_mrow])
            if stop == "mod1":
                dump("mrow", mrow[0:2, :], k_mrow)
                raise _Stop()
            setup_small()
            DMA("sp", mctx[0:1, :], mrow[1:2, 0:2 * D], R=[k_mrow], W=[k_mctx])
            if stop == "mod2":
                dump("mctx", mctx[0:1, :], k_mctx)
                raise _Stop()
            pcol, kpc = PB[2]
            col_srcs = [(0, mrow, 0), (1, mrow, D), (4, mctx, 0), (5, mctx, D)]
            for (vi, src, off) in col_srcs:
                for kc in range(8):
                    ci = vi * 8 + kc
                    MM(pcol[:, 2 * ci: 2 * ci + 2], src[0:1, off + kc * 128: off + (kc + 1) * 128], onesf[0:1, 0:2],
                       R=[k_mrow, k_mctx, k_onesf], W=[kpc])
            CP("dve", mcols[:, 0:16], pcol[:, 0:32].rearrange("p (c two) -> p c two", two=2)[:, :, 0], R=[kpc], W=[k_mcols])
            CP("dve", mcols[:, 32:48], pcol[:, 64:96].rearrange("p (c two) -> p c two", two=2)[:, :, 0], R=[kpc], W=[k_mcols])
            if stop == "mod3":
                dump("mcols", mcols[:, 0:48], k_mcols)
                raise _Stop()
            for (sl, g0) in [(slice(8, 16), 0), (slice(40, 48), 0)]:
                TS("dve", mcols[:, sl], mcols[:, sl], 1.0, ALU.add, R=[k_mcols], W=[k_mcols])
                TT("dve", mcols[:, sl], mcols[:, sl], ncol[:, g0:g0 + 8], ALU.mult, R=[k_mcols, k_ncol], W=[k_mcols])
            SH_LAT, G_LAT, SH_FFN, G_FFN, SH_CTX, G_CTX = 0, 8, 16, 24, 32, 40
            if stop == "mod":
                dump("mcols", mcols[:, 0:48], k_mcols)
                dump("gate_mix", gate_mix, k_gmix)
                dump("gate_ffn", gate_ffn, k_gffn)
                raise _Stop()
            if debug and "mcols" in dbg_out:
                DMA("sp", dbg_out["mcols"], mcols[:, 0:48], R=[k_mcols], W=["dbg_mcols"])
            if debug and "gate_mix" in dbg_out:
                DMA("sp", dbg_out["gate_mix"], gate_mix, R=[k_gmix], W=["dbg_gm"])
            S.barrier()
            A.off = persist_mark

            if stop == "A0":
                dump("win0", win[:, 0, 0:512], K_WIN[0])
                dump("win7", win[:, 7, 2352:2864], K_WIN[7])
                dump("cwd", cwd[:, 2, :, :], k_cwd)
                dump("aneg", aneg, k_aneg)
                dump("wgs", wgs[0:16, :], k_wgs)
                raise _Stop()
            class FS:
                pass

            def new_fs(tag):
                f = FS()
                f.tag = tag
                f.qT, f.k_qT = A.alloc([2, 256], BF16, "qT" + tag)
                f.kT, f.k_kT = A.alloc([2, 256], BF16, "kT" + tag)
                f.ktm, f.k_ktm = A.alloc([2, 256], BF16, "ktm" + tag)
                f.v, f.k_v = A.alloc([2, 512], BF16, "v" + tag)
                f.g, f.k_g = A.alloc([2, 2, 256], BF16, "g" + tag)
                f.x, f.k_x = A.alloc([2, 512], BF16, "x" + tag)
                f.Btm, f.k_Btm = A.alloc([2, 128], BF16, "Btm" + tag)
                f.BTm, f.k_BTm = A.alloc([2, 256], BF16, "BTm" + tag)
                f.CT, f.k_CT = A.alloc([256], BF16, "CT" + tag)
                f.dt, f.k_dt = A.alloc([2, 16], F32, "dt" + tag)
                f.a, f.k_a = A.alloc([2, 16], F32, "a" + tag)
                f.r, f.k_r = A.alloc([2, 512], BF16, "r" + tag)
                f.z, f.k_z = A.alloc([2, 512], BF16, "z" + tag)
                return f

            fsets = [new_fs("A"), new_fs("B")]

            xt = [A.alloc([D], F32, "xt%d" % i) for i in range(3)]
            xsb = [A.alloc([D], BF16, "xs%d" % i) for i in range(3)]
            stat, k_stat = A.alloc([3, 4], F32, "stat")
            hTs = [A.alloc([8, 384], BF16, "hT%d" % i) for i in range(2)]
            hT, k_hT = hTs[0]
            hT_rr = [0]
            pcl, k_pcl = A.alloc([6, 6, 66], BF16, "pcl")
            xa, k_xa = A.alloc([6, 256], BF16, "xa")
            lrT, k_lrT = A.alloc([2, 256], BF16, "lrT")
            gwe, k_gwe = A.alloc([512], F32, "gwe")
            gwl, k_gwl = gwe, k_gwe
            sile, k_sile = A.alloc([512], F32, "sile")
            cbn, k_cbn = A.alloc([6], F32, "cbn")
            TS("dve", cbn, cb, -1.0, ALU.mult, R=[k_cb], W=[k_cbn])
            dtw, k_dtw = A.alloc([16], F32, "dtw")
            MSET("pool", pcl, 0.0, W=[k_pcl])

            PT_BF = [(psum[4][:, :].bitcast(BF16), "pb4"), (psum[5][:, :].bitcast(BF16), "pb5")]

            evac_rr = [0]

            def evac_eng():
                evac_rr[0] += 1
                return "dve" if evac_rr[0] % 3 == 0 else "act"

            def feats_norm(hsel, src, tok0, is_ctx, nseq):
                hT, k_hT = hTs[hsel]
                ntok_ext = 256 if is_ctx else 384
                ntile = 2 if is_ctx else 3
                for ti in range(ntile):
                    xtt, kx = xt[ti]
                    if ti < 2:
                        DMA("sp", xtt, src[tok0 + ti * 128: tok0 + (ti + 1) * 128, :], W=[kx])
                    else:
                        lo = tok0 - 64 if tok0 >= 64 else 0
                        hi = tok0 + 256 if tok0 + 320 <= nseq else 0
                        DMA("sp", xtt[0:64, :], src[lo:lo + 64, :], W=[kx])
                        DMA("sp", xtt[64:128, :], src[hi:hi + 64, :], W=[kx])
                    ss = stat[:, ti, 0:1]
                    kst_ = k_stat + ":%d" % ti
                    ACT(xsb[ti][0], xtt, AF.Square, accum=ss, R=[kx], W=[xsb[ti][1], kst_])
                    ACT(stat[:, ti, 1:2], ss, AF.Ln, scale=1.0 / D, bias=1e-6, R=[kst_], W=[kst_])
                    ACT(stat[:, ti, 3:4], stat[:, ti, 1:2], AF.Exp, scale=-0.5, R=[kst_], W=[kst_])
                    xs_, kxs = xsb[ti]
                    TS("dve", xs_, xtt, stat[:, ti, 3:4], ALU.mult, R=[kx, kst_], W=[kxs])
                gofs = G_CTX if is_ctx else G_LAT
                sofs = SH_CTX if is_ctx else SH_LAT
                pt, kpt = PT_BF[0]
                for kp in range(4):
                    with atomic():
                        for j in range(2):
                            kc = kp * 2 + j
                            for ti in range(ntile):
                                TR(pt[:, j * 384 + ti * 128: j * 384 + (ti + 1) * 128], xsb[ti][0][:, kc * 128:(kc + 1) * 128],
                                   R=[xsb[ti][1], k_ident], W=[kpt])
                        for j in range(2):
                            kc = kp * 2 + j
                            TS("dve", hT[:, kc, 0:ntok_ext], pt[:, j * 384: j * 384 + ntok_ext],
                               mcols[:, gofs + kc: gofs + kc + 1], ALU.mult, mcols[:, sofs + kc: sofs + kc + 1], ALU.add,
                               R=[kpt, k_mcols], W=[k_hT + ":%d" % kc])

            prr = [0]

            def nextp():
                prr[0] += 1
                return PB[prr[0] % 2]

            def feats_proj(fs, hsel, src, tok0, is_ctx, nseq, mode="full"):
                hT, k_hT = hTs[hsel]
                ntok_ext = 256 if is_ctx else 384
                K_HT = [k_hT + ":%d" % kc for kc in range(8)]
                full = (mode == "full")
                dirs = [1] if mode == "sB" else [0, 1]
                nc6 = 6 if full else 5
                fm = []
                if full:
                    fm += [("q", 0, 128, 0), ("q", 128, 128, 1), ("k", 256, 128, 0), ("k", 384, 128, 1)]
                fm += [("lr", 1536 + 16 * d_, 16, d_) for d_ in dirs]
                for (nm, c0, M, idx) in fm:
                    pt, kpt = nextp()
                    with atomic():
                        for kc in range(8):
                            MM(pt[0:M, 0:256], win[:, kc, c0:c0 + M], hT[:, kc, 0:256], start=(kc == 0), stop=(kc == 7),
                               R=[K_WIN[kc], K_HT[kc]], W=[kpt])
                        if nm == "q":
                            ACT(fs.qT[:, idx, :], pt[:, 0:256], AF.Copy, scale=0.125, R=[kpt], W=[fs.k_qT])
                        elif nm == "k":
                            CP("act", fs.kT[:, idx, :], pt[:, 0:256], R=[kpt], W=[fs.k_kT])
                        else:
                            CP("act", lrT[0:16, idx, :], pt[0:16, 0:256], R=[kpt], W=[k_lrT])
                pc = pcx if is_ctx else pcl
                kpc_ = k_pcx if is_ctx else k_pcl
                for c6 in range(nc6):
                    pt, kpt = nextp()
                    with atomic():
                        for kc in range(8):
                            MM(pt[:, 0:ntok_ext], win[:, kc, 2080 + c6 * 128: 2080 + (c6 + 1) * 128], hT[:, kc, 0:ntok_ext],
                               start=(kc == 0), stop=(kc == 7), R=[K_WIN[kc], K_HT[kc]], W=[kpt])
                        eng = evac_eng()
                        if is_ctx:
                            CP(eng, pc[:, c6, 1, 1:257], pt[:, 0:256], R=[kpt], W=[kpc_])
                        else:
                            CP(eng, pc[:, c6, 1:5, 1:65], pt[:, 0:256].rearrange("p (r c) -> p r c", r=4), R=[kpt], W=[kpc_])
                            eng = evac_eng()
                            if tok0 >= 64 and tok0 + 320 <= nseq:
                                CP(eng, pc[:, c6, 0:6:5, 1:65], pt[:, 256:384].rearrange("p (r c) -> p r c", r=2), R=[kpt], W=[kpc_])
                            else:
                                if tok0 >= 64:
                                    CP(eng, pc[:, c6, 0, 1:65], pt[:, 256:320], R=[kpt], W=[kpc_])
                                else:
                                    MSET("pool", pc[:, c6, 0, 1:65], 0.0, W=[kpc_])
                                if tok0 + 320 <= nseq:
                                    CP(eng, pc[:, c6, 5, 1:65], pt[:, 320:384], R=[kpt], W=[kpc_])
                                else:
                                    MSET("pool", pc[:, c6, 5, 1:65], 0.0, W=[kpc_])
                for tpass in range(4):
                    ti = tpass % 2
                    if tpass < 2:
                        tml = [("v", 512, 512), ("k", 256, 256), ("dt", 2848, 16)]
                    else:
                        tml = [("r", 1024, 512), ("z", 1568, 512)] if full else []
                    for (nm, c0, N) in tml:
                        pt, kpt = nextp()
                        with atomic():
                            for kc in range(8):
                                MM(pt[:, 0:N], hT[:, kc, ti * 128:(ti + 1) * 128], win[:, kc, c0:c0 + N], start=(kc == 0), stop=(kc == 7),
                                   R=[K_WIN[kc], K_HT[kc]], W=[kpt])
                            if nm == "v":
                                CP("act", fs.v[:, ti, :], pt[:, :], R=[kpt], W=[fs.k_v])
                            elif nm == "r":
                                ACT(fs.r[:, ti, :], pt[:, :], AF.Silu, R=[kpt], W=[fs.k_r])
                            elif nm == "z":
                                ACT(fs.z[:, ti, :], pt[:, :], AF.Silu, R=[kpt], W=[fs.k_z])
                            elif nm == "k":
                                CP("act", fs.ktm[:, ti, :], pt[:, 0:256], R=[kpt], W=[fs.k_ktm])
                            else:
                                TT("dve", dtw, pt[:, 0:16], dtb, ALU.add, R=[kpt, k_dtb], W=[k_dtw])
                        if nm == "dt":
                            ACT(dtw, dtw, AF.Exp, R=[k_dtw], W=[k_dtw])
                            ACT(fs.dt[:, ti, :], dtw, AF.Ln, bias=1.0, R=[k_dtw], W=[fs.k_dt])
                            TT("dve", fs.a[:, ti, :], fs.dt[:, ti, :], aneg, ALU.mult, R=[fs.k_dt, k_aneg], W=[fs.k_a])
                    if tpass >= 2:
                        continue
                    pt, kpt = nextp()
                    gsl = slice(dirs[0] * 256, 512)
                    with atomic():
                        for d in dirs:
                            MM(pt[:, d * 256:(d + 1) * 256], lrT[0:16, d, ti * 128:(ti + 1) * 128], wgs[0:16, d * 256:(d + 1) * 256],
                               start=True, stop=False, R=[k_lrT, k_wgs], W=[kpt])
                            MM(pt[:, d * 256:(d + 1) * 256], onesb[0:1, :], bgs[0:1, d * 256:(d + 1) * 256],
                               start=False, stop=True, R=[k_onesb, k_bgs], W=[kpt])
                        ACT(gwe[:, gsl], pt[:, gsl], AF.Exp, scale=-1.0, R=[kpt], W=[k_gwe])
                    ACT(gwl[:, gsl], gwe[:, gsl], AF.Ln, bias=1.0, R=[k_gwe], W=[k_gwl])
                    if os.environ.get("X_G"):
                        TS("pool", fs.g[:, dirs[0]:2, ti, :], gwl[:, gsl].rearrange("p (d f) -> p d f", f=256), -1.0 / 16.0, ALU.mult,
                           R=[k_gwl], W=[fs.k_g])
                    else:
                        ACT(fs.g[:, dirs[0]:2, ti, :], gwl[:, gsl].rearrange("p (d f) -> p d f", f=256), AF.Copy, scale=-1.0 / 16.0,
                            R=[k_gwl], W=[fs.k_g])
                taps = [(0, dc) for dc in (-1, 0, 1)] if is_ctx else [(dr, dc) for dr in (-1, 0, 1) for dc in (-1, 0, 1)]
                for c6 in range(nc6):
                    pt, kpt = nextp()
                    with atomic():
                        for i, (dr, dc) in enumerate(taps):
                            tap = (dr + 1) * 3 + (dc + 1)
                            if is_ctx:
                                rhs = pc[:, c6, 1, 1 + dc: 257 + dc]
                            else:
                                rhs = pc[:, c6, 1 + dr: 5 + dr, 1 + dc: 65 + dc]
                            MM(pt[:, 0:256], cwd[:, c6, tap, :], rhs, start=(i == 0), stop=(i == len(taps) - 1), R=[k_cwd, kpc_], W=[kpt])
                        if c6 == 5:
                            cdst, kcd = fs.CT, fs.k_CT
                        else:
                            cdst, kcd = xa[:, c6, :], k_xa + ":%d" % c6
                        ACT(cdst, pt[:, 0:256], AF.Silu, bias=cb[:, c6:c6 + 1], R=[kpt, k_cb], W=[kcd])
                if full:
                    for g in range(2):
                        TS("dve", fs.BTm[:, g, :], xa[:, 4, :], gmask[:, g:g + 1], ALU.mult, R=[k_xa + ":4", k_gmask], W=[fs.k_BTm])
                pt, kpt = PT_BF[0]
                for ti in range(2):
                    with atomic():
                        for c6 in range(5):
                            TR(pt[:, c6 * 128:(c6 + 1) * 128], xa[:, c6, ti * 128:(ti + 1) * 128], R=[k_xa + ":%d" % c6, k_ident], W=[kpt])
                        CP("dve", fs.x[:, ti, :], pt[:, 0:512], R=[kpt], W=[fs.k_x])
                        CP("dve", fs.Btm[:, ti, :], pt[:, 512:640], R=[kpt], W=[fs.k_Btm])

            Sg = [A.alloc([2, 256], F32, "Sg%d" % d) for d in range(2)]
            Sgb = [A.alloc([2, 256], BF16, "Sgb%d" % d) for d in range(2)]
            Ss = [A.alloc([512], F32, "Ss%d" % d) for d in range(2)]
            Ssb = [A.alloc([512], BF16, "Ssb%d" % d) for d in range(2)]
            for d in range(2):
                MSET("pool", Sg[d][0], 0.0, W=[Sg[d][1]])
                MSET("pool", Sgb[d][0], 0.0, W=[Sgb[d][1]])
                MSET("pool", Ss[d][0], 0.0, W=[Ss[d][1]])
                MSET("pool", Ssb[d][0], 0.0, W=[Ssb[d][1]])
            eq, k_eq = A.alloc([2, 256], F32, "eq")
            ek, k_ek = A.alloc([2, 256], F32, "ek")
            qs, k_qs = A.alloc([2, 256], BF16, "qs")
            ks = [A.alloc([2, 256], BF16, "ks%d" % h) for h in range(2)]
            ekt, k_ekt = A.alloc([2, 256], F32, "ekt")
            kst, k_kst = A.alloc([2, 256], BF16, "kst")
            attm, k_attm = A.alloc([4, 128], BF16, "attm")
            tmpg, k_tmpg = A.alloc([2, 256], F32, "tmpg")
            _mark = A.off
            pcx, k_pcx = A.alloc([6, 3, 258], BF16, "pcx")
            MSET("dve", pcx, 0.0, W=[k_pcx])
            A.off = _mark
            yo = [A.alloc([1024], F32, "yo%d" % i) for i in range(2)]
            sq, k_sq = A.alloc([512], F32, "sq")
            t1, k_t1 = A.alloc([512], F32, "t1")
            yfr = yo
            e3, k_e3 = A.alloc([3, 8], F32, "e3")
            wj, k_wj = A.alloc([8], F32, "wj")
            p1s, k_p1s = A.alloc([8], F32, "p1s")
            cspb, k_cspb = A.alloc([48], BF16, "cspb")
            uall, k_uall = A.alloc([128], BF16, "uall")
            vall, k_vall = A.alloc([1024], BF16, "vall")
            LT, k_LT = A.alloc([1024], BF16, "LT")
            scT, k_scT = A.alloc([8, 128], BF16, "scT")
            xdt, k_xdt = A.alloc([8, 64], BF16, "xdt")
            xw, k_xw = A.alloc([8, 64], BF16, "xw")
            tmpi, k_tmpi = A.alloc([512], F32, "tmpi")
            tmps, k_tmps = A.alloc([512], F32, "tmps")
            MSET("pool", cspb, 0.0, W=[k_cspb])
            DMA("sp", uall[0:48, :], uallt_d, W=[k_uall])
            DMA("sp", vall[0:48, :], vallt_d, W=[k_vall])
            bd16, k_bd16 = A.alloc([1024], BF16, "bd16")
            DMA("sp", bd16[0:16, :], vallt_d[32:48, :], W=[k_bd16])

            mst, k_mst = A.alloc([16], F32, "mst")
            ob, k_ob = xsb[2]
            oT, k_oT = xsb[1][0].rearrange("p (k t) -> p k t", k=8), xsb[1][1]
            xo, k_xo = xt[2]
            tmo, k_tmo = xt[1]
            pst, k_pst = A.alloc([4], F32, "pst")
            print("arena phase A bytes:", A.off)

            PG_A, PG_B = PB[2], PB[3]
            P5 = psum[5]
            P5BF = psum[5][:, :].bitcast(BF16)
            PBIG = psum2

            def gla_prep(fs, d, full):
                pt, kpt = PG_A
                with atomic():
                    for ti in range(2):
                        for p in range(2):
                            MM(pt[:, p * 256 + ti * 128: p * 256 + (ti + 1) * 128], fs.g[:, d, ti, p * 128:(p + 1) * 128], tribf[:, d, :],
                               R=[fs.k_g, k_tribf], W=[kpt])
                    ptv = pt[:, :].rearrange("p (a b) -> p a b", a=2)
                    ACT(eq, ptv, AF.Exp, R=[kpt], W=[k_eq])
                    if full:
                        ACT(ek, ptv, AF.Exp, scale=-1.0, R=[kpt], W=[k_ek])
                if full:
                    TT("dve", qs, fs.qT, eq, ALU.mult, R=[fs.k_qT, k_eq], W=[k_qs])
                    for hh in range(2):
                        STT(ks[hh][0], fs.kT, gmask[:, hh:hh + 1], ek, ALU.mult, ALU.mult, R=[fs.k_kT, k_gmask, k_ek], W=[ks[hh][1]])
                pt2, kpt2 = PG_B
                with atomic():
                    for ti in range(2):
                        MM(pt2[:, ti * 256:(ti + 1) * 256], tribf[:, d, :], fs.g[:, d, ti, :], R=[fs.k_g, k_tribf], W=[kpt2])
                    ACT(ekt, pt2[:, :].rearrange("p (a b) -> p a b", a=2), AF.Exp, scale=-1.0, R=[kpt2], W=[k_ekt])
                TT("pool", kst, fs.ktm, ekt, ALU.mult, R=[fs.k_ktm, k_ekt], W=[k_kst])

            def gla_tile(fs, d, full, ti):
                last = 127 if d == 0 else 0
                sg, ksg = Sg[d]
                sgb, ksgb = Sgb[d]
                yot, kyo = yo[ti]
                tsl = slice(ti * 128, (ti + 1) * 128)
                if full:
                    pa, kpa = PG_A
                    with atomic():
                        for h in range(4):
                            p, hh = h // 2, h % 2
                            MM(pa[:, h * 128:(h + 1) * 128], ks[hh][0][:, p, tsl], qs[:, p, tsl], R=[ks[hh][1], k_qs], W=[kpa])
                        TT("dve", attm, pa[:, :].rearrange("p (h i) -> p h i", h=4), bc(trif[:, d, :].unsqueeze(1), [128, 4, 128]), ALU.mult,
                           R=[kpa, k_trif], W=[k_attm])
                    po, kpo = PG_B
                    with atomic():
                        for p in range(2):
                            MM(po[:, p * 256:(p + 1) * 256], qs[:, p, tsl], sgb[:, p, :], start=True, stop=False, R=[k_qs, ksgb], W=[kpo])
                            for hh in range(2):
                                h = 2 * p + hh
                                MM(po[:, h * 128:(h + 1) * 128], attm[:, h, :], fs.v[:, ti, h * 128:(h + 1) * 128], start=False, stop=(hh == 1),
                                   R=[k_attm, fs.k_v], W=[kpo])
                        if d == 0:
                            CP("act", yot[:, 0:512], po[:, :], R=[kpo], W=[kyo + ":g"])
                        else:
                            TT("dve", yot[:, 0:512], po[:, :], yot[:, 0:512], ALU.add, R=[kpo, kyo + ":g"], W=[kyo + ":g"])
                psg, kpsg = PG_A
                with atomic():
                    for p in range(2):
                        MM(psg[:, p * 256:(p + 1) * 256], kst[:, ti, p * 128:(p + 1) * 128], fs.v[:, ti, p * 256:(p + 1) * 256],
                           R=[k_kst, fs.k_v], W=[kpsg])
                    TT("dve", tmpg, sg, psg[:, :].rearrange("p (a b) -> p a b", a=2), ALU.add, R=[ksg, kpsg], W=[k_tmpg])
                for p in range(2):
                    STT(sg[:, p, :], tmpg[:, p, :], eq[:, p, ti * 128 + last: ti * 128 + last + 1], bdg, ALU.mult, ALU.mult,
                        R=[k_tmpg, k_eq, k_bdg], W=[ksg])
                CP("act", sgb, sg, R=[ksg], W=[ksgb])

            def ssd_tile(fs, d, full, ti):
                ssd_, kss = Ss[d]
                ssb, kssb = Ssb[d]
                yot, kyo = yo[ti]
                tsl = slice(ti * 128, (ti + 1) * 128)
                av = fs.a[:, ti, d * 8:(d + 1) * 8]
                kp1 = "pb5:a"
                with atomic():
                    MM(P5[:, 0:8], trif[:, d, :], av, R=[k_trif, fs.k_a], W=[kp1])
                    MM(P5[:, 8:16], uppf[:, d, :], av, R=[k_uppf, fs.k_a], W=[kp1])
                    MM(P5[:, 16:24], onesf, av, R=[k_onesf, fs.k_a], W=[kp1])
                    ACT(e3, P5[:, 0:24].rearrange("p (a b) -> p a b", a=3), AF.Exp, R=[kp1], W=[k_e3])
                    if full:
                        ACT(p1s, P5[:, 0:8], AF.Copy, R=[kp1], W=[k_p1s])
                if full:
                    CP("pool", cspb[:, 0:8], p1s, R=[k_p1s], W=[k_cspb])
                    TT("pool", cspb[:, 8:16], p1s, cspb[:, 0:8], ALU.subtract, R=[k_p1s, k_cspb], W=[k_cspb])
                    TS("pool", cspb[:, 32:48], cspb[:, 0:16], -1.0, ALU.mult, R=[k_cspb], W=[k_cspb])
                TT("pool", wj, e3[:, 1, :], fs.dt[:, ti, d * 8:(d + 1) * 8], ALU.mult, R=[k_e3, fs.k_dt], W=[k_wj])
                if full:
                    ptb = P5BF[:, 640:768]
                    kptb = "pb5:t"
                    with atomic():
                        TR(ptb[0:48, :], cspb, R=[k_cspb, k_ident], W=[kptb])
                        TT("dve", vall[0:16, :].rearrange("p (h i) -> p h i", h=8), bc(ptb[0:16, :].unsqueeze(1), [16, 8, 128]),
                           bd16[0:16, :].rearrange("p (h i) -> p h i", h=8), ALU.mult, R=[kptb, k_bd16], W=[k_vall])
                        CP("act", uall[32:48, :], ptb[32:48, :], R=[kptb], W=[k_uall])
                    kb0, kb1 = "pbig:0", "pbig:1"
                    with atomic():
                        for half in range(2):
                            MM(PBIG[:, half * 512:(half + 1) * 512], uall[0:48, :], vall[0:48, half * 512:(half + 1) * 512], start=True, stop=False,
                               R=[k_uall, k_vall], W=[kb0 if half == 0 else kb1])
                            MM(PBIG[:, half * 512:(half + 1) * 512], ident, negm[:, d, :], start=False, stop=True, R=[k_ident, k_negm],
                               W=[kb0 if half == 0 else kb1])
                        for half in range(2):
                            ACT(LT[:, half * 512:(half + 1) * 512], PBIG[:, half * 512:(half + 1) * 512], AF.Exp,
                                R=[kb0 if half == 0 else kb1], W=[k_LT])
                    kpG = "pb5:g"
                    with atomic():
                        for g in range(2):
                            MM(P5[:, 32 + g * 128: 32 + (g + 1) * 128], fs.BTm[:, g, tsl], fs.CT[:, tsl], R=[fs.k_BTm, fs.k_CT], W=[kpG])
                        TT("dve", scT.rearrange("p (g h) i -> p g h i", g=2),
                           bc(P5[:, 32:288].rearrange("p (g i) -> p g i", g=2).unsqueeze(2), [128, 2, 4, 128]),
                           LT.rearrange("p (g h i) -> p g h i", g=2, h=4), ALU.mult, R=[kpG, k_LT], W=[k_scT])
                    TT("pool", xdt, fs.x[:, ti, :].rearrange("p (h q) -> p h q", h=8),
                       bc(fs.dt[:, ti, d * 8:(d + 1) * 8].unsqueeze(2), [128, 8, 64]), ALU.mult, R=[fs.k_x, fs.k_dt], W=[k_xdt])
                    with atomic():
                        for h in range(8):
                            MM(PBIG[:, h * 64:(h + 1) * 64], scT[:, h, :], xdt[:, h, :], R=[k_scT, k_xdt], W=[kb0])
                        MM(PBIG[:, 512:1024], fs.CT[:, tsl], ssb, R=[fs.k_CT, kssb], W=[kb1])
                        TT("dve", tmpi.rearrange("p (h q) -> p h q", h=8), PBIG[:, 512:1024].rearrange("p (h q) -> p h q", h=8),
                           bc(e3[:, 0, :].unsqueeze(2), [128, 8, 64]), ALU.mult, R=[kb1, k_e3], W=[k_tmpi])
                        if d == 1:
                            TT("dve", tmpi, tmpi, yot[:, 512:1024], ALU.add, R=[k_tmpi, kyo + ":s"], W=[k_tmpi])
                        TT("dve", yot[:, 512:1024], PBIG[:, 0:512], tmpi, ALU.add, R=[kb0, k_tmpi], W=[kyo + ":s"])
                TT("pool", xw, fs.x[:, ti, :].rearrange("p (h q) -> p h q", h=8), bc(wj.unsqueeze(2), [128, 8, 64]), ALU.mult,
                   R=[fs.k_x, k_wj], W=[k_xw])
                TT("pool" if (full and d == 1) else "dve", tmps.rearrange("p (h q) -> p h q", h=8), ssd_.rearrange("p (h q) -> p h q", h=8),
                   bc(e3[:, 2, :].unsqueeze(2), [128, 8, 64]), ALU.mult, R=[kss, k_e3], W=[k_tmps])
                with atomic():
                    MM(PBIG[:, 0:512], fs.Btm[:, ti, :], xw.rearrange("p h q -> p (h q)"), R=[fs.k_Btm, k_xw], W=["pbig:0"])
                    TT("dve", ssd_, tmps, PBIG[:, 0:512], ALU.add, R=[k_tmps, "pbig:0"], W=[kss])
                if os.environ.get("X_SSB"):
                    TT("pool", ssb.rearrange("p (h q) -> p h q", h=8), ssd_.rearrange("p (h q) -> p h q", h=8),
                       bc(bds.unsqueeze(2), [128, 8, 64]), ALU.mult, R=[kss, k_bds], W=[kssb])
                else:
                    for g in range(2):
                        TS("dve", ssb[:, g * 256:(g + 1) * 256], ssd_[:, g * 256:(g + 1) * 256], gmask[:, g:g + 1], ALU.mult,
                           R=[kss, k_gmask], W=[kssb])

            obs = [xsb[2], xsb[0]]

            def merge_gla(fs, ti, T):
                yot, kyo = yo[ti]
                ob_, kob = obs[T % 2]
                for h in range(4):
                    ACT(sq.bitcast(BF16)[:, h * 128:(h + 1) * 128], yot[:, h * 128:(h + 1) * 128], AF.Square, accum=mst[:, h:h + 1],
                        R=[kyo + ":g"], W=[k_sq + ":g", k_mst])
                ACT(mst[:, 0:4], mst[:, 0:4], AF.Ln, scale=1.0 / 128, bias=1e-6, R=[k_mst], W=[k_mst])
                ACT(mst[:, 0:4], mst[:, 0:4], AF.Exp, scale=-0.5, R=[k_mst], W=[k_mst])
                for h in range(4):
                    STT(ob_[:, h * 128:(h + 1) * 128], yot[:, h * 128:(h + 1) * 128], mst[:, h:h + 1], fs.r[:, ti, h * 128:(h + 1) * 128],
                        ALU.mult, ALU.mult, R=[kyo + ":g", k_mst, fs.k_r], W=[kob + ":g"])

            def merge_ssd(fs, ti, T):
                yot, kyo = yo[ti]
                ob_, kob = obs[T % 2]
                TT("pool", t2, fs.x[:, ti, :], dskb, ALU.mult, R=[fs.k_x, k_dskb], W=[k_t2])
                TT("dve", t2, t2, yot[:, 512:1024], ALU.add, R=[k_t2, kyo + ":s"], W=[k_t2])
                TT("pool", t2, t2, fs.z[:, ti, :], ALU.mult, R=[k_t2, fs.k_z], W=[k_t2])
                kmb = k_mst + "b"
                for g in range(2):
                    ACT(sq.bitcast(BF16)[:, 512 + g * 256: 512 + (g + 1) * 256], t2[:, g * 256:(g + 1) * 256], AF.Square, accum=mst[:, 4 + g:5 + g],
                        R=[k_t2], W=[k_sq + ":s", kmb])
                ACT(mst[:, 4:6], mst[:, 4:6], AF.Ln, scale=1.0 / 256, bias=1e-6, R=[kmb], W=[kmb])
                ACT(mst[:, 4:6], mst[:, 4:6], AF.Exp, scale=-0.5, R=[kmb], W=[kmb])
                for g in range(2):
                    ACT(ob_[:, 512 + g * 256: 512 + (g + 1) * 256], t2[:, g * 256:(g + 1) * 256], AF.Copy, scale=mst[:, 4 + g:5 + g],
                        R=[k_t2, kmb], W=[kob + ":s"])

            def merge_out(T):
                ob_, kob = obs[T % 2]
                ptb, kptb = PT_BF[0]
                DMA("sp", xo, x_d[T * 128:(T + 1) * 128, :], W=[k_xo])
                with atomic():
                    for kc in range(8):
                        TR(ptb[:, kc * 128:(kc + 1) * 128], ob_[:, kc * 128:(kc + 1) * 128], R=[kob + ":g", kob + ":s", k_ident], W=[kptb])
                    TT("dve", oT, ptb[:, 0:1024].rearrange("p (k t) -> p k t", k=8), bc(mixg.unsqueeze(2), [128, 8, 128]), ALU.mult,
                       R=[kptb, k_mixg], W=[k_oT])
                with atomic():
                    for nb in range(2):
                        pop, kpop = PB[nb]
                        for kc in range(8):
                            MM(pop[:, :], oT[:, kc, :], wout[:, kc, nb * 512:(nb + 1) * 512], start=(kc == 0), stop=(kc == 7),
                               R=[k_oT, k_wout], W=[kpop])
                        ACT(t1.bitcast(BF16)[:, nb * 512:(nb + 1) * 512], pop[:, :], AF.Square, accum=pst[:, nb:nb + 1],
                            R=[kpop], W=[k_t1, k_pst])
                    TT("dve", pst[:, 2:3], pst[:, 0:1], pst[:, 1:2], ALU.add, R=[k_pst], W=[k_pst])
                    ACT(pst[:, 2:3], pst[:, 2:3], AF.Ln, scale=1.0 / D, bias=1e-6, R=[k_pst], W=[k_pst])
                    ACT(pst[:, 3:4], pst[:, 2:3], AF.Exp, scale=-0.5, R=[k_pst], W=[k_pst])
                    for nb in range(2):
                        pop, kpop = PB[nb]
                        STT(tmo[:, nb * 512:(nb + 1) * 512], pop[:, :], pst[:, 3:4], gate_mix[:, nb * 512:(nb + 1) * 512],
                            ALU.mult, ALU.mult, R=[kpop, k_pst, k_gmix], W=[k_tmo])
                TT("dve", tmo[:, 0:512], tmo[:, 0:512], xo[:, 0:512], ALU.add, R=[k_tmo, k_xo], W=[k_tmo])
                TT("pool", tmo[:, 512:1024], tmo[:, 512:1024], xo[:, 512:1024], ALU.add, R=[k_tmo, k_xo], W=[k_tmo])
                DMA("pool", xn_scr[T * 128:(T + 1) * 128, :], tmo, R=[k_tmo], W=["xn_scr:%d" % T])

            fs_i = [0]

            def next_fs():
                fs_i[0] += 1
                return fsets[fs_i[0] % 2]

            FS_NAMES = ["qT", "kT", "ktm", "v", "g", "x", "Btm", "BTm", "CT", "dt", "a", "r", "z"]
            fs_scr = {}
            for nm in FS_NAMES:
                ap0 = getattr(fsets[0], nm)
                shp = [8] + [int(v) for v in ap0.shape]
                fs_scr[nm] = nc.dram_tensor("fsscr_" + nm, shp, ap0.dtype, kind="Internal").ap()

            def spill_fs(f, blk):
                for nm in FS_NAMES:
                    DMA("pool", fs_scr[nm][blk], getattr(f, nm), R=[getattr(f, "k_" + nm)], W=["fsscr:%s:%d" % (nm, blk)])

            def reload_fs(f, blk):
                for nm in FS_NAMES:
                    DMA("sp", getattr(f, nm), fs_scr[nm][blk], R=["fsscr:%s:%d" % (nm, blk)], W=[getattr(f, "k_" + nm)])

            seq = [(ctx_d, 0, True, CTX, "s2", [(0, False), (1, False)], -1)]
            for blk in range(0, 8):
                seq.append((x_d, blk * 256, False, SEQ, "full", [(0, True)], blk))
            for blk in range(15, 7, -1):
                seq.append((x_d, blk * 256, False, SEQ, "sB", [(1, False)], blk))
            fsl = []
            for j in range(len(seq)):
                fsl.append(next_fs())
            feats_norm(0, *seq[0][0:4])
            with Chain() as cn:
                feats_norm(1, *seq[1][0:4])
            with Chain() as cf:
                feats_proj(fsl[0], 0, *seq[0][0:5])
            interleave([cn, cf])
            for i, item in enumerate(seq):
                fcur = fsl[i]
                chains = []
                if i + 2 < len(seq):
                    with Chain() as cn:
                        feats_norm((i + 2) % 2, *seq[i + 2][0:4])
                    chains.append(cn)
                if i + 1 < len(seq):
                    with Chain() as cf:
                        feats_proj(fsl[i + 1], (i + 1) % 2, *seq[i + 1][0:5])
                    chains.append(cf)
                with Chain() as cg:
                    for (d, full) in item[5]:
                        gla_prep(fcur, d, full)
                        for ti in ([0, 1] if d == 0 else [1, 0]):
                            gla_tile(fcur, d, full, ti)
                with Chain() as cs:
                    for (d, full) in item[5]:
                        for ti in ([0, 1] if d == 0 else [1, 0]):
                            ssd_tile(fcur, d, full, ti)
                chains += [cg, cs]
                interleave(chains)
                if item[4] == "full":
                    blk = item[6]
                    for ti in range(2):
                        T = blk * 2 + ti
                        DMA("pool", yf_scr[T * 128:(T + 1) * 128, :], yo[ti][0], R=[yo[ti][1] + ":g", yo[ti][1] + ":s"], W=["yf_scr:%d" % T])
                    spill_fs(fcur, blk)
            S.barrier()
            A2 = Arena(arena_t, win_end)
            A2.off = win_off
            wout, k_wout = A2.alloc([8, D], BF16, "wout")
            t2, k_t2 = A.alloc([512], F32, "t2")
            wmbL = [A2.alloc([8, 256], F32, "wmbL%d" % i) for i in range(2)]
            wmbbL = [A2.alloc([8, 256], BF16, "wmbbL%d" % i) for i in range(2)]
            rowp, k_rowp = A2.alloc([256], F32, "rowp")
            bmodp, k_bmodp = A2.alloc([256], F32, "bmodp")
            nrowp, k_nrowp = A2.alloc([256], F32, "nrowp")
            ncolp, k_ncolp = A2.alloc([16], F32, "ncolp")
            DMA("sp", ncolp, ncol_d, W=[k_ncolp])
            scvP = scbP.rearrange("p (k j) -> p k j", j=2)

            def late_mod(k):
                c0 = 2 * D + 256 * k
                vec, q = k // 4, k % 4
                wt, kw_ = wmbL[k % 2]
                wb, kwb = wmbbL[k % 2]
                DMA("sp", wt, wm_view[:, :, c0:c0 + 256], W=[kw_])
                DMA("sp", bmodp[0:2, :], bmod_d[:, c0:c0 + 256], W=[k_bmodp])
                CP("act", wb[:, 0:4, :], wt[:, 0:4, :], R=[kw_], W=[kwb])
                CP("act", wb[:, 4:8, :], wt[:, 4:8, :], R=[kw_], W=[kwb])
                pt, kp = PB[k % 2]
                with atomic():
                    for kc in range(8):
                        MM(pt[0:2, 0:256], scvP[:, kc, :], wb[:, kc, :], start=(kc == 0), stop=(kc == 7), R=[k_scbP, kwb], W=[kp])
                    TT("dve", rowp[0:2, :], pt[0:2, 0:256], bmodp[0:2, :], ALU.add, R=[kp, k_bmodp], W=[k_rowp])
                if vec in (0, 3):
                    gt, kg = (gate_mix, k_gmix) if vec == 0 else (gate_ffn, k_gffn)
                    n0 = (0 if vec == 0 else D) + q * 256
                    DMA("sp", nrowp[0:1, :], nrow_d[0:1, n0:n0 + 256], W=[k_nrowp])
                    TT("dve", rowp[0:1, :], rowp[0:1, :], nrowp[0:1, :], ALU.mult, R=[k_rowp, k_nrowp], W=[k_rowp])
                    with atomic():
                        MM(pt[:, 0:256], onesf[0:1, :], rowp[0:1, :], R=[k_onesf, k_rowp], W=[kp])
                        CP("act", gt[:, q * 256:(q + 1) * 256], pt[:, 0:256], R=[kp], W=[kg])
                else:
                    base = SH_FFN if vec == 1 else G_FFN
                    with atomic():
                        for j in range(2):
                            MM(pt[:, 2 * j:2 * j + 2], rowp[0:1, j * 128:(j + 1) * 128], onesf[0:1, 0:2], R=[k_rowp, k_onesf], W=[kp])
                        CP("dve", mcols[:, base + 2 * q: base + 2 * q + 2], pt[:, 0:4].rearrange("p (c two) -> p c two", two=2)[:, :, 0],
                           R=[kp], W=[k_mcols])
                    if vec == 2:
                        sl = slice(base + 2 * q, base + 2 * q + 2)
                        TS("dve", mcols[:, sl], mcols[:, sl], 1.0, ALU.add, R=[k_mcols], W=[k_mcols])
                        TT("dve", mcols[:, sl], mcols[:, sl], ncolp[:, 8 + 2 * q: 8 + 2 * q + 2], ALU.mult, R=[k_mcols, k_ncolp], W=[k_mcols])
            wout_view = wout_d.rearrange("(k p) n -> p k n", p=128)
            for kc in range(8):
                DMA("pool", wout[:, kc, :], wout_view[:, kc, :], W=[k_wout], max_dma_last_dim=4096)
            st1 = None
            st2 = None
            for blk in range(7, -1, -1):
                f = next_fs()
                reload_fs(f, blk)
                for ti in (1, 0):
                    T = blk * 2 + ti
                    if True:
                        DMA("sp", yo[ti][0], yf_scr[T * 128:(T + 1) * 128, :], R=["yf_scr:%d" % T], W=[yo[ti][1] + ":g", yo[ti][1] + ":s"])
                    with Chain() as cg:
                        if ti == 1:
                            gla_prep(f, 1, True)
                        gla_tile(f, 1, True, ti)
                    with Chain() as cs:
                        ssd_tile(f, 1, True, ti)
                    chains = [cg, cs]
                    if T >= 14:
                        late_ks = [2 * (15 - T), 2 * (15 - T) + 1]
                    elif T >= 2:
                        late_ks = [4 + (13 - T)]
                    else:
                        late_ks = []
                    if late_ks:
                        with Chain() as cl:
                            for k_ in late_ks:
                                late_mod(k_)
                        chains.append(cl)
                    if st2 is not None:
                        with Chain() as c3:
                            merge_out(st2)
                        chains.append(c3)
                    if st1 is not None:
                        with Chain() as c1:
                            merge_gla(*st1)
                        with Chain() as c2:
                            merge_ssd(*st1)
                        chains += [c1, c2]
                    interleave(chains)
                    st2 = st1[2] if st1 is not None else None
                    st1 = (f, ti, T)
            chains = []
            if st2 is not None:
                with Chain() as c3:
                    merge_out(st2)
                chains.append(c3)
            with Chain() as c1:
                merge_gla(*st1)
            with Chain() as c2:
                merge_ssd(*st1)
            interleave(chains + [c1, c2])
            merge_out(st1[2])

            S.barrier()
            A.off = win_off

            wdn, k_wdn = A.alloc([22, D], BF16, "wdn")
            wdn_view = wdown_d.rearrange("(k p) n -> p k n", p=128)
            wdn_pending = list(range(22))
            h2Ts = [A.alloc([8, 1024], BF16, "h2T%d" % i) for i in range(2)]
            actT, k_actT = A.alloc([22, 1024], BF16, "actT")
            wgb = [A.alloc([8, 512], BF16, "wgb%d" % i) for i in range(2)]
            wub = [A.alloc([8, 512], BF16, "wub%d" % i) for i in range(2)]
            xn2 = [A.alloc([D], F32, "xn2_%d" % i) for i in range(2)]
            xn3 = [A.alloc([D], F32, "xn3_%d" % i) for i in range(2)]
            xs2 = [A.alloc([D], BF16, "xs2_%d" % i) for i in range(2)]
            junk2, k_junk2 = A.alloc([D], BF16, "junk2")
            st2, k_st2 = A.alloc([2, 4], F32, "st2")
            sgt = [A.alloc([512], F32, "sgt%d" % i) for i in range(2)]
            tf, k_tf = A.alloc([D], F32, "tf")
            fst, k_fst = A.alloc([2, 4], F32, "fst")
            print("arena phase F bytes:", A.off)
            wg_view = wgate_d.rearrange("(k p) n -> p k n", p=128)
            wu_view = wup_d.rearrange("(k p) n -> p k n", p=128)

            def ffn_prenorm(sb):
                h2T, k_h2T = h2Ts[sb % 2]
                xnb = (xn2 + xn3) if sb == 0 else xn2
                for tl in range(8):
                    T = sb * 8 + tl
                    j = tl % 2
                    xnt, kxn = xnb[tl % len(xnb)]
                    kst_ = k_st2 + ":%d" % j
                    DMA("sp", xnt, xn_scr[T * 128:(T + 1) * 128, :], R=["xn_scr:%d" % T], W=[kxn])
                    ACT(junk2, xnt, AF.Square, accum=st2[:, j, 0:1], R=[kxn], W=[k_junk2, kst_])
                    ACT(st2[:, j, 1:2], st2[:, j, 0:1], AF.Ln, scale=1.0 / D, bias=1e-6, R=[kst_], W=[kst_])
                    ACT(st2[:, j, 3:4], st2[:, j, 1:2], AF.Exp, scale=-0.5, R=[kst_], W=[kst_])
                    xs_, kxs = xs2[j]
                    ACT(xs_, xnt, AF.Copy, scale=st2[:, j, 3:4], R=[kxn, kst_], W=[kxs])
                    ptb, kptb = PT_BF[j]
                    with atomic():
                        for kc in range(8):
                            TR(ptb[:, kc * 128:(kc + 1) * 128], xs_[:, kc * 128:(kc + 1) * 128], R=[kxs, k_ident], W=[kptb])
                        for kc in range(8):
                            TS("dve", h2T[:, kc, tl * 128:(tl + 1) * 128], ptb[:, kc * 128:(kc + 1) * 128],
                               mcols[:, G_FFN + kc: G_FFN + kc + 1], ALU.mult, mcols[:, SH_FFN + kc: SH_FFN + kc + 1], ALU.add,
                               R=[kptb, k_mcols], W=[k_h2T + ":%d" % tl])

            w_issued = set()

            def ffn_wload(sb, fg):
                if (sb, fg) in w_issued:
                    return
                w_issued.add((sb, fg))
                nf = 4 if fg < 5 else 2
                wgt, kwg = wgb[fg % 2]
                wut, kwu = wub[fg % 2]
                c0 = fg * 512
                DMA("pool", wgt[:, :, 0:nf * 128], wg_view[:, :, c0:c0 + nf * 128], W=[kwg])
                DMA("pool", wut[:, :, 0:nf * 128], wu_view[:, :, c0:c0 + nf * 128], W=[kwu])

            def ffn_gateup(sb):
                h2T, k_h2T = h2Ts[sb % 2]
                K_H2 = [k_h2T + ":%d" % tl for tl in range(8)]
                for fg in range(6):
                    nf = 4 if fg < 5 else 2
                    wgt, kwg = wgb[fg % 2]
                    wut, kwu = wub[fg % 2]
                    ffn_wload(sb, fg)
                    if fg >= 1:
                        for _ in range(6):
                            if wdn_pending:
                                fc_ = wdn_pending.pop(0)
                                DMA("pool", wdn[:, fc_, :], wdn_view[:, fc_, :], W=[k_wdn + ":%d" % fc_])
                    for fi in range(nf):
                        fc = fg * 4 + fi
                        for nbk in range(2):
                            pg, kpg = PB[(2 * nbk) % 4]
                            pu, kpu = PB[(2 * nbk + 1) % 4]
                            rk = K_H2[nbk * 4:(nbk + 1) * 4]
                            for kc in range(8):
                                MM(pg[:, :], wgt[:, kc, fi * 128:(fi + 1) * 128], h2T[:, kc, nbk * 512:(nbk + 1) * 512], start=(kc == 0), stop=(kc == 7),
                                   R=[kwg] + rk, W=[kpg])
                            for kc in range(8):
                                MM(pu[:, :], wut[:, kc, fi * 128:(fi + 1) * 128], h2T[:, kc, nbk * 512:(nbk + 1) * 512], start=(kc == 0), stop=(kc == 7),
                                   R=[kwu] + rk, W=[kpu])
                            sg_, ksg_ = sgt[nbk]
                            ACT(sg_, pg[:, :], AF.Silu, R=[kpg], W=[ksg_])
                            TT("dve", actT[:, fc, nbk * 512:(nbk + 1) * 512], pu[:, :], sg_, ALU.mult, R=[kpu, ksg_], W=[k_actT + ":%d" % nbk])

            def ffn_down(sb):
                djunk = sgt[0][0].bitcast(BF16)
                kdj = sgt[0][1]
                for tl in range(8):
                    T = sb * 8 + tl
                    nbk = tl // 4
                    j = tl % 2
                    halves = []
                    for nb in range(2):
                        if j == 0:
                            halves.append((psum2[:, nb * 512:(nb + 1) * 512], "pbig:%d" % nb))
                        else:
                            halves.append((PB[nb][0][:, :], PB[nb][1]))
                    xnt, kxn = xn3[j]
                    DMA("sp", xnt, xn_scr[T * 128:(T + 1) * 128, :], R=["xn_scr:%d" % T], W=[kxn])
                    kf = k_fst + ":%d" % j
                    with atomic():
                        for nb in range(2):
                            pf, kpf = halves[nb]
                            for fc in range(22):
                                MM(pf, actT[:, fc, tl * 128:(tl + 1) * 128], wdn[:, fc, nb * 512:(nb + 1) * 512],
                                   start=(fc == 0), stop=(fc == 21), R=[k_actT + ":%d" % nbk, k_wdn + ":%d" % fc], W=[kpf])
                            ACT(djunk[:, nb * 512:(nb + 1) * 512], pf, AF.Square, accum=fst[:, j, nb:nb + 1], R=[kpf], W=[kdj, kf])
                        TT("dve", fst[:, j, 2:3], fst[:, j, 0:1], fst[:, j, 1:2], ALU.add, R=[kf], W=[kf])
                        ACT(fst[:, j, 2:3], fst[:, j, 2:3], AF.Ln, scale=1.0 / D, bias=1e-6, R=[kf], W=[kf])
                        ACT(fst[:, j, 3:4], fst[:, j, 2:3], AF.Exp, scale=-0.5, R=[kf], W=[kf])
                        for nb in range(2):
                            pf, kpf = halves[nb]
                            STT(tf[:, nb * 512:(nb + 1) * 512], pf, fst[:, j, 3:4], gate_ffn[:, nb * 512:(nb + 1) * 512],
                                ALU.mult, ALU.mult, R=[kpf, kf, k_gffn], W=[k_tf])
                    TT("dve", xnt[:, 0:512], tf[:, 0:512], xnt[:, 0:512], ALU.add, R=[k_tf, kxn], W=[kxn])
                    TT("pool", xnt[:, 512:1024], tf[:, 512:1024], xnt[:, 512:1024], ALU.add, R=[k_tf, kxn], W=[kxn])
                    tok_holder = _sched("dma", "act", (lambda e, T=T, xnt=xnt: e.dma_start(out=y_d[T * 128:(T + 1) * 128, :], in_=xnt)),
                                        [kxn], ["y:%d" % T])
                    if tok_holder is not None:
                        out_toks.append(tok_holder)

            ffn_prenorm(0)
            ffn_gateup(0)
            ffn_wload(1, 0)
            ffn_wload(1, 1)
            with Chain() as cd:
                ffn_down(0)
            with Chain() as cp:
                ffn_prenorm(1)
            interleave([cd, cp])
            ffn_gateup(1)
            ffn_down(1)
        except _Stop:
            pass
        S.barrier()
        S.emit(st)
    return nc


_PROG = {}


def _consts():
    bf = ml_dtypes.bfloat16
    t = np.arange(128)
    TF = (t[:, None] <= t[None, :]).astype(np.float32)
    TB = (t[:, None] >= t[None, :]).astype(np.float32)
    UF = (t[:, None] > t[None, :]).astype(np.float32)
    UB = (t[:, None] < t[None, :]).astype(np.float32)
    tri = np.concatenate([TF, TB], axis=1)
    upp = np.concatenate([UF, UB], axis=1)
    negF = np.tile((1.0 - TF) * NEG, (1, 4))
    negB = np.tile((1.0 - TB) * NEG, (1, 4))
    negm = np.concatenate([negF, negB], axis=1)
    bd = np.zeros((16, 8, 128), np.float32)
    for s in range(2):
        for h in range(8):
            bd[s * 8 + h, h, :] = 1.0
    bd = bd.reshape(16, 1024)
    uall = np.zeros((48, 128), np.float32)
    uall[0:16] = 1.0
    vall = np.zeros((48, 1024), np.float32)
    vall[32:48] = bd
    bdg = np.zeros((128, 256), np.float32)
    bdg[0:64, 0:128] = 1.0
    bdg[64:128, 128:256] = 1.0
    bds = np.zeros((128, 8), np.float32)
    bds[0:64, 0:4] = 1.0
    bds[64:128, 4:8] = 1.0
    gmask = np.zeros((128, 2), np.float32)
    gmask[0:64, 0] = 1.0
    gmask[64:128, 1] = 1.0
    return {
        "ident": np.eye(128, dtype=np.float32).astype(bf),
        "tri_bf": tri.astype(bf), "tri_f": tri, "upp_f": upp,
        "ones_f": np.ones((128, 128), np.float32),
        "negm": negm.astype(bf), "uall_t": uall.astype(bf), "vall_t": vall.astype(bf),
        "bdg": bdg, "bds": bds, "gmask": gmask,
    }


def _col(v):
    return np.ascontiguousarray(v.reshape(8, 128).T)


def make_in_maps(x, c, ctx, c_ctx, w_mod, b_mod, norm_mix_pre, norm_mix_post, norm_ffn_pre, norm_ffn_post,
                 w_in, conv_w, conv_b, gla_wg_f, gla_bg_f, gla_wg_b, gla_bg_b, gla_norm,
                 a_log_f, a_log_b, dt_bias_f, dt_bias_b, d_skip, ssd_norm, w_out, w_gate, w_up, w_down):
    f32 = np.float32
    L = 0
    consts = _consts()
    w_in0 = np.asarray(w_in[L], f32)
    perm = np.arange(DIN)
    perm[1536:1552], perm[1552:1568] = np.arange(1552, 1568), np.arange(1536, 1552)
    perm[2848:2856], perm[2856:2864] = np.arange(2856, 2864), np.arange(2848, 2856)
    w_in1 = np.ascontiguousarray(w_in0[:, perm])
    shared = dict(consts)
    shared.update({
        "w_mod": np.asarray(w_mod[L], f32),
        "b_mod2": np.ascontiguousarray(np.broadcast_to(np.asarray(b_mod[L], f32)[None, :], (2, 6 * D))),
        "ncol": np.concatenate([_col(np.asarray(norm_mix_pre[L], f32)), _col(np.asarray(norm_ffn_pre[L], f32))], axis=1),
        "nrow": np.concatenate([np.asarray(norm_mix_post[L], f32), np.asarray(norm_ffn_post[L], f32)])[None, :],
        "cb": np.ascontiguousarray(np.asarray(conv_b[L], f32).reshape(6, 128).T),
        "mixg": np.concatenate([np.broadcast_to(np.asarray(gla_norm[L], f32)[:, None], (128, 4)),
                                np.asarray(ssd_norm[L], f32).reshape(4, 128).T], axis=1).astype(f32),
        "dsk": np.ascontiguousarray(np.broadcast_to(np.repeat(np.asarray(d_skip[L], f32), 64)[None, :], (128, 512))),
        "w_out": np.asarray(w_out[L], f32), "w_gate": np.asarray(w_gate[L], f32),
        "w_up": np.asarray(w_up[L], f32), "w_down": np.asarray(w_down[L], f32),
    })
    per_th = []
    for th in range(2):
        cw = np.asarray(conv_w[L], f32)
        if th == 1:
            cw = cw[::-1, ::-1, :]
        cwl = np.ascontiguousarray(cw.reshape(9, 6, 128).transpose(2, 1, 0).reshape(128, 54))
        wgf, wgb_ = np.asarray(gla_wg_f[L], f32), np.asarray(gla_wg_b[L], f32)
        bgf, bgb = np.asarray(gla_bg_f[L], f32), np.asarray(gla_bg_b[L], f32)
        alf, alb = np.asarray(a_log_f[L], f32), np.asarray(a_log_b[L], f32)
        dbf, dbb = np.asarray(dt_bias_f[L], f32), np.asarray(dt_bias_b[L], f32)
        if th == 1:
            wgf, wgb_, bgf, bgb = wgb_, wgf, bgb, bgf
            alf, alb, dbf, dbb = alb, alf, dbb, dbf
        per_th.append({
            "w_in": w_in0 if th == 0 else w_in1,
            "cw": cwl,
            "wg": np.ascontiguousarray(np.concatenate([wgf, wgb_], axis=1)),
            "bg": np.concatenate([bgf, bgb])[None, :].astype(f32),
            "alog": np.ascontiguousarray(np.broadcast_to(np.concatenate([alf, alb])[None, :], (128, 16))),
            "dtb": np.ascontiguousarray(np.broadcast_to(np.concatenate([dbf, dbb])[None, :], (128, 16))),
        })
    in_maps = []
    x = np.asarray(x, f32)
    ctx = np.asarray(ctx, f32)
    c = np.asarray(c, f32)
    c_ctx = np.asarray(c_ctx, f32)
    for core in range(8):
        b, th = core // 2, core % 2
        m = dict(shared)
        m.update(per_th[th])
        xb, cxb = x[b], ctx[b]
        if th == 1:
            xb, cxb = xb[::-1], cxb[::-1]
        m["x"] = np.ascontiguousarray(xb)
        m["ctx"] = np.ascontiguousarray(cxb)
        cc = np.stack([_col(c[b]), _col(c_ctx)], axis=2).reshape(128, 16)
        m["cc"] = np.ascontiguousarray(cc)
        in_maps.append(m)
    return in_maps


def kernel(**inputs):
    if "nc" not in _PROG:
        _PROG["nc"] = build_program()
    nc = _PROG["nc"]
    in_maps = make_in_maps(**inputs)
    res = run_bass_kernel_spmd(nc, in_maps, core_ids=list(range(8)))
    out = np.empty((4, SEQ, D), np.float32)
    for core in range(8):
        b, th = core // 2, core % 2
        y = np.asarray(res.results[core]["y"], np.float32)
        if th == 0:
            out[b, 0:OWN] = y
        else:
            out[b, OWN:SEQ] = y[::-1]
    return out
```
